# Optimizing a Trainium2 kernel written in Bass

```python
import jax, jax.numpy as jnp
from jax import lax
import numpy as np

D_MODEL = 2048
BATCH = 2
SEQ = 16384
DEPTH = 2

GRID_W = 64
CTX_LEN = 256
HEAD_DIM = 128
HALF_DIM = HEAD_DIM // 2
AXIS_FREQS = HEAD_DIM // 4
ROPE_THETA = 10000.0
N_HEADS_A = 8
N_KV_A = 2
N_HEADS_B = 8
N_KV_B = 2
G_A = N_HEADS_A // N_KV_A
G_B = N_HEADS_B // N_KV_B
Q_BLOCK = 128
WINDOW = 128
BAND = Q_BLOCK + 2 * WINDOW
ATTN_SCALE = HEAD_DIM ** -0.5
NEG_INF = -1e30
Q_W = (N_HEADS_A + N_HEADS_B) * HEAD_DIM
KV_A_W = N_KV_A * HEAD_DIM
KV_B_W = N_KV_B * HEAD_DIM
KV_W = 2 * KV_A_W + 2 * KV_B_W
ATTN_WIDTH = Q_W
EVEN_IN = Q_W + KV_W + ATTN_WIDTH
POOL_WIDTH = D_MODEL
POOL_SIZES = (2, 4, 8, 16)
N_POOL_GROUPS = len(POOL_SIZES)
POOL_GROUP = POOL_WIDTH // N_POOL_GROUPS
ODD_IN = 2 * POOL_WIDTH
EPS = 1e-6
N_EVEN = (DEPTH + 1) // 2
N_ODD = DEPTH // 2

kernel_name = 'hybrid_dit_gqa_window_pool_prefix'


def _rmsnorm(x, g):
    xf = x.astype(jnp.float32)
    y = xf * lax.rsqrt(jnp.mean(xf * xf, axis=-1, keepdims=True) + EPS)
    return (y * g.astype(jnp.float32)).astype(x.dtype)


def _modulation(cvec, w, b):
    m = jax.nn.silu(cvec) @ w + b
    m = m[..., None, :]
    return m[..., :D_MODEL], m[..., D_MODEL:2 * D_MODEL], m[..., 2 * D_MODEL:]


def _axial_rope(rows):
    row = jnp.broadcast_to(jnp.arange(rows)[:, None], (rows, GRID_W)).reshape(-1).astype(jnp.float32)
    col = jnp.broadcast_to(jnp.arange(GRID_W)[None, :], (rows, GRID_W)).reshape(-1).astype(jnp.float32)
    inv = ROPE_THETA ** (-jnp.arange(AXIS_FREQS, dtype=jnp.float32) / AXIS_FREQS)
    ang = jnp.concatenate([row[:, None] * inv, col[:, None] * inv], axis=-1)
    return jnp.cos(ang), jnp.sin(ang)


def _rope(x, cos, sin):
    shape = (cos.shape[0],) + (1,) * (x.ndim - 3) + (cos.shape[1],)
    cos = cos.reshape(shape)
    sin = sin.reshape(shape)
    xf = x.astype(jnp.float32)
    x1, x2 = xf[..., :HALF_DIM], xf[..., HALF_DIM:]
    return jnp.concatenate([x1 * cos - x2 * sin, x2 * cos + x1 * sin], axis=-1).astype(x.dtype)


def _split_kv(kv):
    B, L = kv.shape[:2]
    kA = kv[..., :KV_A_W].reshape(B, L, N_KV_A, HEAD_DIM)
    vA = kv[..., KV_A_W:2 * KV_A_W].reshape(B, L, N_KV_A, HEAD_DIM)
    kB = kv[..., 2 * KV_A_W:2 * KV_A_W + KV_B_W].reshape(B, L, N_KV_B, HEAD_DIM)
    vB = kv[..., 2 * KV_A_W + KV_B_W:].reshape(B, L, N_KV_B, HEAD_DIM)
    return kA, vA, kB, vB


def _split_q(q):
    B, L = q.shape[:2]
    q = q.reshape(B, L, N_HEADS_A + N_HEADS_B, HEAD_DIM)
    qA = q[:, :, :N_HEADS_A].reshape(B, L, N_KV_A, G_A, HEAD_DIM)
    qB = q[:, :, N_HEADS_A:].reshape(B, L, N_KV_B, G_B, HEAD_DIM)
    return qA, qB


def _dense_attention(q, k, v):
    B, L = q.shape[:2]
    nb = L // Q_BLOCK

    def block(n):
        qb = lax.dynamic_slice_in_dim(q, n * Q_BLOCK, Q_BLOCK, axis=1)
        s = jnp.einsum('bqkgd,bskd->bkgqs', qb, k).astype(jnp.float32) * ATTN_SCALE
        p = jax.nn.softmax(s, axis=-1).astype(v.dtype)
        return jnp.einsum('bkgqs,bskd->bqkgd', p, v)

    o = lax.map(block, jnp.arange(nb))
    return jnp.moveaxis(o, 0, 1).reshape(B, L, -1)


def _window_attention(q, k, v, kc, vc, sink):
    B, L, KV, G, _ = q.shape
    C = kc.shape[1]
    nb = L // Q_BLOCK
    pad = ((0, 0), (WINDOW, WINDOW), (0, 0), (0, 0))
    kp = jnp.pad(k, pad)
    vp = jnp.pad(v, pad)
    sink_l = sink.astype(jnp.float32).reshape(1, KV, G, 1, 1)

    def block(n):
        start = n * Q_BLOCK
        qb = lax.dynamic_slice_in_dim(q, start, Q_BLOCK, axis=1)
        kb = lax.dynamic_slice_in_dim(kp, start, BAND, axis=1)
        vb = lax.dynamic_slice_in_dim(vp, start, BAND, axis=1)
        qpos = start + jnp.arange(Q_BLOCK)
        kpos = start - WINDOW + jnp.arange(BAND)
        valid = (jnp.abs(kpos[None, :] - qpos[:, None]) <= WINDOW) & (kpos >= 0)[None, :] & (kpos < L)[None, :]
        s_band = jnp.einsum('bqkgd,bjkd->bkgqj', qb, kb).astype(jnp.float32) * ATTN_SCALE
        s_band = jnp.where(valid, s_band, NEG_INF)
        s_ctx = jnp.einsum('bqkgd,bckd->bkgqc', qb, kc).astype(jnp.float32) * ATTN_SCALE
        s_sink = jnp.broadcast_to(sink_l, s_ctx.shape[:-1] + (1,))
        p = jax.nn.softmax(jnp.concatenate([s_band, s_ctx, s_sink], axis=-1), axis=-1).astype(v.dtype)
        return (jnp.einsum('bkgqj,bjkd->bqkgd', p[..., :BAND], vb)
                + jnp.einsum('bkgqc,bckd->bqkgd', p[..., BAND:BAND + C], vc))

    o = lax.map(block, jnp.arange(nb))
    return jnp.moveaxis(o, 0, 1).reshape(B, L, -1)


def _ctx_attention(q, k, v, sink=None):
    B, C = q.shape[:2]
    s = jnp.einsum('bqkgd,bckd->bkgqc', q, k).astype(jnp.float32) * ATTN_SCALE
    if sink is not None:
        KV, G = q.shape[2], q.shape[3]
        s_sink = jnp.broadcast_to(sink.astype(jnp.float32).reshape(1, KV, G, 1, 1), s.shape[:-1] + (1,))
        p = jax.nn.softmax(jnp.concatenate([s, s_sink], axis=-1), axis=-1)[..., :C]
    else:
        p = jax.nn.softmax(s, axis=-1)
    return jnp.einsum('bkgqc,bckd->bqkgd', p.astype(v.dtype), v).reshape(B, C, -1)


def _attention_layer(x, ctx, cos, sin, c, c_ctx, mod_w, mod_b, pre_g, post_g, w_in, q_norm, k_norm, sink, w_out, need_ctx_out):
    shift, scale, gate = _modulation(c, mod_w, mod_b)
    shift_c, scale_c, gate_c = _modulation(c_ctx, mod_w, mod_b)
    h = _rmsnorm(x, pre_g) * (1 + scale) + shift
    hc = _rmsnorm(ctx, pre_g) * (1 + scale_c) + shift_c

    proj = h @ w_in
    qA, qB = _split_q(proj[..., :Q_W])
    kA, vA, kB, vB = _split_kv(proj[..., Q_W:Q_W + KV_W])
    z = proj[..., Q_W + KV_W:]

    kA_c, vA_c, kB_c, vB_c = _split_kv(hc @ w_in[:, Q_W:Q_W + KV_W])
    kA_c = _rmsnorm(kA_c, k_norm)

    qA = _rope(_rmsnorm(qA, q_norm), cos, sin)
    kA = _rope(_rmsnorm(kA, k_norm), cos, sin)
    outA = _dense_attention(qA, jnp.concatenate([kA, kA_c], axis=1), jnp.concatenate([vA, vA_c], axis=1))

    outB = _window_attention(_rope(qB, cos, sin), _rope(kB, cos, sin), vB, kB_c, vB_c, sink)

    y = jnp.concatenate([outA, outB], axis=-1) * jax.nn.silu(z)
    x = x + gate * _rmsnorm(y @ w_out, post_g)

    if need_ctx_out:
        qA_c, qB_c = _split_q(hc @ w_in[:, :Q_W])
        z_c = hc @ w_in[:, Q_W + KV_W:]
        oA_c = _ctx_attention(_rmsnorm(qA_c, q_norm), kA_c, vA_c)
        oB_c = _ctx_attention(qB_c, kB_c, vB_c, sink)
        yc = jnp.concatenate([oA_c, oB_c], axis=-1) * jax.nn.silu(z_c)
        ctx = ctx + gate_c * _rmsnorm(yc @ w_out, post_g)
    return x, ctx


def _multiscale_pool(u, pool_w, pool_scale):
    B, L, _ = u.shape
    uf = u.astype(jnp.float32)
    cs = jnp.concatenate([jnp.zeros_like(uf[:, :1]), lax.cumsum(uf, axis=1)], axis=1)
    t = jnp.arange(L)
    outs = []
    for g, w in enumerate(POOL_SIZES):
        lo = jnp.clip(t - w // 2, 0, L)
        hi = jnp.clip(t + w // 2, 0, L)
        csg = cs[..., g * POOL_GROUP:(g + 1) * POOL_GROUP]
        mean = (jnp.take(csg, hi, axis=1) - jnp.take(csg, lo, axis=1)) / (hi - lo).astype(jnp.float32)[:, None]
        outs.append(mean - uf[..., g * POOL_GROUP:(g + 1) * POOL_GROUP])
    pooled = jnp.stack(outs, axis=2).astype(u.dtype)
    mixed = jnp.einsum('blgc,gcd->blgd', pooled, pool_w).reshape(B, L, POOL_WIDTH)
    return mixed * pool_scale


def _pool_layer(x, ctx, c, c_ctx, mod_w, mod_b, pre_g, post_g, w_in, pool_w, pool_scale, w_out, need_ctx_out):
    shift, scale, gate = _modulation(c, mod_w, mod_b)
    h = _rmsnorm(x, pre_g) * (1 + scale) + shift
    proj = h @ w_in
    y = _multiscale_pool(proj[..., :POOL_WIDTH], pool_w, pool_scale) * jax.nn.silu(proj[..., POOL_WIDTH:])
    x = x + gate * _rmsnorm(y @ w_out, post_g)
    if need_ctx_out:
        shift_c, scale_c, gate_c = _modulation(c_ctx, mod_w, mod_b)
        hc = _rmsnorm(ctx, pre_g) * (1 + scale_c) + shift_c
        pc = hc @ w_in
        yc = _multiscale_pool(pc[..., :POOL_WIDTH], pool_w, pool_scale) * jax.nn.silu(pc[..., POOL_WIDTH:])
        ctx = ctx + gate_c * _rmsnorm(yc @ w_out, post_g)
    return x, ctx


def setup_inputs(seed: int = 0) -> dict:
    key = jax.random.key(seed)
    ks = jax.random.split(key, 24)
    f32 = jnp.float32

    def nrm(k, shape, s):
        return jax.random.normal(k, shape, f32) * s

    return {
        'x': nrm(ks[0], (BATCH, SEQ, D_MODEL), 1.0),
        'c': nrm(ks[1], (BATCH, D_MODEL), 1.0),
        'ctx': nrm(ks[2], (BATCH, CTX_LEN, D_MODEL), 1.0),
        'c_ctx': nrm(ks[3], (D_MODEL,), 1.0),
        'ev_mod_w': nrm(ks[4], (N_EVEN, D_MODEL, 3 * D_MODEL), 0.5 * D_MODEL ** -0.5),
        'ev_mod_b': nrm(ks[5], (N_EVEN, 3 * D_MODEL), 0.02),
        'ev_pre_g': 1.0 + nrm(ks[6], (N_EVEN, D_MODEL), 0.05),
        'ev_post_g': 1.0 + nrm(ks[7], (N_EVEN, D_MODEL), 0.05),
        'ev_w_in': nrm(ks[8], (N_EVEN, D_MODEL, EVEN_IN), D_MODEL ** -0.5),
        'ev_q_norm': 1.0 + nrm(ks[9], (N_EVEN, HEAD_DIM), 0.05),
        'ev_k_norm': 1.0 + nrm(ks[10], (N_EVEN, HEAD_DIM), 0.05),
        'ev_sink': nrm(ks[11], (N_EVEN, N_HEADS_B), 0.5),
        'ev_w_out': nrm(ks[12], (N_EVEN, ATTN_WIDTH, D_MODEL), ATTN_WIDTH ** -0.5),
        'od_mod_w': nrm(ks[13], (N_ODD, D_MODEL, 3 * D_MODEL), 0.5 * D_MODEL ** -0.5),
        'od_mod_b': nrm(ks[14], (N_ODD, 3 * D_MODEL), 0.02),
        'od_pre_g': 1.0 + nrm(ks[15], (N_ODD, D_MODEL), 0.05),
        'od_post_g': 1.0 + nrm(ks[16], (N_ODD, D_MODEL), 0.05),
        'od_w_in': nrm(ks[17], (N_ODD, D_MODEL, ODD_IN), D_MODEL ** -0.5),
        'od_pool_w': nrm(ks[18], (N_ODD, N_POOL_GROUPS, POOL_GROUP, POOL_GROUP), POOL_GROUP ** -0.5),
        'od_pool_scale': 1.0 + nrm(ks[19], (N_ODD, POOL_WIDTH), 0.1),
        'od_w_out': nrm(ks[20], (N_ODD, POOL_WIDTH, D_MODEL), POOL_WIDTH ** -0.5),
    }


def reference(x, c, ctx, c_ctx, ev_mod_w, ev_mod_b, ev_pre_g, ev_post_g, ev_w_in, ev_q_norm, ev_k_norm, ev_sink, ev_w_out,
              od_mod_w, od_mod_b, od_pre_g, od_post_g, od_w_in, od_pool_w, od_pool_scale, od_w_out):
    ROWS = x.shape[1] // GRID_W
    cos, sin = _axial_rope(ROWS)
    for i in range(DEPTH):
        need_ctx_out = any(j % 2 == 0 for j in range(i + 1, DEPTH))
        if i % 2 == 0:
            e = i // 2
            x, ctx = _attention_layer(x, ctx, cos, sin, c, c_ctx, ev_mod_w[e], ev_mod_b[e], ev_pre_g[e], ev_post_g[e],
                                      ev_w_in[e], ev_q_norm[e], ev_k_norm[e], ev_sink[e], ev_w_out[e], need_ctx_out)
        else:
            o = i // 2
            x, ctx = _pool_layer(x, ctx, c, c_ctx, od_mod_w[o], od_mod_b[o], od_pre_g[o], od_post_g[o],
                                 od_w_in[o], od_pool_w[o], od_pool_scale[o], od_w_out[o], need_ctx_out)
    return x
```

```python
import numpy as np
import ml_dtypes
import concourse.bass as bass
import concourse.mybir as mybir
from concourse.bass_utils import run_bass_kernel_spmd

F32 = mybir.dt.float32
BF16 = mybir.dt.bfloat16
AF = mybir.ActivationFunctionType
ALU = mybir.AluOpType
AX = mybir.AxisListType

D = 2048
SEQ = 16384
NBATCH = 2
CTX = 256
HD = 128
OWN = 4096
NTO = OWN // 128
NTE = NTO + 2
NTB = SEQ // 128
NKC = NTB + CTX // 128
EPS = 1e-6
ATTN_SCALE = HD ** -0.5
GRID_W = 64
ENGS = ("pe", "act", "dve", "pool", "sp")


class Tok:
    __slots__ = ("sem", "val")

    def __init__(self, sem, val):
        self.sem = sem
        self.val = val


class DmaSlot:
    def __init__(self, prog, name):
        self.sem = prog.nc.alloc_semaphore(name)
        self.count = 0


class Prog:
    def __init__(self, nc):
        self.nc = nc
        self.ops = {e: [] for e in ENGS}
        self.sems = {e: nc.alloc_semaphore("s_" + e) for e in ENGS}
        self.cnt = {e: 0 for e in ENGS}
        self.nslot = 0
        self.slots = []
        self.named = {}

    def slot(self, name=None):
        self.nslot += 1
        sl = DmaSlot(self, name or ("dslot%d" % self.nslot))
        self.slots.append(sl)
        return sl

    def pslot(self, name):
        if name not in self.named:
            self.named[name] = self.slot(name)
        return self.named[name]

    def emit(self, eng, fn, waits=()):
        self.cnt[eng] += 1
        tok = Tok(self.sems[eng], self.cnt[eng])
        self.ops[eng].append((fn, _flat(waits), (self.sems[eng], 1)))
        return tok

    def op(self, eng, method, kwargs, waits=()):
        kw = dict(kwargs)
        return self.emit(eng, lambda e, m=method, k=kw: getattr(e, m)(**k), waits)

    def dma(self, eng, slot, out, in_, waits=(), **kw):
        slot.count += 16
        tok = Tok(slot.sem, slot.count)
        self.ops[eng].append((lambda e, o=out, i=in_, k=kw: e.dma_start(out=o, in_=i, **k),
                              _flat(waits), (slot.sem, 16)))
        return tok

    def wait_only(self, eng, waits):
        self.ops[eng].append((None, _flat(waits), None))

    def barrier(self, toks):
        for e in ENGS:
            self.wait_only(e, toks)

    def last(self, eng):
        return Tok(self.sems[eng], self.cnt[eng]) if self.cnt[eng] else None

    def full_barrier(self):
        toks = [self.last(e) for e in ENGS] + [Tok(s.sem, s.count) for s in self.slots if s.count]
        self.barrier(toks)
        return toks

    def build(self):
        nc = self.nc
        with nc.Block() as block:
            def make(engname):
                def body(e):
                    seen = {}
                    for fn, waits, inc in self.ops[engname]:
                        for w in waits:
                            key = id(w.sem)
                            if seen.get(key, 0) >= w.val:
                                continue
                            e.wait_ge(w.sem, w.val)
                            seen[key] = w.val
                        if fn is not None:
                            inst = fn(e)
                            inst.then_inc(inc[0], inc[1])
                return body
            block.tensor(make("pe"))
            block.scalar(make("act"))
            block.vector(make("dve"))
            block.gpsimd(make("pool"))
            block.sync(make("sp"))


def _flat(waits):
    out = []
    for w in waits:
        if w is None:
            continue
        if isinstance(w, (list, tuple)):
            out.extend(_flat(w))
        else:
            out.append(w)
    return tuple(out)


class Arena:
    def __init__(self, nc, nbytes):
        self.t = nc.alloc_sbuf_tensor("arena", [128, nbytes // 4], F32).ap()
        self.nbytes = nbytes
        self.off = 0

    def alloc(self, free_shape, dtype):
        n = int(np.prod(free_shape))
        esz = 2 if dtype == BF16 else 4
        nb = (n * esz + 31) // 32 * 32
        assert self.off + nb <= self.nbytes, ("SBUF arena overflow", self.off, nb, self.nbytes)
        ap = self.t[:, self.off // 4:(self.off + nb) // 4]
        self.off += nb
        if dtype == BF16:
            ap = ap.bitcast(BF16)
        ap = ap[:, 0:n]
        if len(free_shape) == 2:
            ap = ap.rearrange("p (a b) -> p a b", b=free_shape[1])
        elif len(free_shape) == 3:
            ap = ap.rearrange("p (a b c) -> p a b c", b=free_shape[1], c=free_shape[2])
        return ap

    def mark(self):
        return self.off

    def release(self, m):
        self.off = m


class Psum:
    def __init__(self, nc):
        self.t = nc.alloc_psum_tensor("psum_all", [128, 8, 512], F32).ap()

    def f32(self, bank, nbanks=1):
        ap = self.t[:, bank:bank + nbanks, :]
        return ap.rearrange("p a b -> p (a b)")

    def bf16(self, bank, nbanks, inner):
        ap = self.t[:, bank:bank + nbanks, :].rearrange("p a b -> p (a b)").bitcast(BF16)
        return ap.rearrange("p (c t) -> p c t", t=inner)


class Ctx:
    pass


def load_consts(P, A, C, ident_d):
    C.slot_c = P.slot("c_const")
    C.identf = A.alloc([128], F32)
    C.identb = A.alloc([128], BF16)
    C.onesf = A.alloc([128], F32)
    C.onesb = A.alloc([128], BF16)
    t = P.dma("sp", C.slot_c, C.identf, ident_d)
    t1 = P.op("dve", "tensor_copy", dict(out=C.identb, in_=C.identf), [t])
    t2 = P.op("dve", "memset", dict(ap=C.onesf, constant=1.0))
    t3 = P.op("dve", "memset", dict(ap=C.onesb, constant=1.0))
    return [t1, t2, t3]


def modulation(P, A, PS, C, cvec_d, ncv, mod_w, mod_b, res, deps):
    m1 = A.mark()
    cfm = A.alloc([ncv, 16], F32)
    screp = A.alloc([ncv, 16, 128], F32)
    sl = P.pslot("mod_c")
    tl = None
    for v in range(ncv):
        tl = P.dma("sp", sl, cfm[:, v, :], cvec_d[v].rearrange("(c p) -> p c", p=128),
                   allow_slow_non_contiguous=True)
    ts = P.op("act", "activation", dict(out=cfm, in_=cfm, func=AF.Silu), [tl, deps])
    tr = None
    for v in range(ncv):
        for c in range(16):
            tr = P.op("dve", "tensor_scalar", dict(
                out=screp[:, v, c, :], in0=C.onesf, scalar1=cfm[:, v, c:c + 1], scalar2=None, op0=ALU.mult),
                [ts, deps])
    allg = sorted(set(g for (v, g) in res))
    wt = [A.alloc([2048], F32) for _ in range(3)]
    brow = A.alloc([2048], F32)
    wslots = [P.pslot("pw%d" % i) for i in range(3)]
    bslot = P.pslot("mod_b")
    wfree = [deps, deps, deps]
    k = 0
    ev_prev = None
    for g in allg:
        vs = [v for v in range(ncv) if (v, g) in res]
        tb = P.dma("pool", bslot, brow, mod_b[g * 2048:(g + 1) * 2048].partition_broadcast(128), [ev_prev])
        last_mm = None
        for c in range(16):
            s = k % 3
            tw = P.dma("sp", wslots[s], wt[s], mod_w[c * 128:(c + 1) * 128, g * 2048:(g + 1) * 2048], [wfree[s]])
            for vi, v in enumerate(vs):
                for nb in range(4):
                    last_mm = P.op("pe", "matmul", dict(
                        out=PS.f32(4 * vi + nb), lhsT=screp[:, v, c, :], rhs=wt[s][:, nb * 512:(nb + 1) * 512],
                        start=(c == 0), stop=(c == 15)), [tw, tr, ev_prev if c == 0 else None])
            wfree[s] = last_mm
            k += 1
        for vi, v in enumerate(vs):
            ev_prev = P.op("dve", "tensor_tensor", dict(
                out=res[(v, g)], in0=PS.f32(4 * vi, 4), in1=brow, op=ALU.add), [last_mm, tb])
    return ev_prev


def diag_extract(P, A, C, row, out_fm, deps):
    m = A.mark()
    tmp = A.alloc([16, 128], F32)
    t = None
    for j in range(16):
        t = P.op("dve", "tensor_tensor", dict(out=tmp[:, j, :], in0=row[:, j * 128:(j + 1) * 128],
                                                         in1=C.identf, op=ALU.mult), [deps])
    t = P.op("dve", "tensor_reduce", dict(out=out_fm, in_=tmp, axis=AX.X, op=ALU.add), [t])
    return t


def prep_weight(P, A, PS, C, w_dram, col0, ncols, a_fm, brep, wdst, bias_row, deps, psum_bank0=0):
    m = A.mark()
    stage = [A.alloc([512], F32) for _ in range(3)]
    slots = [P.pslot("pw%d" % i) for i in range(3)]
    free = [deps, deps, deps]
    k = 0
    last = None
    nsl = ncols // 512
    assert nsl <= 4 or bias_row is None or True
    for s0 in range(0, nsl, 4):
        grp = list(range(s0, min(nsl, s0 + 4)))
        lastmm = {}
        for c in range(16):
            for sl in grp:
                b = k % 3
                tw = P.dma("sp", slots[b], stage[b],
                           w_dram[c * 128:(c + 1) * 128, col0 + sl * 512:col0 + (sl + 1) * 512], [free[b]])
                rd = []
                if a_fm is not None:
                    t1 = P.op("dve", "tensor_scalar", dict(
                        out=wdst[:, c, sl * 512:(sl + 1) * 512], in0=stage[b], scalar1=a_fm[:, c:c + 1], scalar2=None,
                        op0=ALU.mult), [tw, deps])
                else:
                    t1 = P.op("dve", "tensor_copy", dict(
                        out=wdst[:, c, sl * 512:(sl + 1) * 512], in_=stage[b]), [tw, deps])
                rd.append(t1)
                last = t1
                if bias_row is not None:
                    t2 = P.op("pe", "matmul", dict(
                        out=PS.f32(psum_bank0 + sl - s0), lhsT=brep[:, c, :], rhs=stage[b],
                        start=(c == 0), stop=(c == 15)), [tw, deps])
                    rd.append(t2)
                    lastmm[sl] = t2
                free[b] = rd
                k += 1
        if bias_row is not None:
            for sl in grp:
                last = P.op("act", "activation", dict(
                    out=bias_row[:, sl * 512:(sl + 1) * 512], in_=PS.f32(psum_bank0 + sl - s0), func=AF.Identity),
                    [lastmm[sl]])
            deps = [deps, last]
    return [last, t1]


class FrontEnd:
    def __init__(self, P, A, PS, C, pt_bank, keep_x=False):
        self.P, self.C, self.PS = P, C, PS
        self.xt = [A.alloc([2048], F32) for _ in range(2)]
        self.xn = [A.alloc([2048], BF16) for _ in range(2)]
        self.xnT = [A.alloc([16, 128], BF16) for _ in range(2)]
        self.junk = A.alloc([2048], BF16)
        self.ss = A.alloc([8], F32)
        self.rstd = A.alloc([8], F32)
        self.slots = [P.pslot("fe0"), P.pslot("fe1")]
        self.pt = PS.bf16(pt_bank, 2, 128)
        self.xt_free = [[], []]
        self.xn_free = [None, None]
        self.xnT_free = [[], []]
        self.pt_free = None
        self.n = 0

    def run(self, src_ap, deps=None):
        P, C = self.P, self.C
        i = self.n
        self.n += 1
        b = i % 2
        k = i % 8
        xt, xn, xnT = self.xt[b], self.xn[b], self.xnT[b]
        tx = P.dma("sp", self.slots[b], xt, src_ap, [self.xt_free[b], deps])
        tsq = P.op("act", "activation", dict(out=self.junk, in_=xt, func=AF.Square,
                                                   accum_out=self.ss[:, k:k + 1]), [tx])
        tms = P.op("dve", "tensor_scalar", dict(out=self.rstd[:, k:k + 1], in0=self.ss[:, k:k + 1],
                                                      scalar1=1.0 / D, scalar2=EPS, op0=ALU.mult, op1=ALU.add),
                     [tsq])
        tsr = P.op("act", "activation", dict(out=self.rstd[:, k:k + 1], in_=self.rstd[:, k:k + 1],
                                                   func=AF.Sqrt), [tms])
        trc = P.op("dve", "reciprocal", dict(out=self.rstd[:, k:k + 1], in_=self.rstd[:, k:k + 1]), [tsr])
        txn = P.op("dve", "tensor_scalar", dict(out=xn, in0=xt, scalar1=self.rstd[:, k:k + 1], scalar2=None,
                                                      op0=ALU.mult), [trc, self.xn_free[b]])
        tt = None
        for c in range(16):
            tt = P.op("pe", "transpose", dict(out=self.pt[:, c, :], in_=xn[:, c * 128:(c + 1) * 128],
                                                         identity=C.identb),
                        [txn, self.pt_free] if c == 0 else [])
        self.xn_free[b] = tt
        tcp = P.op("act", "activation", dict(out=xnT, in_=self.pt, func=AF.Copy), [tt, self.xnT_free[b]])
        self.pt_free = tcp
        self.xt_free[b] = [tsq, txn]
        self.xnT_free[b] = []
        self.cur = b
        return xnT, tcp, xt, tx, self.rstd[:, k:k + 1]

    def readers(self, b, toks, x_toks=()):
        self.xnT_free[b] = list(self.xnT_free[b]) + list(_flat(toks))
        self.xt_free[b] = list(self.xt_free[b]) + list(_flat(x_toks))


def head_post(P, A, src, nh, dst_bf, do_norm, gain_row, rope_cs, tmp, deps, scale=None):
    t = deps
    sq, st8 = tmp["sq"], tmp["st8"]
    if do_norm:
        t = P.op("dve", "tensor_tensor", dict(out=sq[:, 0:nh, :], in0=src, in1=src, op=ALU.mult), [t])
        t = P.op("dve", "tensor_reduce", dict(out=st8[:, 0:nh], in_=sq[:, 0:nh, :], axis=AX.X, op=ALU.add), [t])
        t = P.op("dve", "tensor_scalar", dict(out=st8[:, 0:nh], in0=st8[:, 0:nh], scalar1=1.0 / HD, scalar2=EPS,
                                                    op0=ALU.mult, op1=ALU.add), [t])
        t = P.op("act", "activation", dict(out=st8[:, 0:nh], in_=st8[:, 0:nh], func=AF.Sqrt), [t])
        t = P.op("dve", "reciprocal", dict(out=st8[:, 0:nh], in_=st8[:, 0:nh]), [t])
        t = P.op("dve", "tensor_tensor", dict(out=src, in0=src,
                                                    in1=st8[:, 0:nh].unsqueeze(2).broadcast_to([128, nh, 128]),
                                                    op=ALU.mult), [t])
        t = P.op("dve", "tensor_tensor", dict(out=src, in0=src,
                                                    in1=gain_row.unsqueeze(1).broadcast_to([128, nh, 128]),
                                                    op=ALU.mult), [t])
    elif scale is not None:
        t = P.op("dve", "tensor_scalar", dict(out=src, in0=src, scalar1=float(scale), scalar2=None,
                                                    op0=ALU.mult), [t])
    if rope_cs is not None:
        cosb = rope_cs[:, 0:64].unsqueeze(1).broadcast_to([128, nh, 64])
        sinb = rope_cs[:, 64:128].unsqueeze(1).broadcast_to([128, nh, 64])
        x1 = src[:, :, 0:64]
        x2 = src[:, :, 64:128]
        ta, tb_ = sq[:, 0:nh, 0:64], sq[:, 0:nh, 64:128]
        t = P.op("dve", "tensor_tensor", dict(out=ta, in0=x1, in1=cosb, op=ALU.mult), [t])
        t = P.op("dve", "tensor_tensor", dict(out=tb_, in0=x2, in1=sinb, op=ALU.mult), [t])
        t = P.op("dve", "tensor_tensor", dict(out=dst_bf[:, :, 0:64], in0=ta, in1=tb_, op=ALU.subtract), [t])
        t = P.op("dve", "tensor_tensor", dict(out=ta, in0=x2, in1=cosb, op=ALU.mult), [t])
        t = P.op("dve", "tensor_tensor", dict(out=tb_, in0=x1, in1=sinb, op=ALU.mult), [t])
        t = P.op("dve", "tensor_tensor", dict(out=dst_bf[:, :, 64:128], in0=ta, in1=tb_, op=ALU.add), [t])
    else:
        t = P.op("dve", "tensor_copy", dict(out=dst_bf, in_=src), [t])
    return t


def rep16(P, C, fm, rep, deps):
    t = None
    for c in range(16):
        t = P.op("dve", "tensor_scalar", dict(out=rep[:, c, :], in0=C.onesf, scalar1=fm[:, c:c + 1],
                                                         scalar2=None, op0=ALU.mult), [deps])
    return t


def adaln_vectors(P, A, PS, C, cvec_d, ncv, mod_w, mod_b, pre_g, post_g, G_d, deps):
    fms = [(A.alloc([16], F32), A.alloc([16], F32)) for _ in range(ncv)]
    m = A.mark()
    res = {}
    for v in range(ncv):
        res[(v, 0)] = A.alloc([2048], F32)
        res[(v, 1)] = A.alloc([2048], F32)
    res[(0, 2)] = A.alloc([2048], F32)
    prow = A.alloc([2048], F32)
    sl = P.pslot("adaln")
    tp = P.dma("pool", sl, prow, pre_g.partition_broadcast(128), [deps])
    tm = modulation(P, A, PS, C, cvec_d, ncv, mod_w, mod_b, res, deps)
    last = []
    for v in range(ncv):
        t = P.op("dve", "scalar_tensor_tensor", dict(out=res[(v, 1)], in0=res[(v, 1)], scalar=1.0, in1=prow,
                                                                op0=ALU.add, op1=ALU.mult), [tm, tp])
        t1 = diag_extract(P, A, C, res[(v, 1)], fms[v][0], [t])
        t2 = diag_extract(P, A, C, res[(v, 0)], fms[v][1], [tm])
        last += [t1, t2]
    tp2 = P.dma("pool", sl, prow, post_g.partition_broadcast(128), [last])
    tg = P.op("dve", "tensor_tensor", dict(out=res[(0, 2)], in0=res[(0, 2)], in1=prow, op=ALU.mult), [tm, tp2])
    tgd = P.dma("pool", sl, G_d, res[(0, 2)], [tg])
    last.append(tgd)
    A.release(m)
    return fms, last


def build_l0(debug=False):
    nc = bass.Bass("TRN2", target_bir_lowering=False)

    def din(name, shape, dt=F32):
        return nc.dram_tensor(name, list(shape), dt, kind="ExternalInput").ap()

    def dscr(name, shape, dt=BF16):
        return nc.dram_tensor(name, list(shape), dt, kind=("ExternalOutput" if debug else "Internal")).ap()

    xb = din("xb", [SEQ, D])
    xo = din("xo", [NTE * 128, D])
    ctx_d = din("ctx", [CTX, D])
    cvec = din("cvec", [2, D])
    mod_w = din("mod_w", [D, 3 * D])
    mod_b = din("mod_b", [3 * D])
    pre_g = din("pre_g", [D])
    post_g = din("post_g", [D])
    w_in = din("w_in", [D, 5120])
    q_norm = din("q_norm", [HD])
    k_norm = din("k_norm", [HD])
    sink = din("sink", [8])
    w_out = din("w_out", [D, D])
    ropeb = din("ropeb", [SEQ, 128])
    ropeo = din("ropeo", [NTE * 128, 128])
    masks_d = din("masks", [128, 4, 128])
    ident_d = din("ident", [128, 128])
    x1 = nc.dram_tensor("x1", [OWN, D], F32, kind="ExternalOutput").ap()

    G_d = dscr("G_d", [128, D], F32)
    KAT_d = dscr("KAT_d", [2, 128, NKC * 128])
    VA_d = dscr("VA_d", [NKC * 128, 2, 128])
    QT_d = dscr("QT_d", [16, 128, OWN])
    XNT_d = dscr("XNT_d", [NTO, 128, D])
    SZT_d = dscr("SZT_d", [16, 128, OWN])
    YT_d = dscr("YT_d", [16, 128, OWN])

    P = Prog(nc)
    A = Arena(nc, 206 * 1024)
    PS = Psum(nc)
    C = Ctx()

    cons = load_consts(P, A, C, ident_d)
    fms, t_ad = adaln_vectors(P, A, PS, C, cvec, 2, mod_w, mod_b, pre_g, post_g, G_d, cons)
    (A_fm, B_fm), (Ac_fm, Bc_fm) = fms
    Brep = A.alloc([16, 128], F32)
    t_brep = rep16(P, C, B_fm, Brep, t_ad)
    qg_row = A.alloc([128], F32)
    kg_row = A.alloc([128], F32)
    masks = A.alloc([4, 128], BF16)
    esink = A.alloc([8], F32)
    KBT = A.alloc([2, NTE * 128], BF16)
    VB = A.alloc([NTE, 2, 128], BF16)
    KBTc = A.alloc([2, CTX], BF16)
    VBc = A.alloc([2, 2, 128], BF16)
    sl0 = P.slot("p0misc")
    mk = A.mark()
    mstage = A.alloc([4, 128], F32)
    t1 = P.dma("pool", sl0, qg_row, q_norm.partition_broadcast(128), [t_ad, t_brep])
    t2 = P.dma("pool", sl0, kg_row, k_norm.partition_broadcast(128), [t_ad, t_brep])
    t3 = P.dma("pool", sl0, esink, sink.partition_broadcast(128), [t_ad, t_brep])
    t4 = P.dma("pool", sl0, mstage, masks_d, [t_ad, t_brep])
    tq = P.op("dve", "tensor_scalar", dict(out=qg_row, in0=qg_row, scalar1=float(ATTN_SCALE), scalar2=None,
                                                 op0=ALU.mult), [t1, t2, t3, t4])
    tmk = P.op("dve", "tensor_copy", dict(out=masks, in_=mstage), [tq])
    tes = P.op("act", "activation", dict(out=esink, in_=esink, func=AF.Exp), [t4])
    A.release(mk)
    p0_done = P.full_barrier()

    mkv = A.mark()
    Brepc = A.alloc([16, 128], F32)
    t_brc = rep16(P, C, Bc_fm, Brepc, p0_done)
    WB = A.alloc([16, 512], BF16)
    WA = A.alloc([16, 512], BF16)
    WC = A.alloc([16, 1024], BF16)
    bB = A.alloc([512], F32)
    bA = A.alloc([512], F32)
    bC = A.alloc([1024], F32)
    tw1 = prep_weight(P, A, PS, C, w_in, 2560, 512, A_fm, Brep, WB, bB, [t_brep, t_brc], psum_bank0=4)
    tw2 = prep_weight(P, A, PS, C, w_in, 2048, 512, A_fm, Brep, WA, bA, [tw1], psum_bank0=4)
    tw3 = prep_weight(P, A, PS, C, w_in, 2048, 1024, Ac_fm, Brepc, WC, bC, [tw2], psum_bank0=4)
    wready = [tw1, tw2, tw3]
    FE = FrontEnd(P, A, PS, C, pt_bank=0)
    pkv = PS.f32(2, 2)
    ptk = PS.bf16(4, 1, 128)
    kvf = [A.alloc([1024], F32) for _ in range(2)]
    kbf = [A.alloc([4, 128], BF16) for _ in range(2)]
    ropes = [A.alloc([128], F32) for _ in range(2)]
    rslots = [P.slot(), P.slot()]
    tmp = {"sq": A.alloc([8, 128], F32), "st8": A.alloc([8], F32)}
    kst = [A.alloc([2, 512], BF16) for _ in range(2)]
    vst = [A.alloc([2, 128], BF16) for _ in range(2)]
    kslots = [P.slot(), P.slot()]
    vslots = [P.slot(), P.slot()]
    st = Ctx()
    st.kvf_free = [None, None]
    st.kbf_free = [None, None]
    st.rope_free = [None, None]
    st.kst_free = [None, None]
    st.vst_free = [None, None]
    st.pkv_free = None
    st.ptk_free = None
    st.n = 0
    st.out_toks = []

    def kv_tile(src_ap, rope_ap, W, brow, ncols, mode, idx):
        i = st.n
        st.n += 1
        b = i % 2
        xnT, tready, xt, tx, _ = FE.run(src_ap, wready if i == 0 else None)
        fb = FE.cur
        if rope_ap is not None:
            trope = P.dma("pool", rslots[b], ropes[b], rope_ap, [st.rope_free[b]])
        else:
            trope = None
        nsl = ncols // 512
        mm = None
        for sl in range(nsl):
            for c in range(16):
                mm = P.op("pe", "matmul", dict(
                    out=pkv[:, sl * 512:(sl + 1) * 512], lhsT=xnT[:, c, :], rhs=W[:, c, sl * 512:(sl + 1) * 512],
                    start=(c == 0), stop=(c == 15)), [tready, st.pkv_free] if (c == 0 and sl == 0) else [])
        FE.readers(fb, [mm])
        kf = kvf[b]
        tev = P.op("dve", "tensor_tensor", dict(out=kf[:, 0:ncols], in0=pkv[:, 0:ncols], in1=brow[:, 0:ncols],
                                                      op=ALU.add), [mm, st.kvf_free[b]])
        st.pkv_free = tev
        kb = kbf[b]
        toks = []
        if mode == "bat":
            ksrc = kf[:, 0:256].rearrange("p (h d) -> p h d", d=128)
            tk = head_post(P, A, ksrc, 2, kb[:, 0:2, :], True, kg_row, ropes[b], tmp, [tev, trope, st.kbf_free[b]])
            st.rope_free[b] = tk
            nk = 2
        elif mode == "ext":
            ksrc = kf[:, 0:256].rearrange("p (h d) -> p h d", d=128)
            tk = head_post(P, A, ksrc, 2, kb[:, 0:2, :], False, None, ropes[b], tmp, [tev, trope, st.kbf_free[b]])
            st.rope_free[b] = tk
            nk = 2
        else:
            ksrc = kf[:, 0:256].rearrange("p (h d) -> p h d", d=128)
            tk = head_post(P, A, ksrc, 2, kb[:, 0:2, :], True, kg_row, None, tmp, [tev, st.kbf_free[b]])
            ksrc2 = kf[:, 512:768].rearrange("p (h d) -> p h d", d=128)
            tk = head_post(P, A, ksrc2, 2, kb[:, 2:4, :], False, None, None, tmp, [tk])
            nk = 4
        tt = None
        for h in range(nk):
            tt = P.op("pe", "transpose", dict(out=ptk[:, h, :], in_=kb[:, h, :], identity=C.identb),
                        [tk, st.ptk_free] if h == 0 else [])
        st.kbf_free[b] = tt
        if mode == "ext":
            tc = P.op("act", "activation", dict(out=KBT[:, :, idx * 128:(idx + 1) * 128], in_=ptk[:, 0:2, :],
                                                      func=AF.Copy), [tt])
            st.ptk_free = tc
            tv = P.op("dve", "tensor_copy", dict(out=VB[:, idx, :, :],
                                                       in_=kf[:, 256:512].rearrange("p (h d) -> p h d", d=128)), [tk])
            st.kvf_free[b] = tv
            toks += [tc, tv]
        else:
            grp = 4 if mode == "bat" else 2
            g, r = divmod(idx, grp)
            sb = g % 2
            tc = P.op("act", "activation", dict(out=kst[sb][:, :, r * 128:(r + 1) * 128], in_=ptk[:, 0:2, :],
                                                      func=AF.Copy), [tt, st.kst_free[sb] if r == 0 else None])
            if mode == "ctx":
                tc2 = P.op("act", "activation", dict(out=KBTc[:, :, idx * 128:(idx + 1) * 128],
                                                           in_=ptk[:, 2:4, :], func=AF.Copy), [tt])
                tc = tc2
            st.ptk_free = tc
            voff = 256
            vb_ = vst[b]
            tv = P.op("dve", "tensor_copy", dict(
                out=vb_, in_=kf[:, voff:voff + 256].rearrange("p (h d) -> p h d", d=128)), [tk, st.vst_free[b]])
            if mode == "ctx":
                tv2 = P.op("dve", "tensor_copy", dict(
                    out=VBc[:, idx, :, :], in_=kf[:, 768:1024].rearrange("p (h d) -> p h d", d=128)), [tk])
                st.kvf_free[b] = tv2
                toks.append(tv2)
            else:
                st.kvf_free[b] = tv
            base = (0 if mode == "bat" else SEQ)
            tok0 = base + idx * 128
            tvd = P.dma("pool", vslots[b], VA_d[tok0:tok0 + 128, :, :], vb_, [tv])
            st.vst_free[b] = tvd
            toks.append(tvd)
            if r == grp - 1:
                c0 = base + g * grp * 128
                tkd = P.dma("pool", kslots[sb], KAT_d[:, :, c0:c0 + grp * 128].rearrange("h d t -> d h t"),
                            kst[sb][:, :, 0:grp * 128], [tc])
                st.kst_free[sb] = tkd
                toks.append(tkd)
        st.out_toks = [st.out_toks[-8:], toks]
        st.all_toks.extend(toks)

    st.all_toks = []
    for e_ in range(NTE):
        kv_tile(xo[e_ * 128:(e_ + 1) * 128, :], ropeo[e_ * 128:(e_ + 1) * 128, :], WB, bB, 512, "ext", e_)
    for t_ in range(CTX // 128):
        kv_tile(ctx_d[t_ * 128:(t_ + 1) * 128, :], None, WC, bC, 1024, "ctx", t_)
    for t_ in range(NTB):
        kv_tile(xb[t_ * 128:(t_ + 1) * 128, :], ropeb[t_ * 128:(t_ + 1) * 128, :], WA, bA, 512, "bat", t_)
    pkv_done = P.full_barrier()
    A.release(mkv)

    m2 = A.mark()
    WQ = A.alloc([16, 2048], BF16)
    bQ = A.alloc([2048], F32)
    twq = prep_weight(P, A, PS, C, w_in, 0, 2048, A_fm, Brep, WQ, bQ, pkv_done, psum_bank0=2)
    FE = FrontEnd(P, A, PS, C, pt_bank=0)
    pq = [PS.f32(2, 2), PS.f32(4, 2)]
    ptq = PS.bf16(6, 1, 128)
    qf = [A.alloc([8, 128], F32) for _ in range(2)]
    qbf = [A.alloc([16, 128], BF16) for _ in range(2)]
    qT = [A.alloc([16, 128], BF16) for _ in range(2)]
    ropes = [A.alloc([128], F32) for _ in range(2)]
    rslots = [P.pslot("rope0"), P.pslot("rope1")]
    qslots = [P.pslot("q0"), P.pslot("q1")]
    xslots = [P.pslot("xn0"), P.pslot("xn1")]
    tmp = {"sq": A.alloc([8, 128], F32), "st8": A.alloc([8], F32)}
    pq_free = [None, None]
    qf_free = [None, None]
    qbf_free = [None, None]
    qT_free = [None, None]
    rope_free = [None, None]
    ptq_free = None
    for o in range(NTO):
        b = o % 2
        e_ = o + 1
        xnT, tready, xt, tx, _ = FE.run(xo[e_ * 128:(e_ + 1) * 128, :], twq if o == 0 else None)
        fb = FE.cur
        trope = P.dma("pool", rslots[b], ropes[b], ropeo[e_ * 128:(e_ + 1) * 128, :], [rope_free[b]])
        txd = P.dma("pool", xslots[b], XNT_d[o].rearrange("p (c t) -> p c t", t=128), xnT, [tready])
        mms = []
        for half in range(2):
            mm = None
            for sl2 in range(2):
                sl = half * 2 + sl2
                for c in range(16):
                    mm = P.op("pe", "matmul", dict(
                        out=pq[half][:, sl2 * 512:(sl2 + 1) * 512], lhsT=xnT[:, c, :],
                        rhs=WQ[:, c, sl * 512:(sl + 1) * 512], start=(c == 0), stop=(c == 15)),
                        [tready, pq_free[half]] if (c == 0 and sl2 == 0) else [])
            mms.append(mm)
        FE.readers(fb, [mms[1], txd])
        tpost = []
        for half in range(2):
            q3 = qf[half]
            tev = P.op("dve", "tensor_tensor", dict(
                out=q3.rearrange("p h d -> p (h d)"), in0=pq[half], in1=bQ[:, half * 1024:(half + 1) * 1024],
                op=ALU.add), [mms[half], qf_free[half]])
            pq_free[half] = tev
            if half == 0:
                tk = head_post(P, A, q3, 8, qbf[b][:, 0:8, :], True, qg_row, ropes[b], tmp,
                               [tev, trope, qbf_free[b]])
            else:
                tk = head_post(P, A, q3, 8, qbf[b][:, 8:16, :], False, None, ropes[b], tmp, [tev, trope],
                               scale=ATTN_SCALE)
            qf_free[half] = tk
            tpost.append(tk)
        rope_free[b] = tpost[1]
        tc = None
        for half in range(2):
            tt = None
            for h in range(8):
                tt = P.op("pe", "transpose", dict(
                    out=ptq[:, h, :], in_=qbf[b][:, half * 8 + h, :], identity=C.identb),
                    [tpost[half], ptq_free] if h == 0 else [])
            tc = P.op("act", "activation", dict(
                out=qT[b][:, half * 8:(half + 1) * 8, :], in_=ptq, func=AF.Copy),
                [tt, qT_free[b] if half == 0 else None])
            ptq_free = tc
        qbf_free[b] = tt
        tqd = P.dma("pool", qslots[b], QT_d[:, :, o * 128:(o + 1) * 128].rearrange("h d t -> d h t"), qT[b], [tc])
        qT_free[b] = tqd
    p2a_done = P.full_barrier()
    A.release(m2)

    p2b_done = gate_phase(P, A, PS, C, w_in, 3072, A_fm, Brep, XNT_d, SZT_d, None, p2a_done)

    m4 = A.mark()
    KATh = A.alloc([NKC * 128], BF16)
    VAh = A.alloc([NKC, 128], BF16)
    qblk = [A.alloc([4, 512], BF16) for _ in range(2)]
    zblk = [A.alloc([4, 512], BF16) for _ in range(2)]
    pT = [A.alloc([1024], BF16) for _ in range(3)]
    rs = [A.alloc([512], F32) for _ in range(2)]
    yf = [A.alloc([512], F32) for _ in range(2)]
    yb = [A.alloc([512], BF16) for _ in range(2)]
    kvs = [P.pslot("kvh0"), P.pslot("kvh1")]
    qs = [P.pslot("qb0"), P.pslot("qb1")]
    zs = [P.pslot("zb0"), P.pslot("zb1")]
    ys = [P.pslot("y0"), P.pslot("y1")]
    Sb = [PS.f32(0, 2), PS.f32(2, 2)]
    Ob = [PS.f32(4), PS.f32(5)]
    Ub = [PS.f32(6), PS.f32(7)]
    S_free = [None, None]
    pT_free = [None, None, None]
    acc_free = [None, None]
    rs_free = [None, None]
    yb_free = [None, None]
    qblk_free = [None, None]
    zblk_free = [None, None]
    kv_free = p2b_done
    NG = NKC // 2
    it = 0
    blkn = 0
    for kvh in range(2):
        tk1 = P.dma("sp", kvs[0], KATh, KAT_d[kvh], [kv_free])
        tk2 = P.dma("sp", kvs[1], VAh, VA_d[:, kvh, :].rearrange("(t p) d -> p t d", p=128), [kv_free])
        kvready = [tk1, tk2]
        lastuse = None
        for tb in range(OWN // 512):
            bb = blkn % 2
            blkn += 1
            tq = P.dma("sp", qs[bb], qblk[bb], QT_d[kvh * 4:(kvh + 1) * 4, :, tb * 512:(tb + 1) * 512]
                       .rearrange("h d t -> d h t"), [qblk_free[bb]])
            tz = P.dma("sp", zs[bb], zblk[bb], SZT_d[kvh * 4:(kvh + 1) * 4, :, tb * 512:(tb + 1) * 512]
                       .rearrange("h d t -> d h t"), [zblk_free[bb]])
            for hd in range(4):
                ab = it % 2
                it += 1
                qrhs = qblk[bb][:, hd, :]

                def QK(g, ab=ab, qrhs=qrhs):
                    sb = g % 2
                    t = None
                    for j in range(2):
                        kc = 2 * g + j
                        t = P.op("pe", "matmul", dict(
                            out=Sb[sb][:, j * 512:(j + 1) * 512], lhsT=KATh[:, kc * 128:(kc + 1) * 128], rhs=qrhs,
                            start=True, stop=True), [S_free[sb], tq, kvready] if j == 0 else [])
                    return t

                tqk = {0: QK(0), 1: QK(1)}
                tpv = None
                for g in range(NG):
                    sb = g % 2
                    pb = g % 3
                    tex = P.op("act", "activation", dict(out=pT[pb], in_=Sb[sb], func=AF.Exp),
                                 [tqk[g], pT_free[pb]])
                    S_free[sb] = tex
                    for j in range(2):
                        kc = 2 * g + j
                        P.op("pe", "matmul", dict(
                            out=Ob[ab], lhsT=VAh[:, kc, :], rhs=pT[pb][:, j * 512:(j + 1) * 512],
                            start=(g == 0 and j == 0), stop=(g == NG - 1 and j == 1)),
                            [tex, acc_free[ab]] if j == 0 else [])
                    for j in range(2):
                        tpv = P.op("pe", "matmul", dict(
                            out=Ub[ab], lhsT=C.onesb, rhs=pT[pb][:, j * 512:(j + 1) * 512],
                            start=(g == 0 and j == 0), stop=(g == NG - 1 and j == 1)), [])
                    pT_free[pb] = tpv
                    if g + 2 < NG:
                        tqk[g + 2] = QK(g + 2)
                t = P.op("dve", "reciprocal", dict(out=rs[ab], in_=Ub[ab]), [tpv, rs_free[ab]])
                t = P.op("dve", "tensor_tensor", dict(out=yf[ab], in0=Ob[ab], in1=rs[ab], op=ALU.mult), [t])
                acc_free[ab] = t
                t = P.op("dve", "tensor_tensor", dict(
                    out=yb[ab], in0=yf[ab], in1=zblk[bb][:, hd, :], op=ALU.mult), [t, tz, yb_free[ab]])
                rs_free[ab] = t
                td = P.dma("pool", ys[ab], YT_d[kvh * 4 + hd, :, tb * 512:(tb + 1) * 512], yb[ab], [t])
                yb_free[ab] = td
                lastuse = [tpv, t]
            qblk_free[bb] = lastuse
            zblk_free[bb] = lastuse
        kv_free = lastuse
    p3a_done = P.full_barrier()
    A.release(m4)

    m5 = A.mark()
    qw = [A.alloc([8, 128], BF16) for _ in range(2)]
    zw = [A.alloc([8, 128], BF16) for _ in range(2)]
    pT5 = [A.alloc([5, 512], BF16) for _ in range(2)]
    su = [A.alloc([512], F32) for _ in range(2)]
    yf = [A.alloc([512], F32) for _ in range(2)]
    yb = [A.alloc([4, 128], BF16) for _ in range(2)]
    qs = [P.pslot("qb0"), P.pslot("qb1")]
    zs = [P.pslot("zb0"), P.pslot("zb1")]
    ys = [P.pslot("y0"), P.pslot("y1")]
    S5 = PS.f32(0, 5)
    Ob = PS.f32(5)
    Ub = PS.f32(6)
    S_free = None
    acc_free = None
    pT_free = [None, None]
    su_free = [None, None]
    yb_free = [None, None]
    qw_free = [None, None]
    it = 0
    for o in range(NTO):
        bb = o % 2
        tq = P.dma("sp", qs[bb], qw[bb], QT_d[8:16, :, o * 128:(o + 1) * 128].rearrange("h d t -> d h t"),
                   [qw_free[bb], p3a_done if o < 2 else None])
        tz = P.dma("sp", zs[bb], zw[bb], SZT_d[8:16, :, o * 128:(o + 1) * 128].rearrange("h d t -> d h t"),
                   [qw_free[bb], p3a_done if o < 2 else None])
        lastuse = None
        for kvh in range(2):
            ab = it % 2
            it += 1
            qrhs = qw[bb][:, kvh * 4:(kvh + 1) * 4, :]
            mm = None
            for j in range(5):
                if j < 3:
                    lhs = KBT[:, kvh, (o + j) * 128:(o + j + 1) * 128]
                else:
                    lhs = KBTc[:, kvh, (j - 3) * 128:(j - 2) * 128]
                mm = P.op("pe", "matmul", dict(
                    out=S5[:, j * 512:(j + 1) * 512], lhsT=lhs, rhs=qrhs, start=True, stop=True),
                    [S_free, tq] if j == 0 else [])
            p5 = pT5[ab]
            tex = P.op("act", "activation", dict(out=p5.rearrange("p a b -> p (a b)"), in_=S5,
                                                              func=AF.Exp), [mm, pT_free[ab]])
            S_free = tex
            mlo = masks[:, 2 if o == 0 else 0, :].unsqueeze(1).broadcast_to([128, 4, 128])
            mhi = masks[:, 3 if o == NTO - 1 else 1, :].unsqueeze(1).broadcast_to([128, 4, 128])
            v0 = p5[:, 0, :].rearrange("p (h t) -> p h t", t=128)
            v2 = p5[:, 2, :].rearrange("p (h t) -> p h t", t=128)
            tm = P.op("dve", "tensor_tensor", dict(out=v0, in0=v0, in1=mlo, op=ALU.mult), [tex])
            tm = P.op("dve", "tensor_tensor", dict(out=v2, in0=v2, in1=mhi, op=ALU.mult), [tm])
            tpv = None
            for j in range(5):
                if j < 3:
                    lhs = VB[:, o + j, kvh, :]
                else:
                    lhs = VBc[:, j - 3, kvh, :]
                P.op("pe", "matmul", dict(
                    out=Ob, lhsT=lhs, rhs=p5[:, j, :], start=(j == 0), stop=(j == 4)),
                    [tm, acc_free] if j == 0 else [])
            for j in range(5):
                tpv = P.op("pe", "matmul", dict(
                    out=Ub, lhsT=C.onesb, rhs=p5[:, j, :], start=(j == 0), stop=(j == 4)), [])
            pT_free[ab] = tpv
            es = esink[:, kvh * 4:(kvh + 1) * 4].unsqueeze(2).broadcast_to([128, 4, 128])
            t = P.op("dve", "tensor_tensor", dict(
                out=su[ab].rearrange("p (h t) -> p h t", t=128), in0=Ub.rearrange("p (h t) -> p h t", t=128),
                in1=es, op=ALU.add), [tpv, su_free[ab]])
            t = P.op("dve", "reciprocal", dict(out=su[ab], in_=su[ab]), [t])
            t = P.op("dve", "tensor_tensor", dict(out=yf[ab], in0=Ob, in1=su[ab], op=ALU.mult), [t])
            acc_free = t
            t = P.op("dve", "tensor_tensor", dict(
                out=yb[ab].rearrange("p h t -> p (h t)"), in0=yf[ab],
                in1=zw[bb][:, kvh * 4:(kvh + 1) * 4, :].rearrange("p h t -> p (h t)"), op=ALU.mult),
                [t, tz, yb_free[ab]])
            su_free[ab] = t
            td = P.dma("pool", ys[ab], YT_d[8 + kvh * 4:8 + (kvh + 1) * 4, :, o * 128:(o + 1) * 128]
                       .rearrange("h d t -> d h t"), yb[ab], [t])
            yb_free[ab] = td
            lastuse = [tpv, t]
        qw_free[bb] = lastuse
    p3b_done = P.full_barrier()
    A.release(m5)

    tl4 = out_proj_phase(P, A, PS, C, w_out, YT_d, G_d, lambda o: xo[(o + 1) * 128:(o + 2) * 128, :], x1, p3b_done)
    P.wait_only("sp", tl4)
    P.wait_only("pool", tl4)
    P.build()
    return nc


def gate_phase(P, A, PS, C, w_dram, col0, A_fm, Brep, XNT_d, OUT_d, MUL_d, deps):
    m3 = A.mark()
    WZ = A.alloc([16, 2048], BF16)
    bZ = A.alloc([2048], F32)
    bz_fm = A.alloc([16], F32)
    twz = prep_weight(P, A, PS, C, w_dram, col0, 2048, A_fm, Brep, WZ, bZ, deps, psum_bank0=4)
    tbz = diag_extract(P, A, C, bZ, bz_fm, twz)
    blk = [A.alloc([16, 512], BF16) for _ in range(2)]
    bslots = [P.pslot("blk0"), P.pslot("blk1")]
    szb = [A.alloc([512], BF16) for _ in range(4)]
    sslots = [P.pslot("sz%d" % i) for i in range(4)]
    blk_free = [None, None]
    szb_free = [None] * 4
    pz = [PS.f32(0), PS.f32(1), PS.f32(2), PS.f32(3)]
    pz_free = [None] * 4
    if MUL_d is not None:
        mblk = [A.alloc([16, 512], BF16) for _ in range(2)]
        mslots = [P.pslot("mb0"), P.pslot("mb1")]
    k = 0
    for tb in range(OWN // 512):
        b = tb % 2
        tl = None
        for j in range(4):
            tl = P.dma("sp", bslots[b], blk[b][:, :, j * 128:(j + 1) * 128],
                       XNT_d[tb * 4 + j].rearrange("p (c t) -> p c t", t=128),
                       [blk_free[b], [twz, tbz] if tb < 2 else None])
        tmul = None
        if MUL_d is not None:
            tmul = P.dma("sp", mslots[b], mblk[b], MUL_d[:, :, tb * 512:(tb + 1) * 512].rearrange("c d t -> d c t"),
                         [blk_free[b], [twz, tbz] if tb < 2 else None])
        lastmm = None
        lastrd = None
        for f in range(16):
            pb = k % 4
            mm = None
            for c in range(16):
                mm = P.op("pe", "matmul", dict(
                    out=pz[pb], lhsT=WZ[:, c, f * 128:(f + 1) * 128], rhs=blk[b][:, c, :],
                    start=(c == 0), stop=(c == 15)), [tl, pz_free[pb], twz] if c == 0 else [])
            ta = P.op("act", "activation", dict(
                out=szb[pb], in_=pz[pb], func=AF.Silu, bias=bz_fm[:, f:f + 1]), [mm, szb_free[pb], tbz])
            pz_free[pb] = ta
            if MUL_d is not None:
                ta = P.op("dve", "tensor_tensor", dict(out=szb[pb], in0=szb[pb], in1=mblk[b][:, f, :], op=ALU.mult),
                          [ta, tmul])
                lastrd = ta
            td = P.dma("pool", sslots[pb], OUT_d[f, :, tb * 512:(tb + 1) * 512], szb[pb], [ta])
            szb_free[pb] = td
            lastmm = mm
            k += 1
        blk_free[b] = [lastmm, lastrd]
    done = P.full_barrier()
    A.release(m3)
    return done


def out_proj_phase(P, A, PS, C, w_out, YT_d, G_d, xsrc, xdst, deps):
    m = A.mark()
    WO = A.alloc([16, 2048], BF16)
    G = A.alloc([2048], F32)
    gs = P.pslot("adaln")
    tg = P.dma("pool", gs, G, G_d, [deps])
    two = prep_weight(P, A, PS, C, w_out, 0, 2048, None, None, WO, None, deps)
    yblk = [A.alloc([16, 512], BF16) for _ in range(2)]
    xt = [A.alloc([2048], F32) for _ in range(2)]
    xo_ = [A.alloc([2048], F32) for _ in range(2)]
    tmpf = A.alloc([2048], F32)
    junk = A.alloc([2048], BF16)
    ss = A.alloc([8], F32)
    bsl = [P.pslot("blk0"), P.pslot("blk1")]
    xsl = [P.pslot("fe0"), P.pslot("fe1")]
    osl = [P.pslot("y0"), P.pslot("y1")]
    po = [PS.f32(0, 4), PS.f32(4, 4)]
    po_free = [None, None]
    yblk_free = [None, None]
    xt_free = [None, None]
    xo_free = [None, None]
    outs = []
    tl = None
    for o in range(NTO):
        b = o % 2
        tb, j = divmod(o, 4)
        bb = tb % 2
        if j == 0:
            tl = P.dma("sp", bsl[bb], yblk[bb], YT_d[:, :, tb * 512:(tb + 1) * 512].rearrange("c d t -> d c t"),
                       [yblk_free[bb], deps])
        tx = P.dma("sp", xsl[b], xt[b], xsrc(o), [xt_free[b], deps])
        mm = None
        for sl in range(4):
            for c in range(16):
                mm = P.op("pe", "matmul", dict(
                    out=po[b][:, sl * 512:(sl + 1) * 512], lhsT=yblk[bb][:, c, j * 128:(j + 1) * 128],
                    rhs=WO[:, c, sl * 512:(sl + 1) * 512], start=(c == 0), stop=(c == 15)),
                    [tl, two, po_free[b]] if (c == 0 and sl == 0) else [])
        if j == 3:
            yblk_free[bb] = mm
        k = o % 8
        tsq = P.op("act", "activation", dict(out=junk, in_=po[b], func=AF.Square,
                                                             accum_out=ss[:, k:k + 1]), [mm])
        t = P.op("dve", "tensor_scalar", dict(out=ss[:, k:k + 1], in0=ss[:, k:k + 1], scalar1=1.0 / D,
                                                         scalar2=EPS, op0=ALU.mult, op1=ALU.add), [tsq])
        t = P.op("act", "activation", dict(out=ss[:, k:k + 1], in_=ss[:, k:k + 1], func=AF.Sqrt), [t])
        t = P.op("dve", "reciprocal", dict(out=ss[:, k:k + 1], in_=ss[:, k:k + 1]), [t])
        t = P.op("dve", "scalar_tensor_tensor", dict(
            out=tmpf, in0=po[b], scalar=ss[:, k:k + 1], in1=G, op0=ALU.mult, op1=ALU.mult), [t, tg])
        po_free[b] = t
        t = P.op("dve", "tensor_tensor", dict(out=xo_[b], in0=tmpf, in1=xt[b], op=ALU.add),
                   [t, tx, xo_free[b]])
        xt_free[b] = t
        td = P.dma("pool", osl[b], xdst[o * 128:(o + 1) * 128, :], xo_[b], [t])
        xo_free[b] = td
        outs.append(td)
    A.release(m)
    return outs[-2:]


POOL_SIZES = (2, 4, 8, 16)


def pool_phase(P, A, PS, C, xe, w_in, pool_w, pool_scale, pm_d, ic_d, A_fm, Brep, XNT_d, MT_d, deps):
    m = A.mark()
    WU = A.alloc([16, 2048], BF16)
    bU = A.alloc([2048], F32)
    twu = prep_weight(P, A, PS, C, w_in, 0, 2048, A_fm, Brep, WU, bU, deps, psum_bank0=2)
    PW = A.alloc([4, 4, 512], BF16)
    pwst = A.alloc([4, 512], F32)
    psc = A.alloc([16], F32)
    PM = A.alloc([36, 128], BF16)
    IC = A.alloc([12, 128], F32)
    sl = P.pslot("adaln")
    tl = None
    for g in range(4):
        tl = P.dma("pool", sl, pwst, pool_w[g].rearrange("(ci p) d -> p ci d", p=128), [twu, tl])
        tl = P.op("dve", "tensor_copy", dict(out=PW[:, g, :, :], in_=pwst), [tl])
    t1 = P.dma("pool", sl, psc, pool_scale.rearrange("(c p) -> p c", p=128), [tl], allow_slow_non_contiguous=True)
    t2 = P.dma("pool", sl, PM, pm_d, [t1])
    t3 = P.dma("pool", sl, IC, ic_d.partition_broadcast(128), [t2])
    t4 = t3
    ready = [twu, t4]
    FE = FrontEnd(P, A, PS, C, pt_bank=0)
    pu = PS.f32(2, 4)
    pp = PS.f32(6).rearrange("p (a b) -> p a b", b=128)
    pmx = PS.f32(7).rearrange("p (a b) -> p a b", b=128)
    ub = [A.alloc([2048], BF16) for _ in range(4)]
    pl = [A.alloc([4, 128], BF16) for _ in range(2)]
    mt = [A.alloc([16, 128], BF16) for _ in range(2)]
    xslots = [P.pslot("xn0"), P.pslot("xn1")]
    mslots = [P.pslot("q0"), P.pslot("q1")]
    ub_free = [None] * 4
    ub_ready = [None] * 4
    pu_free = None
    pp_free = None
    pmx_free = None
    pl_free = [None, None]
    mt_free = [None, None]
    gi = 0
    for e_ in range(NTE):
        xnT, tready, xt, tx, _ = FE.run(xe[e_ * 128:(e_ + 1) * 128, :], ready if e_ == 0 else None)
        fb = FE.cur
        rd = []
        if 1 <= e_ <= NTO:
            txd = P.dma("pool", xslots[e_ % 2], XNT_d[e_ - 1].rearrange("p (c t) -> p c t", t=128), xnT, [tready])
            rd.append(txd)
        mm = None
        for s4 in range(4):
            for c in range(16):
                mm = P.op("pe", "matmul", dict(
                    out=pu[:, s4 * 512:(s4 + 1) * 512], lhsT=xnT[:, c, :], rhs=WU[:, c, s4 * 512:(s4 + 1) * 512],
                    start=(c == 0), stop=(c == 15)), [tready, pu_free] if (c == 0 and s4 == 0) else [])
        rd.append(mm)
        FE.readers(fb, rd)
        ui = e_ % 4
        tev = P.op("dve", "tensor_tensor", dict(out=ub[ui], in0=pu, in1=bU, op=ALU.add), [mm, ub_free[ui]])
        pu_free = tev
        ub_ready[ui] = tev
        if e_ >= 2:
            o = e_ - 2
            kind = 0 if o == 0 else (2 if o == NTO - 1 else 1)
            mb = o % 2
            lastp = None
            for g in range(4):
                pb = gi % 2
                gi += 1
                for fc in range(4):
                    f = g * 4 + fc
                    for r in range(3):
                        lastp = P.op("pe", "matmul", dict(
                            out=pp[:, fc, :], lhsT=ub[(o + r) % 4][:, f * 128:(f + 1) * 128],
                            rhs=PM[:, (kind * 4 + g) * 3 + r, :], start=(r == 0), stop=(r == 2)),
                            [ub_ready[(o + r) % 4], pp_free] if fc == 0 else [])
                icb = IC[:, kind * 4 + g, :].unsqueeze(1).broadcast_to([128, 4, 128])
                tpl = P.op("dve", "tensor_tensor", dict(out=pl[pb], in0=pp, in1=icb, op=ALU.mult),
                           [lastp, pl_free[pb]])
                pp_free = tpl
                lm = None
                for fo in range(4):
                    for ci in range(4):
                        lm = P.op("pe", "matmul", dict(
                            out=pmx[:, fo, :], lhsT=PW[:, g, ci, fo * 128:(fo + 1) * 128], rhs=pl[pb][:, ci, :],
                            start=(ci == 0), stop=(ci == 3)), [tpl, pmx_free] if (fo == 0 and ci == 0) else [])
                pl_free[pb] = lm
                pscb = psc[:, g * 4:(g + 1) * 4].unsqueeze(2).broadcast_to([128, 4, 128])
                tmx = P.op("dve", "tensor_tensor", dict(out=mt[mb][:, g * 4:(g + 1) * 4, :], in0=pmx, in1=pscb,
                                                        op=ALU.mult), [lm, mt_free[mb] if g == 0 else None])
                pmx_free = tmx
            ub_free[o % 4] = lastp
            tmd = P.dma("pool", mslots[mb], MT_d[:, :, o * 128:(o + 1) * 128].rearrange("c d t -> d c t"), mt[mb],
                        [tmx])
            mt_free[mb] = tmd
    done = P.full_barrier()
    A.release(m)
    return done


def build_l1(debug=False):
    nc = bass.Bass("TRN2", target_bir_lowering=False)

    def din(name, shape, dt=F32):
        return nc.dram_tensor(name, list(shape), dt, kind="ExternalInput").ap()

    def dscr(name, shape, dt=BF16):
        return nc.dram_tensor(name, list(shape), dt, kind=("ExternalOutput" if debug else "Internal")).ap()

    xe = din("xe", [NTE * 128, D])
    cvec = din("cvec", [1, D])
    mod_w = din("mod_w", [D, 3 * D])
    mod_b = din("mod_b", [3 * D])
    pre_g = din("pre_g", [D])
    post_g = din("post_g", [D])
    w_in = din("w_in", [D, 4096])
    pool_w = din("pool_w", [4, 512, 512])
    pool_scale = din("pool_scale", [D])
    w_out = din("w_out", [D, D])
    pm_d = din("pm", [128, 36, 128], BF16)
    ic_d = din("ic", [12 * 128])
    ident_d = din("ident", [128, 128])
    out = nc.dram_tensor("out", [OWN, D], F32, kind="ExternalOutput").ap()
    G_d = dscr("G1_d", [128, D], F32)
    XNT_d = dscr("XNT1_d", [NTO, 128, D])
    MT_d = dscr("MT_d", [16, 128, OWN])
    YT_d = dscr("YT1_d", [16, 128, OWN])

    P = Prog(nc)
    A = Arena(nc, 206 * 1024)
    PS = Psum(nc)
    C = Ctx()
    cons = load_consts(P, A, C, ident_d)
    fms, t_ad = adaln_vectors(P, A, PS, C, cvec, 1, mod_w, mod_b, pre_g, post_g, G_d, cons)
    (A_fm, B_fm), = fms
    Brep = A.alloc([16, 128], F32)
    rep16(P, C, B_fm, Brep, t_ad)
    p0_done = P.full_barrier()
    pa_done = pool_phase(P, A, PS, C, xe, w_in, pool_w, pool_scale, pm_d, ic_d.rearrange("(a b) -> a b", b=128)
                         if False else ic_d, A_fm, Brep, XNT_d, MT_d, p0_done)
    pb_done = gate_phase(P, A, PS, C, w_in, 2048, A_fm, Brep, XNT_d, YT_d, MT_d, pa_done)
    tl4 = out_proj_phase(P, A, PS, C, w_out, YT_d, G_d, lambda o: xe[(o + 1) * 128:(o + 2) * 128, :], out, pb_done)
    P.wait_only("sp", tl4)
    P.wait_only("pool", tl4)
    P.build()
    return nc


def _rope_table(pos):
    pos = np.asarray(pos)
    row = (pos // GRID_W).astype(np.float32)
    col = (pos % GRID_W).astype(np.float32)
    inv = (np.float32(10000.0) ** (-np.arange(32, dtype=np.float32) / np.float32(32))).astype(np.float32)
    ang = np.concatenate([row[:, None] * inv, col[:, None] * inv], axis=-1).astype(np.float32)
    return np.concatenate([np.cos(ang), np.sin(ang)], axis=-1).astype(np.float32)


def _ext_rows(xb_, j, halo=128):
    out = np.zeros((OWN + 2 * halo,) + xb_.shape[1:], dtype=xb_.dtype)
    lo = j * OWN - halo
    hi = (j + 1) * OWN + halo
    slo, shi = max(lo, 0), min(hi, xb_.shape[0])
    out[slo - lo:shi - lo] = xb_[slo:shi]
    return out


def _masks(j):
    kl = np.arange(128)[:, None]
    ql = np.arange(128)[None, :]
    lo = (kl >= ql).astype(np.float32)
    hi = (kl <= ql).astype(np.float32)
    m = np.stack([lo, hi, lo * (1.0 if j > 0 else 0.0), hi * (1.0 if j < 3 else 0.0)], axis=1)
    return np.ascontiguousarray(m.astype(np.float32))


_NC_CACHE = {}


def run_l0(inp, debug=False):
    key = "l0d" if debug else "l0"
    if key not in _NC_CACHE:
        _NC_CACHE[key] = build_l0(debug=debug)
    nc = _NC_CACHE[key]
    x = np.asarray(inp["x"], dtype=np.float32)
    ropeb = _rope_table(np.arange(SEQ))
    ident = np.eye(128, dtype=np.float32)
    in_maps = []
    for core in range(8):
        b, j = divmod(core, 4)
        pos = np.clip(np.arange(j * OWN - 128, (j + 1) * OWN + 128), 0, SEQ - 1)
        in_maps.append({
            "xb": np.ascontiguousarray(x[b]),
            "xo": _ext_rows(x[b], j),
            "ctx": np.ascontiguousarray(np.asarray(inp["ctx"], np.float32)[b]),
            "cvec": np.ascontiguousarray(np.stack([np.asarray(inp["c"], np.float32)[b],
                                                   np.asarray(inp["c_ctx"], np.float32)])),
            "mod_w": np.ascontiguousarray(np.asarray(inp["ev_mod_w"], np.float32)[0]),
            "mod_b": np.ascontiguousarray(np.asarray(inp["ev_mod_b"], np.float32)[0]),
            "pre_g": np.ascontiguousarray(np.asarray(inp["ev_pre_g"], np.float32)[0]),
            "post_g": np.ascontiguousarray(np.asarray(inp["ev_post_g"], np.float32)[0]),
            "w_in": np.ascontiguousarray(np.asarray(inp["ev_w_in"], np.float32)[0]),
            "q_norm": np.ascontiguousarray(np.asarray(inp["ev_q_norm"], np.float32)[0]),
            "k_norm": np.ascontiguousarray(np.asarray(inp["ev_k_norm"], np.float32)[0]),
            "sink": np.ascontiguousarray(np.asarray(inp["ev_sink"], np.float32)[0]),
            "w_out": np.ascontiguousarray(np.asarray(inp["ev_w_out"], np.float32)[0]),
            "ropeb": ropeb,
            "ropeo": _rope_table(pos),
            "masks": _masks(j),
            "ident": ident,
        })
    res = run_bass_kernel_spmd(nc, in_maps, core_ids=list(range(8)))
    x1 = np.empty_like(x)
    for core in range(8):
        b, j = divmod(core, 4)
        x1[b, j * OWN:(j + 1) * OWN] = res.results[core]["x1"]
    if debug:
        return x1, res.results
    return x1


def _pool_tables(j):
    pm = np.zeros((128, 36, 128), np.float32)
    ic = np.zeros((12, 128), np.float32)
    bases = [j * OWN, 5 * 128, (j + 1) * OWN - 128]
    for kind, base in enumerate(bases):
        for g, w in enumerate(POOL_SIZES):
            half = w // 2
            t = base + np.arange(128)
            lo = np.clip(t - half, 0, SEQ)
            hi = np.clip(t + half, 0, SEQ)
            cnt = (hi - lo).astype(np.float32)
            ic[kind * 4 + g] = 1.0 / cnt
            for r in range(3):
                sidx = base + (r - 1) * 128 + np.arange(128)
                mtx = ((sidx[:, None] >= lo[None, :]) & (sidx[:, None] < hi[None, :])).astype(np.float32)
                mtx -= (sidx[:, None] == t[None, :]).astype(np.float32) * cnt[None, :]
                pm[:, (kind * 4 + g) * 3 + r, :] = mtx
    return pm, ic.reshape(-1)


def run_l1(inp, x1, debug=False):
    key = "l1d" if debug else "l1"
    if key not in _NC_CACHE:
        _NC_CACHE[key] = build_l1(debug=debug)
    nc = _NC_CACHE[key]
    ident = np.eye(128, dtype=np.float32)
    in_maps = []
    for core in range(8):
        b, j = divmod(core, 4)
        pm, ic = _pool_tables(j)
        in_maps.append({
            "xe": _ext_rows(x1[b], j),
            "cvec": np.ascontiguousarray(np.asarray(inp["c"], np.float32)[b][None]),
            "mod_w": np.ascontiguousarray(np.asarray(inp["od_mod_w"], np.float32)[0]),
            "mod_b": np.ascontiguousarray(np.asarray(inp["od_mod_b"], np.float32)[0]),
            "pre_g": np.ascontiguousarray(np.asarray(inp["od_pre_g"], np.float32)[0]),
            "post_g": np.ascontiguousarray(np.asarray(inp["od_post_g"], np.float32)[0]),
            "w_in": np.ascontiguousarray(np.asarray(inp["od_w_in"], np.float32)[0]),
            "pool_w": np.ascontiguousarray(np.asarray(inp["od_pool_w"], np.float32)[0]),
            "pool_scale": np.ascontiguousarray(np.asarray(inp["od_pool_scale"], np.float32)[0]),
            "w_out": np.ascontiguousarray(np.asarray(inp["od_w_out"], np.float32)[0]),
            "pm": pm.astype(ml_dtypes.bfloat16), "ic": ic, "ident": ident,
        })
    res = run_bass_kernel_spmd(nc, in_maps, core_ids=list(range(8)))
    out = np.empty_like(x1)
    for core in range(8):
        b, j = divmod(core, 4)
        out[b, j * OWN:(j + 1) * OWN] = res.results[core]["out"]
    if debug:
        return out, res.results
    return out


def kernel(**inputs):
    x1 = run_l0(inputs)
    return run_l1(inputs, x1)
```

```python
import numpy as np
import ml_dtypes
import concourse.bass as bass
import concourse.mybir as mybir
from concourse.bass_utils import run_bass_kernel_spmd

F32 = mybir.dt.float32
BF16 = mybir.dt.bfloat16
AF = mybir.ActivationFunctionType
ALU = mybir.AluOpType
AX = mybir.AxisListType

D = 2048
SEQ = 16384
NBATCH = 2
CTX = 256
HD = 128
OWN = 4096
NTO = OWN // 128
NTE = NTO + 2
NTB = SEQ // 128
NKC = NTB + CTX // 128
NQ = NTO + 2
NKB = NQ + 2
QCOLS = NQ * 128


def slotof(q):
    return q - 1 if 1 <= q <= NTO else (NTO if q == 0 else NTO + 1)


def blocks_of(ntiles):
    out = []
    c = 0
    while c < ntiles * 128:
        w = min(512, ntiles * 128 - c)
        out.append((c, w))
        c += w
    return out
EPS = 1e-6
ATTN_SCALE = HD ** -0.5
GRID_W = 64
ENGS = ("pe", "act", "dve", "pool", "sp")


class Tok:
    __slots__ = ("sem", "val")

    def __init__(self, sem, val):
        self.sem = sem
        self.val = val


class DmaSlot:
    def __init__(self, prog, name):
        self.sem = prog.nc.alloc_semaphore(name)
        self.count = 0


class Prog:
    def __init__(self, nc):
        self.nc = nc
        self.ops = {e: [] for e in ENGS}
        self.sems = {e: nc.alloc_semaphore("s_" + e) for e in ENGS}
        self.cnt = {e: 0 for e in ENGS}
        self.nslot = 0
        self.slots = []
        self.named = {}

    def slot(self, name=None):
        self.nslot += 1
        sl = DmaSlot(self, name or ("dslot%d" % self.nslot))
        self.slots.append(sl)
        return sl

    def pslot(self, name):
        if name not in self.named:
            self.named[name] = self.slot(name)
        return self.named[name]

    def emit(self, eng, fn, waits=()):
        self.cnt[eng] += 1
        tok = Tok(self.sems[eng], self.cnt[eng])
        self.ops[eng].append((fn, _flat(waits), (self.sems[eng], 1)))
        return tok

    def op(self, eng, method, kwargs, waits=()):
        kw = dict(kwargs)
        return self.emit(eng, lambda e, m=method, k=kw: getattr(e, m)(**k), waits)

    def dma(self, eng, slot, out, in_, waits=(), **kw):
        slot.count += 16
        tok = Tok(slot.sem, slot.count)
        self.ops[eng].append((lambda e, o=out, i=in_, k=kw: e.dma_start(out=o, in_=i, **k),
                              _flat(waits), (slot.sem, 16)))
        return tok

    def wait_only(self, eng, waits):
        self.ops[eng].append((None, _flat(waits), None))

    def barrier(self, toks):
        for e in ENGS:
            self.wait_only(e, toks)

    def last(self, eng):
        return Tok(self.sems[eng], self.cnt[eng]) if self.cnt[eng] else None

    def full_barrier(self):
        toks = [self.last(e) for e in ENGS] + [Tok(s.sem, s.count) for s in self.slots if s.count]
        self.barrier(toks)
        return toks

    def build(self):
        nc = self.nc
        with nc.Block() as block:
            def make(engname):
                def body(e):
                    seen = {}
                    for fn, waits, inc in self.ops[engname]:
                        for w in waits:
                            key = id(w.sem)
                            if seen.get(key, 0) >= w.val:
                                continue
                            e.wait_ge(w.sem, w.val)
                            seen[key] = w.val
                        if fn is not None:
                            inst = fn(e)
                            inst.then_inc(inc[0], inc[1])
                return body
            block.tensor(make("pe"))
            block.scalar(make("act"))
            block.vector(make("dve"))
            block.gpsimd(make("pool"))
            block.sync(make("sp"))


def _flat(waits):
    out = []
    for w in waits:
        if w is None:
            continue
        if isinstance(w, (list, tuple)):
            out.extend(_flat(w))
        else:
            out.append(w)
    return tuple(out)


class Arena:
    def __init__(self, nc, nbytes):
        self.t = nc.alloc_sbuf_tensor("arena", [128, nbytes // 4], F32).ap()
        self.nbytes = nbytes
        self.off = 0

    def alloc(self, free_shape, dtype):
        n = int(np.prod(free_shape))
        esz = 2 if dtype == BF16 else 4
        nb = (n * esz + 31) // 32 * 32
        assert self.off + nb <= self.nbytes, ("SBUF arena overflow", self.off, nb, self.nbytes)
        ap = self.t[:, self.off // 4:(self.off + nb) // 4]
        self.off += nb
        if dtype == BF16:
            ap = ap.bitcast(BF16)
        ap = ap[:, 0:n]
        if len(free_shape) == 2:
            ap = ap.rearrange("p (a b) -> p a b", b=free_shape[1])
        elif len(free_shape) == 3:
            ap = ap.rearrange("p (a b c) -> p a b c", b=free_shape[1], c=free_shape[2])
        return ap

    def mark(self):
        return self.off

    def release(self, m):
        self.off = m


class Psum:
    def __init__(self, nc):
        self.t = nc.alloc_psum_tensor("psum_all", [128, 8, 512], F32).ap()

    def f32(self, bank, nbanks=1):
        ap = self.t[:, bank:bank + nbanks, :]
        return ap.rearrange("p a b -> p (a b)")

    def bf16(self, bank, nbanks, inner):
        ap = self.t[:, bank:bank + nbanks, :].rearrange("p a b -> p (a b)").bitcast(BF16)
        return ap.rearrange("p (c t) -> p c t", t=inner)


class Ctx:
    pass


def load_consts(P, A, C, ident_d):
    C.slot_c = P.slot("c_const")
    C.identf = A.alloc([128], F32)
    C.identb = A.alloc([128], BF16)
    C.onesf = A.alloc([128], F32)
    C.onesb = A.alloc([128], BF16)
    t = P.dma("sp", C.slot_c, C.identf, ident_d)
    t1 = P.op("dve", "tensor_copy", dict(out=C.identb, in_=C.identf), [t])
    t2 = P.op("dve", "memset", dict(ap=C.onesf, constant=1.0))
    t3 = P.op("dve", "memset", dict(ap=C.onesb, constant=1.0))
    return [t1, t2, t3]


def modulation(P, A, PS, C, cvec_d, ncv, mod_w, mod_b, res, deps):
    m1 = A.mark()
    cfm = A.alloc([ncv, 16], F32)
    screp = A.alloc([ncv, 16, 128], F32)
    sl = P.pslot("mod_c")
    tl = None
    for v in range(ncv):
        tl = P.dma("sp", sl, cfm[:, v, :], cvec_d[v].rearrange("(c p) -> p c", p=128),
                   allow_slow_non_contiguous=True)
    ts = P.op("act", "activation", dict(out=cfm, in_=cfm, func=AF.Silu), [tl, deps])
    tr = None
    for v in range(ncv):
        for c in range(16):
            tr = P.op("dve", "tensor_scalar", dict(
                out=screp[:, v, c, :], in0=C.onesf, scalar1=cfm[:, v, c:c + 1], scalar2=None, op0=ALU.mult),
                [ts, deps])
    allg = sorted(set(g for (v, g) in res))
    wt = [A.alloc([2048], F32) for _ in range(3)]
    brow = A.alloc([2048], F32)
    wslots = [P.pslot("pw%d" % i) for i in range(3)]
    bslot = P.pslot("mod_b")
    wfree = [deps, deps, deps]
    k = 0
    ev_prev = None
    for g in allg:
        vs = [v for v in range(ncv) if (v, g) in res]
        tb = P.dma("pool", bslot, brow, mod_b[g * 2048:(g + 1) * 2048].partition_broadcast(128), [ev_prev])
        last_mm = None
        for c in range(16):
            s = k % 3
            tw = P.dma("sp", wslots[s], wt[s], mod_w[c * 128:(c + 1) * 128, g * 2048:(g + 1) * 2048], [wfree[s]])
            for vi, v in enumerate(vs):
                for nb in range(4):
                    last_mm = P.op("pe", "matmul", dict(
                        out=PS.f32(4 * vi + nb), lhsT=screp[:, v, c, :], rhs=wt[s][:, nb * 512:(nb + 1) * 512],
                        start=(c == 0), stop=(c == 15)), [tw, tr, ev_prev if c == 0 else None])
            wfree[s] = last_mm
            k += 1
        for vi, v in enumerate(vs):
            ev_prev = P.op("dve", "tensor_tensor", dict(
                out=res[(v, g)], in0=PS.f32(4 * vi, 4), in1=brow, op=ALU.add), [last_mm, tb])
    return ev_prev


def diag_extract(P, A, C, row, out_fm, deps):
    m = A.mark()
    tmp = A.alloc([16, 128], F32)
    t = None
    for j in range(16):
        t = P.op("dve", "tensor_tensor", dict(out=tmp[:, j, :], in0=row[:, j * 128:(j + 1) * 128],
                                                         in1=C.identf, op=ALU.mult), [deps])
    t = P.op("dve", "tensor_reduce", dict(out=out_fm, in_=tmp, axis=AX.X, op=ALU.add), [t])
    return t


def prep_weight(P, A, PS, C, w_dram, col0, ncols, a_fm, brep, wdst, bias_row, deps, psum_bank0=0):
    m = A.mark()
    stage = [A.alloc([512], F32) for _ in range(3)]
    slots = [P.pslot("pw%d" % i) for i in range(3)]
    free = [deps, deps, deps]
    k = 0
    last = None
    nsl = ncols // 512
    assert nsl <= 4 or bias_row is None or True
    for s0 in range(0, nsl, 4):
        grp = list(range(s0, min(nsl, s0 + 4)))
        lastmm = {}
        for c in range(16):
            for sl in grp:
                b = k % 3
                tw = P.dma("sp", slots[b], stage[b],
                           w_dram[c * 128:(c + 1) * 128, col0 + sl * 512:col0 + (sl + 1) * 512], [free[b]])
                rd = []
                if a_fm is not None:
                    t1 = P.op("dve", "tensor_scalar", dict(
                        out=wdst[:, c, sl * 512:(sl + 1) * 512], in0=stage[b], scalar1=a_fm[:, c:c + 1], scalar2=None,
                        op0=ALU.mult), [tw, deps])
                else:
                    t1 = P.op("dve", "tensor_copy", dict(
                        out=wdst[:, c, sl * 512:(sl + 1) * 512], in_=stage[b]), [tw, deps])
                rd.append(t1)
                last = t1
                if bias_row is not None:
                    t2 = P.op("pe", "matmul", dict(
                        out=PS.f32(psum_bank0 + sl - s0), lhsT=brep[:, c, :], rhs=stage[b],
                        start=(c == 0), stop=(c == 15)), [tw, deps])
                    rd.append(t2)
                    lastmm[sl] = t2
                free[b] = rd
                k += 1
        if bias_row is not None:
            for sl in grp:
                last = P.op("act", "activation", dict(
                    out=bias_row[:, sl * 512:(sl + 1) * 512], in_=PS.f32(psum_bank0 + sl - s0), func=AF.Identity),
                    [lastmm[sl]])
            deps = [deps, last]
    return [last, t1]


class FrontEnd:
    def __init__(self, P, A, PS, C, pt_bank, keep_x=False):
        self.P, self.C, self.PS = P, C, PS
        self.xt = [A.alloc([2048], F32) for _ in range(2)]
        self.xn = [A.alloc([2048], BF16) for _ in range(2)]
        self.xnT = [A.alloc([16, 128], BF16) for _ in range(2)]
        self.junk = A.alloc([2048], BF16)
        self.ss = A.alloc([8], F32)
        self.rstd = A.alloc([8], F32)
        self.slots = [P.pslot("fe0"), P.pslot("fe1")]
        self.pt = PS.bf16(pt_bank, 2, 128)
        self.xt_free = [[], []]
        self.xn_free = [None, None]
        self.xnT_free = [[], []]
        self.pt_free = None
        self.n = 0

    def run(self, src_ap, deps=None):
        P, C = self.P, self.C
        i = self.n
        self.n += 1
        b = i % 2
        k = i % 8
        xt, xn, xnT = self.xt[b], self.xn[b], self.xnT[b]
        tx = P.dma("sp", self.slots[b], xt, src_ap, [self.xt_free[b], deps])
        tsq = P.op("act", "activation", dict(out=self.junk, in_=xt, func=AF.Square,
                                                   accum_out=self.ss[:, k:k + 1]), [tx])
        tms = P.op("dve", "tensor_scalar", dict(out=self.rstd[:, k:k + 1], in0=self.ss[:, k:k + 1],
                                                      scalar1=1.0 / D, scalar2=EPS, op0=ALU.mult, op1=ALU.add),
                     [tsq])
        tsr = P.op("act", "activation", dict(out=self.rstd[:, k:k + 1], in_=self.rstd[:, k:k + 1],
                                                   func=AF.Sqrt), [tms])
        trc = P.op("dve", "reciprocal", dict(out=self.rstd[:, k:k + 1], in_=self.rstd[:, k:k + 1]), [tsr])
        txn = P.op("dve", "tensor_scalar", dict(out=xn, in0=xt, scalar1=self.rstd[:, k:k + 1], scalar2=None,
                                                      op0=ALU.mult), [trc, self.xn_free[b]])
        tt = None
        for c in range(16):
            tt = P.op("pe", "transpose", dict(out=self.pt[:, c, :], in_=xn[:, c * 128:(c + 1) * 128],
                                                         identity=C.identb),
                        [txn, self.pt_free] if c == 0 else [])
        self.xn_free[b] = tt
        tcp = P.op("act", "activation", dict(out=xnT, in_=self.pt, func=AF.Copy), [tt, self.xnT_free[b]])
        self.pt_free = tcp
        self.xt_free[b] = [tsq, txn]
        self.xnT_free[b] = []
        self.cur = b
        return xnT, tcp, xt, tx, self.rstd[:, k:k + 1]

    def readers(self, b, toks, x_toks=()):
        self.xnT_free[b] = list(self.xnT_free[b]) + list(_flat(toks))
        self.xt_free[b] = list(self.xt_free[b]) + list(_flat(x_toks))


def head_post(P, A, src, nh, dst_bf, do_norm, gain_row, rope_cs, tmp, deps, scale=None):
    t = deps
    sq, st8 = tmp["sq"], tmp["st8"]
    if do_norm:
        t = P.op("dve", "tensor_tensor", dict(out=sq[:, 0:nh, :], in0=src, in1=src, op=ALU.mult), [t])
        t = P.op("dve", "tensor_reduce", dict(out=st8[:, 0:nh], in_=sq[:, 0:nh, :], axis=AX.X, op=ALU.add), [t])
        t = P.op("dve", "tensor_scalar", dict(out=st8[:, 0:nh], in0=st8[:, 0:nh], scalar1=1.0 / HD, scalar2=EPS,
                                                    op0=ALU.mult, op1=ALU.add), [t])
        t = P.op("act", "activation", dict(out=st8[:, 0:nh], in_=st8[:, 0:nh], func=AF.Sqrt), [t])
        t = P.op("dve", "reciprocal", dict(out=st8[:, 0:nh], in_=st8[:, 0:nh]), [t])
        t = P.op("dve", "tensor_tensor", dict(out=src, in0=src,
                                                    in1=st8[:, 0:nh].unsqueeze(2).broadcast_to([128, nh, 128]),
                                                    op=ALU.mult), [t])
        t = P.op("dve", "tensor_tensor", dict(out=src, in0=src,
                                                    in1=gain_row.unsqueeze(1).broadcast_to([128, nh, 128]),
                                                    op=ALU.mult), [t])
    elif scale is not None:
        t = P.op("dve", "tensor_scalar", dict(out=src, in0=src, scalar1=float(scale), scalar2=None,
                                                    op0=ALU.mult), [t])
    if rope_cs is not None:
        cosb = rope_cs[:, 0:64].unsqueeze(1).broadcast_to([128, nh, 64])
        sinb = rope_cs[:, 64:128].unsqueeze(1).broadcast_to([128, nh, 64])
        x1 = src[:, :, 0:64]
        x2 = src[:, :, 64:128]
        ta, tb_ = sq[:, 0:nh, 0:64], sq[:, 0:nh, 64:128]
        t = P.op("dve", "tensor_tensor", dict(out=ta, in0=x1, in1=cosb, op=ALU.mult), [t])
        t = P.op("dve", "tensor_tensor", dict(out=tb_, in0=x2, in1=sinb, op=ALU.mult), [t])
        t = P.op("dve", "tensor_tensor", dict(out=dst_bf[:, :, 0:64], in0=ta, in1=tb_, op=ALU.subtract), [t])
        t = P.op("dve", "tensor_tensor", dict(out=ta, in0=x2, in1=cosb, op=ALU.mult), [t])
        t = P.op("dve", "tensor_tensor", dict(out=tb_, in0=x1, in1=sinb, op=ALU.mult), [t])
        t = P.op("dve", "tensor_tensor", dict(out=dst_bf[:, :, 64:128], in0=ta, in1=tb_, op=ALU.add), [t])
    else:
        t = P.op("dve", "tensor_copy", dict(out=dst_bf, in_=src), [t])
    return t


def rep16(P, C, fm, rep, deps):
    t = None
    for c in range(16):
        t = P.op("dve", "tensor_scalar", dict(out=rep[:, c, :], in0=C.onesf, scalar1=fm[:, c:c + 1],
                                                         scalar2=None, op0=ALU.mult), [deps])
    return t


def adaln_vectors(P, A, PS, C, cvec_d, ncv, mod_w, mod_b, pre_g, post_g, G_d, deps):
    fms = [(A.alloc([16], F32), A.alloc([16], F32)) for _ in range(ncv)]
    m = A.mark()
    res = {}
    for v in range(ncv):
        res[(v, 0)] = A.alloc([2048], F32)
        res[(v, 1)] = A.alloc([2048], F32)
    res[(0, 2)] = A.alloc([2048], F32)
    prow = A.alloc([2048], F32)
    sl = P.pslot("adaln")
    tp = P.dma("pool", sl, prow, pre_g.partition_broadcast(128), [deps])
    tm = modulation(P, A, PS, C, cvec_d, ncv, mod_w, mod_b, res, deps)
    last = []
    for v in range(ncv):
        t = P.op("dve", "scalar_tensor_tensor", dict(out=res[(v, 1)], in0=res[(v, 1)], scalar=1.0, in1=prow,
                                                                op0=ALU.add, op1=ALU.mult), [tm, tp])
        t1 = diag_extract(P, A, C, res[(v, 1)], fms[v][0], [t])
        t2 = diag_extract(P, A, C, res[(v, 0)], fms[v][1], [tm])
        last += [t1, t2]
    tp2 = P.dma("pool", sl, prow, post_g.partition_broadcast(128), [last])
    tg = P.op("dve", "tensor_tensor", dict(out=res[(0, 2)], in0=res[(0, 2)], in1=prow, op=ALU.mult), [tm, tp2])
    tgd = P.dma("pool", sl, G_d, res[(0, 2)], [tg])
    last.append(tgd)
    A.release(m)
    return fms, last


def emit_l0(nc, P, A, PS, C, din, dscr, X1_d):
    xb = din("xb", [SEQ, D])
    xo = din("xo", [NKB * 128, D])
    ctx_d = din("ctx", [CTX, D])
    cvec = din("cvec", [2, D])
    mod_w = din("mod_w", [D, 3 * D])
    mod_b = din("mod_b", [3 * D])
    pre_g = din("pre_g", [D])
    post_g = din("post_g", [D])
    w_in = din("w_in", [D, 5120])
    q_norm = din("q_norm", [HD])
    k_norm = din("k_norm", [HD])
    sink = din("sink", [8])
    w_out = din("w_out", [D, D])
    ropeb = din("ropeb", [SEQ, 128])
    ropeo = din("ropeo", [NKB * 128, 128])
    masks_d = din("masks", [128, 4, 128])

    G_d = dscr("G_d", [128, D], F32)
    KAT_d = dscr("KAT_d", [2, 128, NKC * 128])
    VA_d = dscr("VA_d", [NKC * 128, 2, 128])
    QT_d = dscr("QT_d", [16, 128, QCOLS])
    XNT_d = dscr("XNT_d", [NQ, 128, D])
    SZT_d = dscr("SZT_d", [16, 128, QCOLS])
    YT_d = dscr("YT_d", [16, 128, QCOLS])
    mall = A.mark()

    cons = C.cons
    fms, t_ad = adaln_vectors(P, A, PS, C, cvec, 2, mod_w, mod_b, pre_g, post_g, G_d, cons)
    (A_fm, B_fm), (Ac_fm, Bc_fm) = fms
    Brep = A.alloc([16, 128], F32)
    t_brep = rep16(P, C, B_fm, Brep, t_ad)
    qg_row = A.alloc([128], F32)
    kg_row = A.alloc([128], F32)
    masks = A.alloc([4, 128], BF16)
    esink = A.alloc([8], F32)
    KBT = A.alloc([2, NKB * 128], BF16)
    VB = A.alloc([NKB, 2, 128], BF16)
    KBTc = A.alloc([2, CTX], BF16)
    VBc = A.alloc([2, 2, 128], BF16)
    sl0 = P.slot("p0misc")
    mk = A.mark()
    mstage = A.alloc([4, 128], F32)
    t1 = P.dma("pool", sl0, qg_row, q_norm.partition_broadcast(128), [t_ad, t_brep])
    t2 = P.dma("pool", sl0, kg_row, k_norm.partition_broadcast(128), [t_ad, t_brep])
    t3 = P.dma("pool", sl0, esink, sink.partition_broadcast(128), [t_ad, t_brep])
    t4 = P.dma("pool", sl0, mstage, masks_d, [t_ad, t_brep])
    tq = P.op("dve", "tensor_scalar", dict(out=qg_row, in0=qg_row, scalar1=float(ATTN_SCALE), scalar2=None,
                                                 op0=ALU.mult), [t1, t2, t3, t4])
    tmk = P.op("dve", "tensor_copy", dict(out=masks, in_=mstage), [tq])
    tes = P.op("act", "activation", dict(out=esink, in_=esink, func=AF.Exp), [t4])
    A.release(mk)
    p0_done = P.full_barrier()

    mkv = A.mark()
    Brepc = A.alloc([16, 128], F32)
    t_brc = rep16(P, C, Bc_fm, Brepc, p0_done)
    WB = A.alloc([16, 512], BF16)
    WA = A.alloc([16, 512], BF16)
    WC = A.alloc([16, 1024], BF16)
    bB = A.alloc([512], F32)
    bA = A.alloc([512], F32)
    bC = A.alloc([1024], F32)
    tw1 = prep_weight(P, A, PS, C, w_in, 2560, 512, A_fm, Brep, WB, bB, [t_brep, t_brc], psum_bank0=4)
    tw2 = prep_weight(P, A, PS, C, w_in, 2048, 512, A_fm, Brep, WA, bA, [tw1], psum_bank0=4)
    tw3 = prep_weight(P, A, PS, C, w_in, 2048, 1024, Ac_fm, Brepc, WC, bC, [tw2], psum_bank0=4)
    wready = [tw1, tw2, tw3]
    FE = FrontEnd(P, A, PS, C, pt_bank=0)
    pkv = PS.f32(2, 2)
    ptk = PS.bf16(4, 1, 128)
    kvf = [A.alloc([1024], F32) for _ in range(2)]
    kbf = [A.alloc([4, 128], BF16) for _ in range(2)]
    ropes = [A.alloc([128], F32) for _ in range(2)]
    rslots = [P.slot(), P.slot()]
    tmp = {"sq": A.alloc([8, 128], F32), "st8": A.alloc([8], F32)}
    kst = [A.alloc([2, 512], BF16) for _ in range(2)]
    vst = [A.alloc([2, 128], BF16) for _ in range(2)]
    kslots = [P.slot(), P.slot()]
    vslots = [P.slot(), P.slot()]
    st = Ctx()
    st.kvf_free = [None, None]
    st.kbf_free = [None, None]
    st.rope_free = [None, None]
    st.kst_free = [None, None]
    st.vst_free = [None, None]
    st.pkv_free = None
    st.ptk_free = None
    st.n = 0
    st.out_toks = []

    def kv_tile(src_ap, rope_ap, W, brow, ncols, mode, idx):
        i = st.n
        st.n += 1
        b = i % 2
        xnT, tready, xt, tx, _ = FE.run(src_ap, wready if i == 0 else None)
        fb = FE.cur
        if rope_ap is not None:
            trope = P.dma("pool", rslots[b], ropes[b], rope_ap, [st.rope_free[b]])
        else:
            trope = None
        nsl = ncols // 512
        mm = None
        for sl in range(nsl):
            for c in range(16):
                mm = P.op("pe", "matmul", dict(
                    out=pkv[:, sl * 512:(sl + 1) * 512], lhsT=xnT[:, c, :], rhs=W[:, c, sl * 512:(sl + 1) * 512],
                    start=(c == 0), stop=(c == 15)), [tready, st.pkv_free] if (c == 0 and sl == 0) else [])
        FE.readers(fb, [mm])
        kf = kvf[b]
        tev = P.op("dve", "tensor_tensor", dict(out=kf[:, 0:ncols], in0=pkv[:, 0:ncols], in1=brow[:, 0:ncols],
                                                      op=ALU.add), [mm, st.kvf_free[b]])
        st.pkv_free = tev
        kb = kbf[b]
        toks = []
        if mode == "bat":
            ksrc = kf[:, 0:256].rearrange("p (h d) -> p h d", d=128)
            tk = head_post(P, A, ksrc, 2, kb[:, 0:2, :], True, kg_row, ropes[b], tmp, [tev, trope, st.kbf_free[b]])
            st.rope_free[b] = tk
            nk = 2
        elif mode == "ext":
            ksrc = kf[:, 0:256].rearrange("p (h d) -> p h d", d=128)
            tk = head_post(P, A, ksrc, 2, kb[:, 0:2, :], False, None, ropes[b], tmp, [tev, trope, st.kbf_free[b]])
            st.rope_free[b] = tk
            nk = 2
        else:
            ksrc = kf[:, 0:256].rearrange("p (h d) -> p h d", d=128)
            tk = head_post(P, A, ksrc, 2, kb[:, 0:2, :], True, kg_row, None, tmp, [tev, st.kbf_free[b]])
            ksrc2 = kf[:, 512:768].rearrange("p (h d) -> p h d", d=128)
            tk = head_post(P, A, ksrc2, 2, kb[:, 2:4, :], False, None, None, tmp, [tk])
            nk = 4
        tt = None
        for h in range(nk):
            tt = P.op("pe", "transpose", dict(out=ptk[:, h, :], in_=kb[:, h, :], identity=C.identb),
                        [tk, st.ptk_free] if h == 0 else [])
        st.kbf_free[b] = tt
        if mode == "ext":
            tc = P.op("act", "activation", dict(out=KBT[:, :, idx * 128:(idx + 1) * 128], in_=ptk[:, 0:2, :],
                                                      func=AF.Copy), [tt])
            st.ptk_free = tc
            tv = P.op("dve", "tensor_copy", dict(out=VB[:, idx, :, :],
                                                       in_=kf[:, 256:512].rearrange("p (h d) -> p h d", d=128)), [tk])
            st.kvf_free[b] = tv
            toks += [tc, tv]
        else:
            grp = 4 if mode == "bat" else 2
            g, r = divmod(idx, grp)
            sb = g % 2
            tc = P.op("act", "activation", dict(out=kst[sb][:, :, r * 128:(r + 1) * 128], in_=ptk[:, 0:2, :],
                                                      func=AF.Copy), [tt, st.kst_free[sb] if r == 0 else None])
            if mode == "ctx":
                tc2 = P.op("act", "activation", dict(out=KBTc[:, :, idx * 128:(idx + 1) * 128],
                                                           in_=ptk[:, 2:4, :], func=AF.Copy), [tt])
                tc = tc2
            st.ptk_free = tc
            voff = 256
            vb_ = vst[b]
            tv = P.op("dve", "tensor_copy", dict(
                out=vb_, in_=kf[:, voff:voff + 256].rearrange("p (h d) -> p h d", d=128)), [tk, st.vst_free[b]])
            if mode == "ctx":
                tv2 = P.op("dve", "tensor_copy", dict(
                    out=VBc[:, idx, :, :], in_=kf[:, 768:1024].rearrange("p (h d) -> p h d", d=128)), [tk])
                st.kvf_free[b] = tv2
                toks.append(tv2)
            else:
                st.kvf_free[b] = tv
            base = (0 if mode == "bat" else SEQ)
            tok0 = base + idx * 128
            tvd = P.dma("pool", vslots[b], VA_d[tok0:tok0 + 128, :, :], vb_, [tv])
            st.vst_free[b] = tvd
            toks.append(tvd)
            if r == grp - 1:
                c0 = base + g * grp * 128
                tkd = P.dma("pool", kslots[sb], KAT_d[:, :, c0:c0 + grp * 128].rearrange("h d t -> d h t"),
                            kst[sb][:, :, 0:grp * 128], [tc])
                st.kst_free[sb] = tkd
                toks.append(tkd)
        st.out_toks = [st.out_toks[-8:], toks]
        st.all_toks.extend(toks)

    st.all_toks = []
    for e_ in range(NKB):
        kv_tile(xo[e_ * 128:(e_ + 1) * 128, :], ropeo[e_ * 128:(e_ + 1) * 128, :], WB, bB, 512, "ext", e_)
    for t_ in range(CTX // 128):
        kv_tile(ctx_d[t_ * 128:(t_ + 1) * 128, :], None, WC, bC, 1024, "ctx", t_)
    for t_ in range(NTB):
        kv_tile(xb[t_ * 128:(t_ + 1) * 128, :], ropeb[t_ * 128:(t_ + 1) * 128, :], WA, bA, 512, "bat", t_)
    pkv_done = P.full_barrier()
    A.release(mkv)

    m2 = A.mark()
    WQ = A.alloc([16, 2048], BF16)
    bQ = A.alloc([2048], F32)
    twq = prep_weight(P, A, PS, C, w_in, 0, 2048, A_fm, Brep, WQ, bQ, pkv_done, psum_bank0=2)
    FE = FrontEnd(P, A, PS, C, pt_bank=0)
    pq = [PS.f32(2, 2), PS.f32(4, 2)]
    ptq = PS.bf16(6, 1, 128)
    qf = [A.alloc([8, 128], F32) for _ in range(2)]
    qbf = [A.alloc([16, 128], BF16) for _ in range(2)]
    qT = [A.alloc([16, 128], BF16) for _ in range(2)]
    ropes = [A.alloc([128], F32) for _ in range(2)]
    rslots = [P.pslot("rope0"), P.pslot("rope1")]
    qslots = [P.pslot("q0"), P.pslot("q1")]
    xslots = [P.pslot("xn0"), P.pslot("xn1")]
    tmp = {"sq": A.alloc([8, 128], F32), "st8": A.alloc([8], F32)}
    pq_free = [None, None]
    qf_free = [None, None]
    qbf_free = [None, None]
    qT_free = [None, None]
    rope_free = [None, None]
    ptq_free = None
    for o in range(NQ):
        b = o % 2
        e_ = o + 1
        so = slotof(o)
        xnT, tready, xt, tx, _ = FE.run(xo[e_ * 128:(e_ + 1) * 128, :], twq if o == 0 else None)
        fb = FE.cur
        trope = P.dma("pool", rslots[b], ropes[b], ropeo[e_ * 128:(e_ + 1) * 128, :], [rope_free[b]])
        txd = P.dma("pool", xslots[b], XNT_d[so].rearrange("p (c t) -> p c t", t=128), xnT, [tready])
        mms = []
        for half in range(2):
            mm = None
            for sl2 in range(2):
                sl = half * 2 + sl2
                for c in range(16):
                    mm = P.op("pe", "matmul", dict(
                        out=pq[half][:, sl2 * 512:(sl2 + 1) * 512], lhsT=xnT[:, c, :],
                        rhs=WQ[:, c, sl * 512:(sl + 1) * 512], start=(c == 0), stop=(c == 15)),
                        [tready, pq_free[half]] if (c == 0 and sl2 == 0) else [])
            mms.append(mm)
        FE.readers(fb, [mms[1], txd])
        tpost = []
        for half in range(2):
            q3 = qf[half]
            tev = P.op("dve", "tensor_tensor", dict(
                out=q3.rearrange("p h d -> p (h d)"), in0=pq[half], in1=bQ[:, half * 1024:(half + 1) * 1024],
                op=ALU.add), [mms[half], qf_free[half]])
            pq_free[half] = tev
            if half == 0:
                tk = head_post(P, A, q3, 8, qbf[b][:, 0:8, :], True, qg_row, ropes[b], tmp,
                               [tev, trope, qbf_free[b]])
            else:
                tk = head_post(P, A, q3, 8, qbf[b][:, 8:16, :], False, None, ropes[b], tmp, [tev, trope],
                               scale=ATTN_SCALE)
            qf_free[half] = tk
            tpost.append(tk)
        rope_free[b] = tpost[1]
        tc = None
        for half in range(2):
            tt = None
            for h in range(8):
                tt = P.op("pe", "transpose", dict(
                    out=ptq[:, h, :], in_=qbf[b][:, half * 8 + h, :], identity=C.identb),
                    [tpost[half], ptq_free] if h == 0 else [])
            tc = P.op("act", "activation", dict(
                out=qT[b][:, half * 8:(half + 1) * 8, :], in_=ptq, func=AF.Copy),
                [tt, qT_free[b] if half == 0 else None])
            ptq_free = tc
        qbf_free[b] = tt
        tqd = P.dma("pool", qslots[b], QT_d[:, :, so * 128:(so + 1) * 128].rearrange("h d t -> d h t"), qT[b], [tc])
        qT_free[b] = tqd
    p2a_done = P.full_barrier()
    A.release(m2)

    p2b_done = gate_phase(P, A, PS, C, w_in, 3072, A_fm, Brep, XNT_d, SZT_d, None, p2a_done, NQ)

    m4 = A.mark()
    KATh = A.alloc([NKC * 128], BF16)
    VAh = A.alloc([NKC, 128], BF16)
    qblk = [A.alloc([4, 512], BF16) for _ in range(2)]
    zblk = [A.alloc([4, 512], BF16) for _ in range(2)]
    pT = [A.alloc([1024], BF16) for _ in range(3)]
    pr = [A.alloc([512], BF16) for _ in range(2)]
    acc = [A.alloc([512], F32) for _ in range(2)]
    accs_free = [None, None]
    ahl = [A.alloc([2, 512], BF16) for _ in range(2)]
    ahl_free = [None, None]
    pr_free = [None, None]
    rs = [A.alloc([512], F32) for _ in range(2)]
    yf = [A.alloc([512], F32) for _ in range(2)]
    yb = [A.alloc([512], BF16) for _ in range(2)]
    kvs = [P.pslot("kvh0"), P.pslot("kvh1")]
    qs = [P.pslot("qb0"), P.pslot("qb1")]
    zs = [P.pslot("zb0"), P.pslot("zb1")]
    ys = [P.pslot("y0"), P.pslot("y1")]
    Sb = [PS.f32(0, 2), PS.f32(2, 2)]
    Ob = [PS.f32(4), PS.f32(5)]
    Ub = [PS.f32(6), PS.f32(7)]
    S_free = [None, None]
    pT_free = [None, None, None]
    acc_free = [None, None]
    rs_free = [None, None]
    yb_free = [None, None]
    qblk_free = [None, None]
    zblk_free = [None, None]
    kv_free = p2b_done
    NG = NKC // 2
    it = 0
    blkn = 0
    for kvh in range(2):
        tk1 = P.dma("sp", kvs[0], KATh, KAT_d[kvh], [kv_free])
        tk2 = P.dma("sp", kvs[1], VAh, VA_d[:, kvh, :].rearrange("(t p) d -> p t d", p=128), [kv_free])
        kvready = [tk1, tk2]
        lastuse = None
        for (c0, bw) in blocks_of(NQ):
            bb = blkn % 2
            blkn += 1
            tq = P.dma("sp", qs[bb], qblk[bb][:, :, 0:bw], QT_d[kvh * 4:(kvh + 1) * 4, :, c0:c0 + bw]
                       .rearrange("h d t -> d h t"), [qblk_free[bb]])
            tz = P.dma("sp", zs[bb], zblk[bb][:, :, 0:bw], SZT_d[kvh * 4:(kvh + 1) * 4, :, c0:c0 + bw]
                       .rearrange("h d t -> d h t"), [zblk_free[bb]])
            for hd in range(4):
                ab = it % 2
                it += 1
                qrhs = qblk[bb][:, hd, 0:bw]

                def QK(g, ab=ab, qrhs=qrhs):
                    sb = g % 2
                    t = None
                    for j in range(2):
                        kc = 2 * g + j
                        t = P.op("pe", "matmul", dict(
                            out=Sb[sb][:, j * bw:(j + 1) * bw], lhsT=KATh[:, kc * 128:(kc + 1) * 128], rhs=qrhs,
                            start=True, stop=True), [S_free[sb], tq, kvready] if j == 0 else [])
                    return t

                tqk = {0: QK(0), 1: QK(1)}
                tpv = None
                for g in range(NG):
                    sb = g % 2
                    pb = g % 3
                    tex = P.op("act", "activation", dict(out=pT[pb][:, 0:2 * bw], in_=Sb[sb][:, 0:2 * bw], func=AF.Exp),
                                 [tqk[g], pT_free[pb]])
                    S_free[sb] = tex
                    for j in range(2):
                        kc = 2 * g + j
                        P.op("pe", "matmul", dict(
                            out=Ob[ab][:, 0:bw], lhsT=VAh[:, kc, :], rhs=pT[pb][:, j * bw:(j + 1) * bw],
                            start=(g == 0 and j == 0), stop=(g == NG - 1 and j == 1)),
                            [tex, acc_free[ab]] if j == 0 else [])
                    tpair = P.op("dve", "tensor_tensor", dict(out=pr[g % 2][:, 0:bw], in0=pT[pb][:, 0:bw],
                                                              in1=pT[pb][:, bw:2 * bw], op=ALU.add),
                                  [tex, pr_free[g % 2]])
                    if g == 0:
                        tacc = P.op("dve", "tensor_copy", dict(out=acc[ab][:, 0:bw], in_=pr[g % 2][:, 0:bw]),
                                    [tpair, accs_free[ab]])
                    else:
                        tacc = P.op("dve", "tensor_tensor", dict(out=acc[ab][:, 0:bw], in0=acc[ab][:, 0:bw],
                                                                 in1=pr[g % 2][:, 0:bw], op=ALU.add), [tpair])
                    pr_free[g % 2] = tacc
                    tpv = P.last("pe")
                    pT_free[pb] = [tpv, tpair]
                    if g + 2 < NG:
                        tqk[g + 2] = QK(g + 2)
                thi = P.op("dve", "tensor_copy", dict(out=ahl[ab][:, 0, 0:bw], in_=acc[ab][:, 0:bw]), [tacc, ahl_free[ab]])
                tlo = P.op("dve", "tensor_tensor", dict(out=ahl[ab][:, 1, 0:bw], in0=acc[ab][:, 0:bw],
                                                        in1=ahl[ab][:, 0, 0:bw], op=ALU.subtract), [thi])
                accs_free[ab] = tlo
                P.op("pe", "matmul", dict(out=Ub[ab][:, 0:bw], lhsT=C.onesb, rhs=ahl[ab][:, 0, 0:bw],
                                          start=True, stop=False), [tlo])
                tpv = P.op("pe", "matmul", dict(out=Ub[ab][:, 0:bw], lhsT=C.onesb, rhs=ahl[ab][:, 1, 0:bw],
                                                start=False, stop=True), [])
                ahl_free[ab] = tpv
                t = P.op("dve", "reciprocal", dict(out=rs[ab][:, 0:bw], in_=Ub[ab][:, 0:bw]), [tpv, rs_free[ab]])
                t = P.op("dve", "tensor_tensor", dict(out=yf[ab][:, 0:bw], in0=Ob[ab][:, 0:bw], in1=rs[ab][:, 0:bw],
                                                      op=ALU.mult), [t])
                acc_free[ab] = t
                t = P.op("dve", "tensor_tensor", dict(
                    out=yb[ab][:, 0:bw], in0=yf[ab][:, 0:bw], in1=zblk[bb][:, hd, 0:bw], op=ALU.mult),
                    [t, tz, yb_free[ab]])
                rs_free[ab] = t
                td = P.dma("pool", ys[ab], YT_d[kvh * 4 + hd, :, c0:c0 + bw], yb[ab][:, 0:bw], [t])
                yb_free[ab] = td
                lastuse = [tpv, t]
            qblk_free[bb] = lastuse
            zblk_free[bb] = lastuse
        kv_free = lastuse
    p3a_done = P.full_barrier()
    A.release(m4)

    m5 = A.mark()
    qw = [A.alloc([8, 128], BF16) for _ in range(2)]
    zw = [A.alloc([8, 128], BF16) for _ in range(2)]
    pT5 = [A.alloc([5, 512], BF16) for _ in range(2)]
    su = [A.alloc([512], F32) for _ in range(2)]
    yf = [A.alloc([512], F32) for _ in range(2)]
    yb = [A.alloc([4, 128], BF16) for _ in range(2)]
    qs = [P.pslot("qb0"), P.pslot("qb1")]
    zs = [P.pslot("zb0"), P.pslot("zb1")]
    ys = [P.pslot("y0"), P.pslot("y1")]
    S5 = PS.f32(0, 5)
    Ob = PS.f32(5)
    Ub = PS.f32(6)
    S_free = None
    acc_free = None
    pT_free = [None, None]
    su_free = [None, None]
    yb_free = [None, None]
    qw_free = [None, None]
    it = 0
    for o in range(NQ):
        bb = o % 2
        so = slotof(o)
        tq = P.dma("sp", qs[bb], qw[bb], QT_d[8:16, :, so * 128:(so + 1) * 128].rearrange("h d t -> d h t"),
                   [qw_free[bb], p3a_done if o < 2 else None])
        tz = P.dma("sp", zs[bb], zw[bb], SZT_d[8:16, :, so * 128:(so + 1) * 128].rearrange("h d t -> d h t"),
                   [qw_free[bb], p3a_done if o < 2 else None])
        lastuse = None
        for kvh in range(2):
            ab = it % 2
            it += 1
            qrhs = qw[bb][:, kvh * 4:(kvh + 1) * 4, :]
            mm = None
            for j in range(5):
                if j < 3:
                    lhs = KBT[:, kvh, (o + j) * 128:(o + j + 1) * 128]
                else:
                    lhs = KBTc[:, kvh, (j - 3) * 128:(j - 2) * 128]
                mm = P.op("pe", "matmul", dict(
                    out=S5[:, j * 512:(j + 1) * 512], lhsT=lhs, rhs=qrhs, start=True, stop=True),
                    [S_free, tq] if j == 0 else [])
            p5 = pT5[ab]
            tex = P.op("act", "activation", dict(out=p5.rearrange("p a b -> p (a b)"), in_=S5,
                                                              func=AF.Exp), [mm, pT_free[ab]])
            S_free = tex
            mlo = masks[:, 2 if o <= 1 else 0, :].unsqueeze(1).broadcast_to([128, 4, 128])
            mhi = masks[:, 3 if o >= NQ - 2 else 1, :].unsqueeze(1).broadcast_to([128, 4, 128])
            v0 = p5[:, 0, :].rearrange("p (h t) -> p h t", t=128)
            v2 = p5[:, 2, :].rearrange("p (h t) -> p h t", t=128)
            tm = P.op("dve", "tensor_tensor", dict(out=v0, in0=v0, in1=mlo, op=ALU.mult), [tex])
            tm = P.op("dve", "tensor_tensor", dict(out=v2, in0=v2, in1=mhi, op=ALU.mult), [tm])
            tpv = None
            for j in range(5):
                if j < 3:
                    lhs = VB[:, o + j, kvh, :]
                else:
                    lhs = VBc[:, j - 3, kvh, :]
                P.op("pe", "matmul", dict(
                    out=Ob, lhsT=lhs, rhs=p5[:, j, :], start=(j == 0), stop=(j == 4)),
                    [tm, acc_free] if j == 0 else [])
            for j in range(5):
                tpv = P.op("pe", "matmul", dict(
                    out=Ub, lhsT=C.onesb, rhs=p5[:, j, :], start=(j == 0), stop=(j == 4)), [])
            pT_free[ab] = tpv
            es = esink[:, kvh * 4:(kvh + 1) * 4].unsqueeze(2).broadcast_to([128, 4, 128])
            t = P.op("dve", "tensor_tensor", dict(
                out=su[ab].rearrange("p (h t) -> p h t", t=128), in0=Ub.rearrange("p (h t) -> p h t", t=128),
                in1=es, op=ALU.add), [tpv, su_free[ab]])
            t = P.op("dve", "reciprocal", dict(out=su[ab], in_=su[ab]), [t])
            t = P.op("dve", "tensor_tensor", dict(out=yf[ab], in0=Ob, in1=su[ab], op=ALU.mult), [t])
            acc_free = t
            t = P.op("dve", "tensor_tensor", dict(
                out=yb[ab].rearrange("p h t -> p (h t)"), in0=yf[ab],
                in1=zw[bb][:, kvh * 4:(kvh + 1) * 4, :].rearrange("p h t -> p (h t)"), op=ALU.mult),
                [t, tz, yb_free[ab]])
            su_free[ab] = t
            td = P.dma("pool", ys[ab], YT_d[8 + kvh * 4:8 + (kvh + 1) * 4, :, so * 128:(so + 1) * 128]
                       .rearrange("h d t -> d h t"), yb[ab], [t])
            yb_free[ab] = td
            lastuse = [tpv, t]
        qw_free[bb] = lastuse
    p3b_done = P.full_barrier()
    A.release(m5)

    qs_of_slot = [q for s_ in range(NQ) for q in range(NQ) if slotof(q) == s_]
    out_proj_phase(P, A, PS, C, w_out, YT_d, G_d,
                   lambda s_: xo[(qs_of_slot[s_] + 1) * 128:(qs_of_slot[s_] + 2) * 128, :],
                   lambda s_: X1_d[qs_of_slot[s_] * 128:(qs_of_slot[s_] + 1) * 128, :], p3b_done, NQ)
    done = P.full_barrier()
    A.release(mall)
    return done


def gate_phase(P, A, PS, C, w_dram, col0, A_fm, Brep, XNT_d, OUT_d, MUL_d, deps, ntiles):
    m3 = A.mark()
    WZ = A.alloc([16, 2048], BF16)
    bZ = A.alloc([2048], F32)
    bz_fm = A.alloc([16], F32)
    twz = prep_weight(P, A, PS, C, w_dram, col0, 2048, A_fm, Brep, WZ, bZ, deps, psum_bank0=4)
    tbz = diag_extract(P, A, C, bZ, bz_fm, twz)
    blk = [A.alloc([16, 512], BF16) for _ in range(2)]
    bslots = [P.pslot("blk0"), P.pslot("blk1")]
    szb = [A.alloc([512], BF16) for _ in range(4)]
    sslots = [P.pslot("sz%d" % i) for i in range(4)]
    blk_free = [None, None]
    szb_free = [None] * 4
    pz = [PS.f32(0), PS.f32(1), PS.f32(2), PS.f32(3)]
    pz_free = [None] * 4
    if MUL_d is not None:
        mblk = [A.alloc([16, 512], BF16) for _ in range(2)]
        mslots = [P.pslot("mb0"), P.pslot("mb1")]
    k = 0
    for tb, (c0, bw) in enumerate(blocks_of(ntiles)):
        b = tb % 2
        tl = None
        for j in range(bw // 128):
            tl = P.dma("sp", bslots[b], blk[b][:, :, j * 128:(j + 1) * 128],
                       XNT_d[tb * 4 + j].rearrange("p (c t) -> p c t", t=128),
                       [blk_free[b], [twz, tbz] if tb < 2 else None])
        tmul = None
        if MUL_d is not None:
            tmul = P.dma("sp", mslots[b], mblk[b][:, :, 0:bw], MUL_d[:, :, c0:c0 + bw].rearrange("c d t -> d c t"),
                         [blk_free[b], [twz, tbz] if tb < 2 else None])
        lastmm = None
        lastrd = None
        for f in range(16):
            pb = k % 4
            mm = None
            for c in range(16):
                mm = P.op("pe", "matmul", dict(
                    out=pz[pb][:, 0:bw], lhsT=WZ[:, c, f * 128:(f + 1) * 128], rhs=blk[b][:, c, 0:bw],
                    start=(c == 0), stop=(c == 15)), [tl, pz_free[pb], twz] if c == 0 else [])
            ta = P.op("act", "activation", dict(
                out=szb[pb][:, 0:bw], in_=pz[pb][:, 0:bw], func=AF.Silu, bias=bz_fm[:, f:f + 1]),
                [mm, szb_free[pb], tbz])
            pz_free[pb] = ta
            if MUL_d is not None:
                ta = P.op("dve", "tensor_tensor", dict(out=szb[pb][:, 0:bw], in0=szb[pb][:, 0:bw],
                                                       in1=mblk[b][:, f, 0:bw], op=ALU.mult), [ta, tmul])
                lastrd = ta
            td = P.dma("pool", sslots[pb], OUT_d[f, :, c0:c0 + bw], szb[pb][:, 0:bw], [ta])
            szb_free[pb] = td
            lastmm = mm
            k += 1
        blk_free[b] = [lastmm, lastrd]
    done = P.full_barrier()
    A.release(m3)
    return done


def out_proj_phase(P, A, PS, C, w_out, YT_d, G_d, xsrc, xdst, deps, ntiles):
    m = A.mark()
    WO = A.alloc([16, 2048], BF16)
    G = A.alloc([2048], F32)
    gs = P.pslot("adaln")
    tg = P.dma("pool", gs, G, G_d, [deps])
    two = prep_weight(P, A, PS, C, w_out, 0, 2048, None, None, WO, None, deps)
    yblk = [A.alloc([16, 512], BF16) for _ in range(2)]
    xt = [A.alloc([2048], F32) for _ in range(2)]
    xo_ = [A.alloc([2048], F32) for _ in range(2)]
    tmpf = A.alloc([2048], F32)
    junk = A.alloc([2048], BF16)
    ss = A.alloc([8], F32)
    bsl = [P.pslot("blk0"), P.pslot("blk1")]
    xsl = [P.pslot("fe0"), P.pslot("fe1")]
    osl = [P.pslot("y0"), P.pslot("y1")]
    po = [PS.f32(0, 4), PS.f32(4, 4)]
    po_free = [None, None]
    yblk_free = [None, None]
    xt_free = [None, None]
    xo_free = [None, None]
    outs = []
    tl = None
    blks = blocks_of(ntiles)
    for o in range(ntiles):
        b = o % 2
        tb, j = divmod(o, 4)
        bb = tb % 2
        c0, bw = blks[tb]
        if j == 0:
            tl = P.dma("sp", bsl[bb], yblk[bb][:, :, 0:bw], YT_d[:, :, c0:c0 + bw].rearrange("c d t -> d c t"),
                       [yblk_free[bb], deps])
        tx = P.dma("sp", xsl[b], xt[b], xsrc(o), [xt_free[b], deps])
        mm = None
        for sl in range(4):
            for c in range(16):
                mm = P.op("pe", "matmul", dict(
                    out=po[b][:, sl * 512:(sl + 1) * 512], lhsT=yblk[bb][:, c, j * 128:(j + 1) * 128],
                    rhs=WO[:, c, sl * 512:(sl + 1) * 512], start=(c == 0), stop=(c == 15)),
                    [tl, two, po_free[b]] if (c == 0 and sl == 0) else [])
        if j == bw // 128 - 1:
            yblk_free[bb] = mm
        k = o % 8
        tsq = P.op("act", "activation", dict(out=junk, in_=po[b], func=AF.Square,
                                                             accum_out=ss[:, k:k + 1]), [mm])
        t = P.op("dve", "tensor_scalar", dict(out=ss[:, k:k + 1], in0=ss[:, k:k + 1], scalar1=1.0 / D,
                                                         scalar2=EPS, op0=ALU.mult, op1=ALU.add), [tsq])
        t = P.op("act", "activation", dict(out=ss[:, k:k + 1], in_=ss[:, k:k + 1], func=AF.Sqrt), [t])
        t = P.op("dve", "reciprocal", dict(out=ss[:, k:k + 1], in_=ss[:, k:k + 1]), [t])
        t = P.op("dve", "scalar_tensor_tensor", dict(
            out=tmpf, in0=po[b], scalar=ss[:, k:k + 1], in1=G, op0=ALU.mult, op1=ALU.mult), [t, tg])
        po_free[b] = t
        t = P.op("dve", "tensor_tensor", dict(out=xo_[b], in0=tmpf, in1=xt[b], op=ALU.add),
                   [t, tx, xo_free[b]])
        xt_free[b] = t
        td = P.dma("pool", osl[b], xdst(o), xo_[b], [t])
        xo_free[b] = td
        outs.append(td)
    A.release(m)
    return outs[-2:]


POOL_SIZES = (2, 4, 8, 16)


def pool_phase(P, A, PS, C, xe, w_in, pool_w, pool_scale, pm_d, ic_d, A_fm, Brep, XNT_d, MT_d, deps):
    m = A.mark()
    WU = A.alloc([16, 2048], BF16)
    bU = A.alloc([2048], F32)
    twu = prep_weight(P, A, PS, C, w_in, 0, 2048, A_fm, Brep, WU, bU, deps, psum_bank0=2)
    PW = A.alloc([4, 4, 512], BF16)
    pwst = A.alloc([4, 512], F32)
    psc = A.alloc([16], F32)
    PM = A.alloc([36, 128], BF16)
    IC = A.alloc([12, 128], F32)
    sl = P.pslot("adaln")
    tl = None
    for g in range(4):
        tl = P.dma("pool", sl, pwst, pool_w[g].rearrange("(ci p) d -> p ci d", p=128), [twu, tl])
        tl = P.op("dve", "tensor_copy", dict(out=PW[:, g, :, :], in_=pwst), [tl])
    t1 = P.dma("pool", sl, psc, pool_scale.rearrange("(c p) -> p c", p=128), [tl], allow_slow_non_contiguous=True)
    t2 = P.dma("pool", sl, PM, pm_d, [t1])
    t3 = P.dma("pool", sl, IC, ic_d.partition_broadcast(128), [t2])
    t4 = t3
    ready = [twu, t4]
    FE = FrontEnd(P, A, PS, C, pt_bank=0)
    pu = PS.f32(2, 4)
    pp = PS.f32(6).rearrange("p (a b) -> p a b", b=128)
    pmx = PS.f32(7).rearrange("p (a b) -> p a b", b=128)
    ub = [A.alloc([2048], BF16) for _ in range(4)]
    pl = [A.alloc([4, 128], BF16) for _ in range(2)]
    mt = [A.alloc([16, 128], BF16) for _ in range(2)]
    xslots = [P.pslot("xn0"), P.pslot("xn1")]
    mslots = [P.pslot("q0"), P.pslot("q1")]
    ub_free = [None] * 4
    ub_ready = [None] * 4
    pu_free = None
    pp_free = None
    pmx_free = None
    pl_free = [None, None]
    mt_free = [None, None]
    gi = 0
    for e_ in range(NTE):
        xnT, tready, xt, tx, _ = FE.run(xe[e_ * 128:(e_ + 1) * 128, :], ready if e_ == 0 else None)
        fb = FE.cur
        rd = []
        if 1 <= e_ <= NTO:
            txd = P.dma("pool", xslots[e_ % 2], XNT_d[e_ - 1].rearrange("p (c t) -> p c t", t=128), xnT, [tready])
            rd.append(txd)
        mm = None
        for s4 in range(4):
            for c in range(16):
                mm = P.op("pe", "matmul", dict(
                    out=pu[:, s4 * 512:(s4 + 1) * 512], lhsT=xnT[:, c, :], rhs=WU[:, c, s4 * 512:(s4 + 1) * 512],
                    start=(c == 0), stop=(c == 15)), [tready, pu_free] if (c == 0 and s4 == 0) else [])
        rd.append(mm)
        FE.readers(fb, rd)
        ui = e_ % 4
        tev = P.op("dve", "tensor_tensor", dict(out=ub[ui], in0=pu, in1=bU, op=ALU.add), [mm, ub_free[ui]])
        pu_free = tev
        ub_ready[ui] = tev
        if e_ >= 2:
            o = e_ - 2
            kind = 0 if o == 0 else (2 if o == NTO - 1 else 1)
            mb = o % 2
            lastp = None
            for g in range(4):
                pb = gi % 2
                gi += 1
                for fc in range(4):
                    f = g * 4 + fc
                    for r in range(3):
                        lastp = P.op("pe", "matmul", dict(
                            out=pp[:, fc, :], lhsT=ub[(o + r) % 4][:, f * 128:(f + 1) * 128],
                            rhs=PM[:, (kind * 4 + g) * 3 + r, :], start=(r == 0), stop=(r == 2)),
                            [ub_ready[(o + r) % 4], pp_free] if fc == 0 else [])
                icb = IC[:, kind * 4 + g, :].unsqueeze(1).broadcast_to([128, 4, 128])
                tpl = P.op("dve", "tensor_tensor", dict(out=pl[pb], in0=pp, in1=icb, op=ALU.mult),
                           [lastp, pl_free[pb]])
                pp_free = tpl
                lm = None
                for fo in range(4):
                    for ci in range(4):
                        lm = P.op("pe", "matmul", dict(
                            out=pmx[:, fo, :], lhsT=PW[:, g, ci, fo * 128:(fo + 1) * 128], rhs=pl[pb][:, ci, :],
                            start=(ci == 0), stop=(ci == 3)), [tpl, pmx_free] if (fo == 0 and ci == 0) else [])
                pl_free[pb] = lm
                pscb = psc[:, g * 4:(g + 1) * 4].unsqueeze(2).broadcast_to([128, 4, 128])
                tmx = P.op("dve", "tensor_tensor", dict(out=mt[mb][:, g * 4:(g + 1) * 4, :], in0=pmx, in1=pscb,
                                                        op=ALU.mult), [lm, mt_free[mb] if g == 0 else None])
                pmx_free = tmx
            ub_free[o % 4] = lastp
            tmd = P.dma("pool", mslots[mb], MT_d[:, :, o * 128:(o + 1) * 128].rearrange("c d t -> d c t"), mt[mb],
                        [tmx])
            mt_free[mb] = tmd
    done = P.full_barrier()
    A.release(m)
    return done


def emit_l1(nc, P, A, PS, C, din, dscr, X1_d, deps):
    cvec = din("cvec1", [1, D])
    mod_w = din("mod_w1", [D, 3 * D])
    mod_b = din("mod_b1", [3 * D])
    pre_g = din("pre_g1", [D])
    post_g = din("post_g1", [D])
    w_in = din("w_in1", [D, 4096])
    pool_w = din("pool_w", [4, 512, 512])
    pool_scale = din("pool_scale", [D])
    w_out = din("w_out1", [D, D])
    pm_d = din("pm", [128, 36, 128], BF16)
    ic_d = din("ic", [12 * 128])
    out = nc.dram_tensor("out", [OWN, D], F32, kind="ExternalOutput").ap()
    G_d = dscr("G1_d", [128, D], F32)
    XNT_d = dscr("XNT1_d", [NTO, 128, D])
    MT_d = dscr("MT_d", [16, 128, OWN])
    YT_d = dscr("YT1_d", [16, 128, OWN])
    fms, t_ad = adaln_vectors(P, A, PS, C, cvec, 1, mod_w, mod_b, pre_g, post_g, G_d, deps)
    (A_fm, B_fm), = fms
    Brep = A.alloc([16, 128], F32)
    rep16(P, C, B_fm, Brep, t_ad)
    p0_done = P.full_barrier()
    pa_done = pool_phase(P, A, PS, C, X1_d, w_in, pool_w, pool_scale, pm_d, ic_d, A_fm, Brep, XNT_d, MT_d, p0_done)
    pb_done = gate_phase(P, A, PS, C, w_in, 2048, A_fm, Brep, XNT_d, YT_d, MT_d, pa_done, NTO)
    out_proj_phase(P, A, PS, C, w_out, YT_d, G_d, lambda o: X1_d[(o + 1) * 128:(o + 2) * 128, :],
                   lambda o: out[o * 128:(o + 1) * 128, :], pb_done, NTO)
    return P.full_barrier()


def build_fused(debug=False):
    nc = bass.Bass("TRN2", target_bir_lowering=False)

    def din(name, shape, dt=F32):
        return nc.dram_tensor(name, list(shape), dt, kind="ExternalInput").ap()

    def dscr(name, shape, dt=BF16):
        return nc.dram_tensor(name, list(shape), dt, kind=("ExternalOutput" if debug else "Internal")).ap()

    P = Prog(nc)
    A = Arena(nc, 206 * 1024)
    PS = Psum(nc)
    C = Ctx()
    X1_d = dscr("X1_d", [NQ * 128, D], F32)
    ident_d = din("ident", [128, 128])
    C.cons = load_consts(P, A, C, ident_d)
    d0 = emit_l0(nc, P, A, PS, C, din, dscr, X1_d)
    emit_l1(nc, P, A, PS, C, din, dscr, X1_d, d0)
    P.build()
    return nc


def _rope_table(pos):
    pos = np.asarray(pos)
    row = (pos // GRID_W).astype(np.float32)
    col = (pos % GRID_W).astype(np.float32)
    inv = (np.float32(10000.0) ** (-np.arange(32, dtype=np.float32) / np.float32(32))).astype(np.float32)
    ang = np.concatenate([row[:, None] * inv, col[:, None] * inv], axis=-1).astype(np.float32)
    return np.concatenate([np.cos(ang), np.sin(ang)], axis=-1).astype(np.float32)


def _ext_rows(xb_, j, halo=128):
    out = np.zeros((OWN + 2 * halo,) + xb_.shape[1:], dtype=xb_.dtype)
    lo = j * OWN - halo
    hi = (j + 1) * OWN + halo
    slo, shi = max(lo, 0), min(hi, xb_.shape[0])
    out[slo - lo:shi - lo] = xb_[slo:shi]
    return out


def _masks(j):
    kl = np.arange(128)[:, None]
    ql = np.arange(128)[None, :]
    lo = (kl >= ql).astype(np.float32)
    hi = (kl <= ql).astype(np.float32)
    m = np.stack([lo, hi, lo * (1.0 if j > 0 else 0.0), hi * (1.0 if j < 3 else 0.0)], axis=1)
    return np.ascontiguousarray(m.astype(np.float32))


_NC_CACHE = {}


def _pool_tables(j):
    pm = np.zeros((128, 36, 128), np.float32)
    ic = np.zeros((12, 128), np.float32)
    bases = [j * OWN, 5 * 128, (j + 1) * OWN - 128]
    for kind, base in enumerate(bases):
        for g, w in enumerate(POOL_SIZES):
            half = w // 2
            t = base + np.arange(128)
            lo = np.clip(t - half, 0, SEQ)
            hi = np.clip(t + half, 0, SEQ)
            cnt = (hi - lo).astype(np.float32)
            ic[kind * 4 + g] = 1.0 / cnt
            for r in range(3):
                sidx = base + (r - 1) * 128 + np.arange(128)
                mtx = ((sidx[:, None] >= lo[None, :]) & (sidx[:, None] < hi[None, :])).astype(np.float32)
                mtx -= (sidx[:, None] == t[None, :]).astype(np.float32) * cnt[None, :]
                pm[:, (kind * 4 + g) * 3 + r, :] = mtx
    return pm, ic.reshape(-1)


def _f32(a):
    return np.ascontiguousarray(np.asarray(a, dtype=np.float32))


def run_fused(inp, debug=False):
    key = "fd" if debug else "f"
    if key not in _NC_CACHE:
        _NC_CACHE[key] = build_fused(debug=debug)
    nc = _NC_CACHE[key]
    x = _f32(inp["x"])
    ropeb = _rope_table(np.arange(SEQ))
    ident = np.eye(128, dtype=np.float32)
    shared = {
        "mod_w": _f32(inp["ev_mod_w"][0]), "mod_b": _f32(inp["ev_mod_b"][0]),
        "pre_g": _f32(inp["ev_pre_g"][0]), "post_g": _f32(inp["ev_post_g"][0]),
        "w_in": _f32(inp["ev_w_in"][0]), "q_norm": _f32(inp["ev_q_norm"][0]), "k_norm": _f32(inp["ev_k_norm"][0]),
        "sink": _f32(inp["ev_sink"][0]), "w_out": _f32(inp["ev_w_out"][0]),
        "mod_w1": _f32(inp["od_mod_w"][0]), "mod_b1": _f32(inp["od_mod_b"][0]),
        "pre_g1": _f32(inp["od_pre_g"][0]), "post_g1": _f32(inp["od_post_g"][0]),
        "w_in1": _f32(inp["od_w_in"][0]), "pool_w": _f32(inp["od_pool_w"][0]),
        "pool_scale": _f32(inp["od_pool_scale"][0]), "w_out1": _f32(inp["od_w_out"][0]),
        "ropeb": ropeb, "ident": ident,
    }
    c = _f32(inp["c"])
    c_ctx = _f32(inp["c_ctx"])
    ctx = _f32(inp["ctx"])
    in_maps = []
    for core in range(8):
        b, j = divmod(core, 4)
        pos = np.clip(np.arange(j * OWN - 256, (j + 1) * OWN + 256), 0, SEQ - 1)
        pm, ic = _pool_tables(j)
        m = dict(shared)
        m.update({
            "xb": x[b], "xo": _ext_rows(x[b], j, halo=256), "ctx": ctx[b],
            "cvec": np.ascontiguousarray(np.stack([c[b], c_ctx])), "cvec1": np.ascontiguousarray(c[b][None]),
            "ropeo": _rope_table(pos), "masks": _masks(j),
            "pm": pm.astype(ml_dtypes.bfloat16), "ic": ic,
        })
        in_maps.append(m)
    res = run_bass_kernel_spmd(nc, in_maps, core_ids=list(range(8)))
    out = np.empty_like(x)
    for core in range(8):
        b, j = divmod(core, 4)
        out[b, j * OWN:(j + 1) * OWN] = res.results[core]["out"]
    if debug:
        return out, res.results
    return out


def kernel(**inputs):
    return run_fused(inputs)
```

```python
import numpy as np
import ml_dtypes
import concourse.bass as bass
import concourse.mybir as mybir
from concourse.bass_utils import run_bass_kernel_spmd

F32 = mybir.dt.float32
BF16 = mybir.dt.bfloat16
AF = mybir.ActivationFunctionType
ALU = mybir.AluOpType
AX = mybir.AxisListType

D = 2048
SEQ = 16384
NBATCH = 2
CTX = 256
HD = 128
OWN = 4096
NTO = OWN // 128
NTE = NTO + 2
NTB = SEQ // 128
NKC = NTB + CTX // 128
NQ = NTO + 2
NKB = NQ + 2
QCOLS = NQ * 128


def slotof(q):
    return q - 1 if 1 <= q <= NTO else (NTO if q == 0 else NTO + 1)


def blocks_of(ntiles):
    out = []
    c = 0
    while c < ntiles * 128:
        w = min(512, ntiles * 128 - c)
        out.append((c, w))
        c += w
    return out
EPS = 1e-6
ATTN_SCALE = HD ** -0.5
GRID_W = 64
ENGS = ("pe", "act", "dve", "pool", "sp")


class Op:
    __slots__ = ("kind", "eng", "fn", "waits", "key", "seq", "slot", "sem", "val", "snap")

    def __init__(self, kind, eng, fn, waits, key, seq, slot=None):
        self.kind, self.eng, self.fn, self.waits, self.key, self.seq, self.slot = kind, eng, fn, waits, key, seq, slot
        self.sem = None
        self.val = None
        self.snap = None


class DmaSlot:
    def __init__(self, prog, name):
        self.sem = prog.nc.alloc_semaphore(name)
        self.count = 0
        self.eng = None


class Prog:
    def __init__(self, nc):
        self.nc = nc
        self.recs = []
        self.sems = {e: nc.alloc_semaphore("s_" + e) for e in ENGS}
        self.last_op = {e: None for e in ENGS}
        self.nslot = 0
        self.slots = []
        self.named = {}
        self.base = 0.0
        self.key = 0.0
        self.seq = 0

    def set_key(self, k):
        self.key = self.base + float(k)

    def bump(self, d=1.0):
        self.key += d

    def slot(self, name=None):
        self.nslot += 1
        sl = DmaSlot(self, name or ("dslot%d" % self.nslot))
        self.slots.append(sl)
        return sl

    def pslot(self, name):
        if name not in self.named:
            self.named[name] = self.slot(name)
        return self.named[name]

    def _new(self, kind, eng, fn, waits, slot=None):
        waits = _flat(waits)
        k = self.key
        for w in waits:
            if w.key > k:
                k = w.key
        self.key = k
        op = Op(kind, eng, fn, waits, k, self.seq, slot)
        self.seq += 1
        self.recs.append(op)
        return op

    def emit(self, eng, fn, waits=()):
        op = self._new("c", eng, fn, waits)
        self.last_op[eng] = op
        return op

    def op(self, eng, method, kwargs, waits=()):
        kw = dict(kwargs)
        return self.emit(eng, lambda e, m=method, k=kw: getattr(e, m)(**k), waits)

    def dma(self, eng, slot, out, in_, waits=(), **kw):
        assert slot.eng in (None, eng), "a DMA slot must be used from a single queue"
        slot.eng = eng
        return self._new("d", eng, lambda e, o=out, i=in_, k=kw: e.dma_start(out=o, in_=i, **k), waits, slot)

    def wait_only(self, eng, waits):
        return self._new("w", eng, None, waits)

    def last(self, eng):
        return self.last_op[eng]

    def full_barrier(self):
        self.key = self.base + 900000.0
        self._new("b", None, None, ())
        self.base += 1000000.0
        self.key = self.base
        return []

    def build(self):
        nc = self.nc
        order = sorted(self.recs, key=lambda r: (r.key, r.seq))
        cnt = {e: 0 for e in ENGS}
        for sl in self.slots:
            sl.count = 0
        per = {e: [] for e in ENGS}
        for r in order:
            if r.kind == "c":
                cnt[r.eng] += 1
                r.sem, r.val = self.sems[r.eng], cnt[r.eng]
                per[r.eng].append(r)
            elif r.kind == "d":
                r.slot.count += 16
                r.sem, r.val = r.slot.sem, r.slot.count
                per[r.eng].append(r)
            elif r.kind == "w":
                per[r.eng].append(r)
            else:
                r.snap = [(self.sems[e], cnt[e]) for e in ENGS if cnt[e]] + \
                         [(sl.sem, sl.count) for sl in self.slots if sl.count]
                for e in ENGS:
                    per[e].append(r)
        with nc.Block() as block:
            def make(engname):
                def body(e):
                    seen = {}

                    def wait(sem, val):
                        key = id(sem)
                        if seen.get(key, 0) >= val:
                            return
                        e.wait_ge(sem, val)
                        seen[key] = val
                    for r in per[engname]:
                        if r.kind == "b":
                            for sem, val in r.snap:
                                wait(sem, val)
                            continue
                        for w in r.waits:
                            wait(w.sem, w.val)
                        if r.fn is not None:
                            inst = r.fn(e)
                            inst.then_inc(r.sem, 1 if r.kind == "c" else 16)
                return body
            block.tensor(make("pe"))
            block.scalar(make("act"))
            block.vector(make("dve"))
            block.gpsimd(make("pool"))
            block.sync(make("sp"))


def _flat(waits):
    out = []
    for w in waits:
        if w is None:
            continue
        if isinstance(w, (list, tuple)):
            out.extend(_flat(w))
        else:
            out.append(w)
    return tuple(out)


class Arena:
    def __init__(self, nc, nbytes):
        self.t = nc.alloc_sbuf_tensor("arena", [128, nbytes // 4], F32).ap()
        self.nbytes = nbytes
        self.off = 0

    def alloc(self, free_shape, dtype):
        n = int(np.prod(free_shape))
        esz = 2 if dtype == BF16 else 4
        nb = (n * esz + 31) // 32 * 32
        assert self.off + nb <= self.nbytes, ("SBUF arena overflow", self.off, nb, self.nbytes)
        ap = self.t[:, self.off // 4:(self.off + nb) // 4]
        self.off += nb
        if dtype == BF16:
            ap = ap.bitcast(BF16)
        ap = ap[:, 0:n]
        if len(free_shape) == 2:
            ap = ap.rearrange("p (a b) -> p a b", b=free_shape[1])
        elif len(free_shape) == 3:
            ap = ap.rearrange("p (a b c) -> p a b c", b=free_shape[1], c=free_shape[2])
        return ap

    def mark(self):
        return self.off

    def release(self, m):
        self.off = m


class Psum:
    def __init__(self, nc):
        self.t = nc.alloc_psum_tensor("psum_all", [128, 8, 512], F32).ap()

    def f32(self, bank, nbanks=1):
        ap = self.t[:, bank:bank + nbanks, :]
        return ap.rearrange("p a b -> p (a b)")

    def bf16(self, bank, nbanks, inner):
        ap = self.t[:, bank:bank + nbanks, :].rearrange("p a b -> p (a b)").bitcast(BF16)
        return ap.rearrange("p (c t) -> p c t", t=inner)


class Ctx:
    pass


def load_consts(P, A, C, ident_d):
    C.slot_c = P.slot("c_const")
    C.identf = A.alloc([128], F32)
    C.identb = A.alloc([128], BF16)
    C.onesf = A.alloc([128], F32)
    C.onesb = A.alloc([128], BF16)
    t = P.dma("sp", C.slot_c, C.identf, ident_d)
    t1 = P.op("dve", "tensor_copy", dict(out=C.identb, in_=C.identf), [t])
    t2 = P.op("dve", "memset", dict(ap=C.onesf, constant=1.0))
    t3 = P.op("dve", "memset", dict(ap=C.onesb, constant=1.0))
    return [t1, t2, t3]


def modulation(P, A, PS, C, cvec_d, ncv, mod_w, mod_b, res, deps):
    m1 = A.mark()
    cfm = A.alloc([ncv, 16], F32)
    screp = A.alloc([ncv, 16, 128], F32)
    sl = P.pslot("mod_c")
    tl = None
    for v in range(ncv):
        tl = P.dma("sp", sl, cfm[:, v, :], cvec_d[v].rearrange("(c p) -> p c", p=128),
                   allow_slow_non_contiguous=True)
    ts = P.op("act", "activation", dict(out=cfm, in_=cfm, func=AF.Silu), [tl, deps])
    tr = None
    for v in range(ncv):
        for c in range(16):
            tr = P.op("dve", "tensor_scalar", dict(
                out=screp[:, v, c, :], in0=C.onesf, scalar1=cfm[:, v, c:c + 1], scalar2=None, op0=ALU.mult),
                [ts, deps])
    allg = sorted(set(g for (v, g) in res))
    wt = [A.alloc([2048], F32) for _ in range(3)]
    brow = A.alloc([2048], F32)
    wslots = [P.pslot("pw%d" % i) for i in range(3)]
    bslot = P.pslot("mod_b")
    wfree = [deps, deps, deps]
    k = 0
    ev_prev = None
    for g in allg:
        vs = [v for v in range(ncv) if (v, g) in res]
        tb = P.dma("pool", bslot, brow, mod_b[g * 2048:(g + 1) * 2048].partition_broadcast(128), [ev_prev])
        last_mm = None
        for c in range(16):
            s = k % 3
            tw = P.dma("sp", wslots[s], wt[s], mod_w[c * 128:(c + 1) * 128, g * 2048:(g + 1) * 2048], [wfree[s]])
            for vi, v in enumerate(vs):
                for nb in range(4):
                    last_mm = P.op("pe", "matmul", dict(
                        out=PS.f32(4 * vi + nb), lhsT=screp[:, v, c, :], rhs=wt[s][:, nb * 512:(nb + 1) * 512],
                        start=(c == 0), stop=(c == 15)), [tw, tr, ev_prev if c == 0 else None])
            wfree[s] = last_mm
            k += 1
        for vi, v in enumerate(vs):
            ev_prev = P.op("dve", "tensor_tensor", dict(
                out=res[(v, g)], in0=PS.f32(4 * vi, 4), in1=brow, op=ALU.add), [last_mm, tb])
    return ev_prev


def diag_extract(P, A, C, row, out_fm, deps):
    m = A.mark()
    tmp = A.alloc([16, 128], F32)
    t = None
    for j in range(16):
        t = P.op("dve", "tensor_tensor", dict(out=tmp[:, j, :], in0=row[:, j * 128:(j + 1) * 128],
                                                         in1=C.identf, op=ALU.mult), [deps])
    t = P.op("dve", "tensor_reduce", dict(out=out_fm, in_=tmp, axis=AX.X, op=ALU.add), [t])
    return t


def prep_weight(P, A, PS, C, w_dram, col0, ncols, a_fm, brep, wdst, bias_row, deps, psum_bank0=0, stage=None):
    m = A.mark()
    if stage is None:
        stage = [A.alloc([512], F32) for _ in range(3)]
    slots = [P.pslot("pw%d" % i) for i in range(3)]
    free = [deps, deps, deps]
    k = 0
    last = None
    nsl = ncols // 512
    assert nsl <= 4 or bias_row is None or True
    for s0 in range(0, nsl, 4):
        grp = list(range(s0, min(nsl, s0 + 4)))
        lastmm = {}
        for c in range(16):
            for sl in grp:
                b = k % 3
                tw = P.dma("sp", slots[b], stage[b],
                           w_dram[c * 128:(c + 1) * 128, col0 + sl * 512:col0 + (sl + 1) * 512], [free[b]])
                rd = []
                if a_fm is not None:
                    t1 = P.op("dve", "tensor_scalar", dict(
                        out=wdst[:, c, sl * 512:(sl + 1) * 512], in0=stage[b], scalar1=a_fm[:, c:c + 1], scalar2=None,
                        op0=ALU.mult), [tw, deps])
                else:
                    t1 = P.op("dve", "tensor_copy", dict(
                        out=wdst[:, c, sl * 512:(sl + 1) * 512], in_=stage[b]), [tw, deps])
                rd.append(t1)
                last = t1
                if bias_row is not None:
                    t2 = P.op("pe", "matmul", dict(
                        out=PS.f32(psum_bank0 + sl - s0), lhsT=brep[:, c, :], rhs=stage[b],
                        start=(c == 0), stop=(c == 15)), [tw, deps])
                    rd.append(t2)
                    lastmm[sl] = t2
                free[b] = rd
                k += 1
        if bias_row is not None:
            for sl in grp:
                last = P.op("act", "activation", dict(
                    out=bias_row[:, sl * 512:(sl + 1) * 512], in_=PS.f32(psum_bank0 + sl - s0), func=AF.Identity),
                    [lastmm[sl]])
            deps = [deps, last]
    return [last, t1]


class FrontEnd:
    def __init__(self, P, A, PS, C, pt_bank, nx=3):
        self.P, self.C, self.PS = P, C, PS
        self.nx = nx
        self.xt = [A.alloc([2048], F32) for _ in range(nx)]
        self.xn = [A.alloc([2048], BF16) for _ in range(2)]
        self.xnT = [A.alloc([16, 128], BF16) for _ in range(2)]
        self.junk = A.alloc([2048], BF16)
        self.ss = A.alloc([8], F32)
        self.rstd = A.alloc([8], F32)
        self.slots = [P.pslot("fe%d" % i) for i in range(nx)]
        self.pt = PS.bf16(pt_bank, 2, 128)
        self.xt_free = [[] for _ in range(nx)]
        self.xn_free = [None, None]
        self.xnT_free = [[], []]
        self.pt_free = None
        self.n = 0

    def run(self, src_ap, deps=None, k0=None):
        P, C = self.P, self.C
        i = self.n
        self.n += 1
        b = i % 2
        bx = i % self.nx
        k = i % 8
        if k0 is None:
            k0 = P.key - P.base
        xt, xn, xnT = self.xt[bx], self.xn[b], self.xnT[b]
        P.set_key(k0)
        tx = P.dma("sp", self.slots[bx], xt, src_ap, [self.xt_free[bx], deps])
        P.set_key(k0 + 1)
        tsq = P.op("act", "activation", dict(out=self.junk, in_=xt, func=AF.Square,
                                                   accum_out=self.ss[:, k:k + 1]), [tx])
        P.set_key(k0 + 2)
        tms = P.op("dve", "tensor_scalar", dict(out=self.rstd[:, k:k + 1], in0=self.ss[:, k:k + 1],
                                                      scalar1=1.0 / D, scalar2=EPS, op0=ALU.mult, op1=ALU.add),
                     [tsq])
        P.set_key(k0 + 3)
        tsr = P.op("act", "activation", dict(out=self.rstd[:, k:k + 1], in_=self.rstd[:, k:k + 1],
                                                   func=AF.Sqrt), [tms])
        P.set_key(k0 + 4)
        trc = P.op("dve", "reciprocal", dict(out=self.rstd[:, k:k + 1], in_=self.rstd[:, k:k + 1]), [tsr])
        txn = P.op("dve", "tensor_scalar", dict(out=xn, in0=xt, scalar1=self.rstd[:, k:k + 1], scalar2=None,
                                                      op0=ALU.mult), [trc, self.xn_free[b]])
        P.set_key(k0 + 5)
        tt = None
        for c in range(16):
            tt = P.op("pe", "transpose", dict(out=self.pt[:, c, :], in_=xn[:, c * 128:(c + 1) * 128],
                                                         identity=C.identb),
                        [txn, self.pt_free] if c == 0 else [])
        self.xn_free[b] = tt
        P.set_key(k0 + 6)
        tcp = P.op("act", "activation", dict(out=xnT, in_=self.pt, func=AF.Copy), [tt, self.xnT_free[b]])
        self.pt_free = tcp
        self.xt_free[bx] = [tsq, txn]
        self.xnT_free[b] = []
        self.cur = b
        P.set_key(k0 + 7)
        return xnT, tcp, xt, tx, self.rstd[:, k:k + 1]

    def readers(self, b, toks, x_toks=()):
        self.xnT_free[b] = list(self.xnT_free[b]) + list(_flat(toks))


def head_post(P, A, src, nh, dst_bf, do_norm, gain_row, rope_cs, tmp, deps, scale=None):
    t = deps
    tmp["n"] = tmp.get("n", 0) + 1
    sq, st8 = tmp["sq"][tmp["n"] % len(tmp["sq"])], tmp["st8"][tmp["n"] % len(tmp["st8"])]
    if do_norm:
        t = P.op("dve", "tensor_tensor", dict(out=sq[:, 0:nh, :], in0=src, in1=src, op=ALU.mult), [t])
        t = P.op("dve", "tensor_reduce", dict(out=st8[:, 0:nh], in_=sq[:, 0:nh, :], axis=AX.X, op=ALU.add), [t])
        t = P.op("dve", "tensor_scalar", dict(out=st8[:, 0:nh], in0=st8[:, 0:nh], scalar1=1.0 / HD, scalar2=EPS,
                                                    op0=ALU.mult, op1=ALU.add), [t])
        P.bump()
        t = P.op("act", "activation", dict(out=st8[:, 0:nh], in_=st8[:, 0:nh], func=AF.Sqrt), [t])
        P.bump()
        t = P.op("dve", "reciprocal", dict(out=st8[:, 0:nh], in_=st8[:, 0:nh]), [t])
        t = P.op("dve", "tensor_tensor", dict(out=src, in0=src,
                                                    in1=st8[:, 0:nh].unsqueeze(2).broadcast_to([128, nh, 128]),
                                                    op=ALU.mult), [t])
        t = P.op("dve", "tensor_tensor", dict(out=src, in0=src,
                                                    in1=gain_row.unsqueeze(1).broadcast_to([128, nh, 128]),
                                                    op=ALU.mult), [t])
    elif scale is not None:
        t = P.op("dve", "tensor_scalar", dict(out=src, in0=src, scalar1=float(scale), scalar2=None,
                                                    op0=ALU.mult), [t])
    if rope_cs is not None:
        cosb = rope_cs[:, 0:64].unsqueeze(1).broadcast_to([128, nh, 64])
        sinb = rope_cs[:, 64:128].unsqueeze(1).broadcast_to([128, nh, 64])
        x1 = src[:, :, 0:64]
        x2 = src[:, :, 64:128]
        ta, tb_ = sq[:, 0:nh, 0:64], sq[:, 0:nh, 64:128]
        t = P.op("dve", "tensor_tensor", dict(out=ta, in0=x1, in1=cosb, op=ALU.mult), [t])
        t = P.op("dve", "tensor_tensor", dict(out=tb_, in0=x2, in1=sinb, op=ALU.mult), [t])
        t = P.op("dve", "tensor_tensor", dict(out=dst_bf[:, :, 0:64], in0=ta, in1=tb_, op=ALU.subtract), [t])
        t = P.op("dve", "tensor_tensor", dict(out=ta, in0=x2, in1=cosb, op=ALU.mult), [t])
        t = P.op("dve", "tensor_tensor", dict(out=tb_, in0=x1, in1=sinb, op=ALU.mult), [t])
        t = P.op("dve", "tensor_tensor", dict(out=dst_bf[:, :, 64:128], in0=ta, in1=tb_, op=ALU.add), [t])
    else:
        t = P.op("dve", "tensor_copy", dict(out=dst_bf, in_=src), [t])
    return t


def rep16(P, C, fm, rep, deps):
    t = None
    for c in range(16):
        t = P.op("dve", "tensor_scalar", dict(out=rep[:, c, :], in0=C.onesf, scalar1=fm[:, c:c + 1],
                                                         scalar2=None, op0=ALU.mult), [deps])
    return t


def adaln_vectors(P, A, PS, C, cvec_d, ncv, mod_w, mod_b, pre_g, post_g, G_d, deps):
    fms = [(A.alloc([16], F32), A.alloc([16], F32)) for _ in range(ncv)]
    m = A.mark()
    res = {}
    for v in range(ncv):
        res[(v, 0)] = A.alloc([2048], F32)
        res[(v, 1)] = A.alloc([2048], F32)
    res[(0, 2)] = A.alloc([2048], F32)
    prow = A.alloc([2048], F32)
    sl = P.pslot("adaln")
    tp = P.dma("pool", sl, prow, pre_g.partition_broadcast(128), [deps])
    tm = modulation(P, A, PS, C, cvec_d, ncv, mod_w, mod_b, res, deps)
    last = []
    for v in range(ncv):
        t = P.op("dve", "scalar_tensor_tensor", dict(out=res[(v, 1)], in0=res[(v, 1)], scalar=1.0, in1=prow,
                                                                op0=ALU.add, op1=ALU.mult), [tm, tp])
        t1 = diag_extract(P, A, C, res[(v, 1)], fms[v][0], [t])
        t2 = diag_extract(P, A, C, res[(v, 0)], fms[v][1], [tm])
        last += [t1, t2]
    tp2 = P.dma("pool", sl, prow, post_g.partition_broadcast(128), [last])
    tg = P.op("dve", "tensor_tensor", dict(out=res[(0, 2)], in0=res[(0, 2)], in1=prow, op=ALU.mult), [tm, tp2])
    tgd = P.dma("pool", sl, G_d, res[(0, 2)], [tg])
    last.append(tgd)
    A.release(m)
    return fms, last


def emit_l0(nc, P, A, PS, C, din, dscr, X1_d):
    xb = din("xb", [SEQ, D])
    xo = din("xo", [NKB * 128, D])
    ctx_d = din("ctx", [CTX, D])
    cvec = din("cvec", [2, D])
    mod_w = din("mod_w", [D, 3 * D])
    mod_b = din("mod_b", [3 * D])
    pre_g = din("pre_g", [D])
    post_g = din("post_g", [D])
    w_in = din("w_in", [D, 5120])
    q_norm = din("q_norm", [HD])
    k_norm = din("k_norm", [HD])
    sink = din("sink", [8])
    w_out = din("w_out", [D, D])
    ropeb = din("ropeb", [SEQ, 128])
    ropeo = din("ropeo", [NKB * 128, 128])
    masks_d = din("masks", [128, 4, 128])

    G_d = dscr("G_d", [128, D], F32)
    KAT_d = dscr("KAT_d", [2, 128, NKC * 128])
    VA_d = dscr("VA_d", [NKC * 128, 2, 128])
    QT_d = dscr("QT_d", [16, 128, QCOLS])
    XNT_d = dscr("XNT_d", [NQ, 128, D])
    SZT_d = dscr("SZT_d", [16, 128, QCOLS])
    YT_d = dscr("YT_d", [16, 128, QCOLS])
    mall = A.mark()

    cons = C.cons
    fms, t_ad = adaln_vectors(P, A, PS, C, cvec, 2, mod_w, mod_b, pre_g, post_g, G_d, cons)
    (A_fm, B_fm), (Ac_fm, Bc_fm) = fms
    Brep = A.alloc([16, 128], F32)
    t_brep = rep16(P, C, B_fm, Brep, t_ad)
    qg_row = A.alloc([128], F32)
    kg_row = A.alloc([128], F32)
    masks = A.alloc([4, 128], BF16)
    esink = A.alloc([8], F32)
    KBT = A.alloc([2, NKB * 128], BF16)
    VB = A.alloc([NKB, 2, 128], BF16)
    KBTc = A.alloc([2, CTX], BF16)
    VBc = A.alloc([2, 2, 128], BF16)
    sl0 = P.slot("p0misc")
    mk = A.mark()
    mstage = A.alloc([4, 128], F32)
    t1 = P.dma("pool", sl0, qg_row, q_norm.partition_broadcast(128), [t_ad, t_brep])
    t2 = P.dma("pool", sl0, kg_row, k_norm.partition_broadcast(128), [t_ad, t_brep])
    t3 = P.dma("pool", sl0, esink, sink.partition_broadcast(128), [t_ad, t_brep])
    t4 = P.dma("pool", sl0, mstage, masks_d, [t_ad, t_brep])
    tq = P.op("dve", "tensor_scalar", dict(out=qg_row, in0=qg_row, scalar1=float(ATTN_SCALE), scalar2=None,
                                                 op0=ALU.mult), [t1, t2, t3, t4])
    tmk = P.op("dve", "tensor_copy", dict(out=masks, in_=mstage), [tq])
    tes = P.op("act", "activation", dict(out=esink, in_=esink, func=AF.Exp), [t4])
    A.release(mk)
    p0_done = P.full_barrier()

    mkv = A.mark()
    Brepc = A.alloc([16, 128], F32)
    t_brc = rep16(P, C, Bc_fm, Brepc, p0_done)
    WB = A.alloc([16, 512], BF16)
    WA = A.alloc([16, 512], BF16)
    WC = A.alloc([16, 1024], BF16)
    bB = A.alloc([512], F32)
    bA = A.alloc([512], F32)
    bC = A.alloc([1024], F32)
    pstage = [A.alloc([512], F32) for _ in range(3)]
    tw1 = prep_weight(P, A, PS, C, w_in, 2560, 512, A_fm, Brep, WB, bB, [t_brep, t_brc], psum_bank0=4, stage=pstage)
    tw2 = prep_weight(P, A, PS, C, w_in, 2048, 512, A_fm, Brep, WA, bA, [tw1], psum_bank0=4, stage=pstage)
    tw3 = prep_weight(P, A, PS, C, w_in, 2048, 1024, Ac_fm, Brepc, WC, bC, [tw2], psum_bank0=4, stage=pstage)
    wready = [tw1, tw2, tw3]
    FE = FrontEnd(P, A, PS, C, pt_bank=0)
    pkv = PS.f32(2, 2)
    ptk = PS.bf16(4, 1, 128)
    kvf = [A.alloc([1024], F32) for _ in range(2)]
    kbf = [A.alloc([4, 128], BF16) for _ in range(2)]
    ropes = [A.alloc([128], F32) for _ in range(4)]
    rslots = [P.pslot("rope%d" % i) for i in range(4)]
    tmp = {"sq": [A.alloc([4, 128], F32)], "st8": [A.alloc([8], F32) for _ in range(4)]}
    kst = [A.alloc([2, 512], BF16) for _ in range(2)]
    vst = [A.alloc([2, 128], BF16) for _ in range(2)]
    kslots = [P.slot(), P.slot()]
    vslots = [P.slot(), P.slot()]
    st = Ctx()
    st.kvf_free = [None, None]
    st.kbf_free = [None, None]
    st.rope_free = [None] * 4
    st.kst_free = [None, None]
    st.vst_free = [None, None]
    st.pkv_free = None
    st.ptk_free = None
    st.n = 0
    st.out_toks = []

    def kv_vcopy(mode, idx, b, kf, tk):
        if mode == "ext":
            tv = P.op("dve", "tensor_copy", dict(out=VB[:, idx, :, :],
                                                 in_=kf[:, 256:512].rearrange("p (h d) -> p h d", d=128)), [tk])
            st.kvf_free[b] = tv
            return [tv]
        tv = P.op("dve", "tensor_copy", dict(
            out=vst[b], in_=kf[:, 256:512].rearrange("p (h d) -> p h d", d=128)), [tk, st.vst_free[b]])
        st.kvf_free[b] = tv
        if mode == "ctx":
            tv2 = P.op("dve", "tensor_copy", dict(
                out=VBc[:, idx, :, :], in_=kf[:, 768:1024].rearrange("p (h d) -> p h d", d=128)), [tk])
            st.kvf_free[b] = tv2
            return [tv, tv2]
        return [tv]

    def kv_tile(src_ap, rope_ap, W, brow, ncols, mode, idx):
        i = st.n
        st.n += 1
        b = i % 2
        rb = i % 4
        k0 = float(i)
        P.set_key(k0 + 4)
        if rope_ap is not None:
            trope = P.dma("pool", rslots[rb], ropes[rb], rope_ap, [st.rope_free[rb], wready if i < 4 else None])
        else:
            trope = None
        xnT, tready, xt, tx, _ = FE.run(src_ap, wready if i < 4 else None, k0)
        fb = FE.cur
        nsl = ncols // 512
        mm = None
        for sl in range(nsl):
            for c in range(16):
                mm = P.op("pe", "matmul", dict(
                    out=pkv[:, sl * 512:(sl + 1) * 512], lhsT=xnT[:, c, :], rhs=W[:, c, sl * 512:(sl + 1) * 512],
                    start=(c == 0), stop=(c == 15)), [tready, st.pkv_free] if (c == 0 and sl == 0) else [])
        FE.readers(fb, [mm])
        kf = kvf[b]
        P.set_key(k0 + 8)
        tev = P.op("dve", "tensor_tensor", dict(out=kf[:, 0:ncols], in0=pkv[:, 0:ncols], in1=brow[:, 0:ncols],
                                                      op=ALU.add), [mm, st.kvf_free[b]])
        st.pkv_free = tev
        kb = kbf[b]
        toks = []
        if mode == "bat":
            ksrc = kf[:, 0:256].rearrange("p (h d) -> p h d", d=128)
            tk = head_post(P, A, ksrc, 2, kb[:, 0:2, :], True, kg_row, ropes[rb], tmp, [tev, trope, st.kbf_free[b]])
            st.rope_free[rb] = tk
            nk = 2
        elif mode == "ext":
            ksrc = kf[:, 0:256].rearrange("p (h d) -> p h d", d=128)
            tk = head_post(P, A, ksrc, 2, kb[:, 0:2, :], False, None, ropes[rb], tmp, [tev, trope, st.kbf_free[b]])
            st.rope_free[rb] = tk
            nk = 2
        else:
            ksrc = kf[:, 0:256].rearrange("p (h d) -> p h d", d=128)
            tk = head_post(P, A, ksrc, 2, kb[:, 0:2, :], True, kg_row, None, tmp, [tev, st.kbf_free[b]])
            ksrc2 = kf[:, 512:768].rearrange("p (h d) -> p h d", d=128)
            tk = head_post(P, A, ksrc2, 2, kb[:, 2:4, :], False, None, None, tmp, [tk])
            nk = 4
        P.set_key(k0 + 8)
        vtoks = kv_vcopy(mode, idx, b, kf, tk)
        P.set_key(k0 + 11)
        tt = None
        for h in range(nk):
            tt = P.op("pe", "transpose", dict(out=ptk[:, h, :], in_=kb[:, h, :], identity=C.identb),
                        [tk, st.ptk_free] if h == 0 else [])
        st.kbf_free[b] = tt
        P.set_key(k0 + 12)
        if mode == "ext":
            tc = P.op("act", "activation", dict(out=KBT[:, :, idx * 128:(idx + 1) * 128], in_=ptk[:, 0:2, :],
                                                      func=AF.Copy), [tt])
            st.ptk_free = tc
            toks += [tc]
        else:
            grp = 4 if mode == "bat" else 2
            g, r = divmod(idx, grp)
            sb = g % 2
            tc = P.op("act", "activation", dict(out=kst[sb][:, :, r * 128:(r + 1) * 128], in_=ptk[:, 0:2, :],
                                                      func=AF.Copy), [tt, st.kst_free[sb] if r == 0 else None])
            if mode == "ctx":
                tc2 = P.op("act", "activation", dict(out=KBTc[:, :, idx * 128:(idx + 1) * 128],
                                                           in_=ptk[:, 2:4, :], func=AF.Copy), [tt])
                tc = tc2
            st.ptk_free = tc
            base = (0 if mode == "bat" else SEQ)
            tok0 = base + idx * 128
            tvd = P.dma("pool", vslots[b], VA_d[tok0:tok0 + 128, :, :], vst[b], [vtoks])
            st.vst_free[b] = tvd
            toks.append(tvd)
            if r == grp - 1:
                c0 = base + g * grp * 128
                tkd = P.dma("pool", kslots[sb], KAT_d[:, :, c0:c0 + grp * 128].rearrange("h d t -> d h t"),
                            kst[sb][:, :, 0:grp * 128], [tc])
                st.kst_free[sb] = tkd
                toks.append(tkd)
        st.out_toks = [st.out_toks[-8:], toks]
        st.all_toks.extend(toks)

    st.all_toks = []
    for e_ in range(NKB):
        kv_tile(xo[e_ * 128:(e_ + 1) * 128, :], ropeo[e_ * 128:(e_ + 1) * 128, :], WB, bB, 512, "ext", e_)
    for t_ in range(CTX // 128):
        kv_tile(ctx_d[t_ * 128:(t_ + 1) * 128, :], None, WC, bC, 1024, "ctx", t_)
    for t_ in range(NTB):
        kv_tile(xb[t_ * 128:(t_ + 1) * 128, :], ropeb[t_ * 128:(t_ + 1) * 128, :], WA, bA, 512, "bat", t_)
    pkv_done = P.full_barrier()
    A.release(mkv)

    m2 = A.mark()
    WQ = A.alloc([16, 2048], BF16)
    bQ = A.alloc([2048], F32)
    twq = prep_weight(P, A, PS, C, w_in, 0, 2048, A_fm, Brep, WQ, bQ, pkv_done, psum_bank0=2)
    FE = FrontEnd(P, A, PS, C, pt_bank=0, nx=2)
    pq = [PS.f32(2, 2), PS.f32(4, 2)]
    ptq = PS.bf16(6, 1, 128)
    qf = [A.alloc([8, 128], F32) for _ in range(4)]
    qbf = [A.alloc([16, 128], BF16) for _ in range(2)]
    qT = [A.alloc([16, 128], BF16) for _ in range(2)]
    ropes = [A.alloc([128], F32) for _ in range(4)]
    rslots = [P.pslot("rope%d" % i) for i in range(4)]
    qslots = [P.pslot("q0"), P.pslot("q1")]
    xslots = [P.pslot("xn0"), P.pslot("xn1")]
    tmp = {"sq": [A.alloc([8, 128], F32)], "st8": [A.alloc([8], F32) for _ in range(4)]}
    pq_free = [None, None]
    qf_free = [None] * 4
    qbf_free = [None, None]
    qT_free = [None, None]
    rope_free = [None] * 4
    ptq_free = None
    for o in range(NQ):
        b = o % 2
        e_ = o + 1
        so = slotof(o)
        k0 = float(o)
        rb = o % 4
        P.set_key(k0 + 4)
        trope = P.dma("pool", rslots[rb], ropes[rb], ropeo[e_ * 128:(e_ + 1) * 128, :],
                      [rope_free[rb], twq if o < 4 else None])
        xnT, tready, xt, tx, _ = FE.run(xo[e_ * 128:(e_ + 1) * 128, :], twq if o < 4 else None, k0)
        fb = FE.cur
        txd = P.dma("pool", xslots[b], XNT_d[so].rearrange("p (c t) -> p c t", t=128), xnT, [tready])
        mms = []
        for half in range(2):
            mm = None
            for sl2 in range(2):
                sl = half * 2 + sl2
                for c in range(16):
                    mm = P.op("pe", "matmul", dict(
                        out=pq[half][:, sl2 * 512:(sl2 + 1) * 512], lhsT=xnT[:, c, :],
                        rhs=WQ[:, c, sl * 512:(sl + 1) * 512], start=(c == 0), stop=(c == 15)),
                        [tready, pq_free[half]] if (c == 0 and sl2 == 0) else [])
            mms.append(mm)
        FE.readers(fb, [mms[1], txd])
        tpost = []
        for half in range(2):
            qi = half * 2 + b
            q3 = qf[qi]
            P.set_key(k0 + 8)
            tev = P.op("dve", "tensor_tensor", dict(
                out=q3.rearrange("p h d -> p (h d)"), in0=pq[half], in1=bQ[:, half * 1024:(half + 1) * 1024],
                op=ALU.add), [mms[half], qf_free[qi]])
            pq_free[half] = tev
            if half == 0:
                tk = head_post(P, A, q3, 8, qbf[b][:, 0:8, :], True, qg_row, ropes[rb], tmp,
                               [tev, trope, qbf_free[b]])
            else:
                tk = head_post(P, A, q3, 8, qbf[b][:, 8:16, :], False, None, ropes[rb], tmp,
                               [tev, trope, qbf_free[b]], scale=ATTN_SCALE)
            qf_free[qi] = tk
            tpost.append(tk)
        rope_free[rb] = tpost
        tc = None
        for half in range(2):
            P.set_key(k0 + 11 + half)
            tt = None
            for h in range(8):
                tt = P.op("pe", "transpose", dict(
                    out=ptq[:, h, :], in_=qbf[b][:, half * 8 + h, :], identity=C.identb),
                    [tpost[half], ptq_free] if h == 0 else [])
            P.set_key(k0 + 12 + half)
            tc = P.op("act", "activation", dict(
                out=qT[b][:, half * 8:(half + 1) * 8, :], in_=ptq, func=AF.Copy),
                [tt, qT_free[b] if half == 0 else None])
            ptq_free = tc
        qbf_free[b] = tt
        tqd = P.dma("pool", qslots[b], QT_d[:, :, so * 128:(so + 1) * 128].rearrange("h d t -> d h t"), qT[b], [tc])
        qT_free[b] = tqd
    p2a_done = P.full_barrier()
    A.release(m2)

    p2b_done = gate_phase(P, A, PS, C, w_in, 3072, A_fm, Brep, XNT_d, SZT_d, None, p2a_done, NQ)

    m4 = A.mark()
    KATh = A.alloc([NKC * 128], BF16)
    VAh = A.alloc([NKC, 128], BF16)
    qblk = [A.alloc([4, 512], BF16) for _ in range(2)]
    zblk = [A.alloc([4, 512], BF16) for _ in range(2)]
    pT = [A.alloc([1024], BF16) for _ in range(3)]
    pr = [A.alloc([512], BF16) for _ in range(2)]
    acc = [A.alloc([512], F32) for _ in range(2)]
    accs_free = [None, None]
    ahl = [A.alloc([2, 512], BF16) for _ in range(2)]
    ahl_free = [None, None]
    pr_free = [None, None]
    rs = [A.alloc([512], F32) for _ in range(2)]
    yf = [A.alloc([512], F32) for _ in range(2)]
    yb = [A.alloc([512], BF16) for _ in range(2)]
    kvs = [P.pslot("kvh0"), P.pslot("kvh1")]
    qs = [P.pslot("qb0"), P.pslot("qb1")]
    zs = [P.pslot("zb0"), P.pslot("zb1")]
    ys = [P.pslot("y0"), P.pslot("y1")]
    Sb = [PS.f32(0, 2), PS.f32(2, 2)]
    Ob = [PS.f32(4), PS.f32(5)]
    Ub = [PS.f32(6), PS.f32(7)]
    S_free = [None, None]
    pT_free = [None, None, None]
    acc_free = [None, None]
    rs_free = [None, None]
    yb_free = [None, None]
    qblk_free = [None, None]
    zblk_free = [None, None]
    kv_free = p2b_done
    NG = NKC // 2
    it = 0
    blkn = 0
    for kvh in range(2):
        tk1 = P.dma("sp", kvs[0], KATh, KAT_d[kvh], [kv_free])
        tk2 = P.dma("sp", kvs[1], VAh, VA_d[:, kvh, :].rearrange("(t p) d -> p t d", p=128), [kv_free])
        kvready = [tk1, tk2]
        lastuse = None
        for (c0, bw) in blocks_of(NQ):
            bb = blkn % 2
            blkn += 1
            tq = P.dma("sp", qs[bb], qblk[bb][:, :, 0:bw], QT_d[kvh * 4:(kvh + 1) * 4, :, c0:c0 + bw]
                       .rearrange("h d t -> d h t"), [qblk_free[bb]])
            tz = P.dma("sp", zs[bb], zblk[bb][:, :, 0:bw], SZT_d[kvh * 4:(kvh + 1) * 4, :, c0:c0 + bw]
                       .rearrange("h d t -> d h t"), [zblk_free[bb]])
            for hd in range(4):
                ab = it % 2
                it += 1
                qrhs = qblk[bb][:, hd, 0:bw]

                def QK(g, ab=ab, qrhs=qrhs):
                    sb = g % 2
                    t = None
                    for j in range(2):
                        kc = 2 * g + j
                        t = P.op("pe", "matmul", dict(
                            out=Sb[sb][:, j * bw:(j + 1) * bw], lhsT=KATh[:, kc * 128:(kc + 1) * 128], rhs=qrhs,
                            start=True, stop=True), [S_free[sb], tq, kvready] if j == 0 else [])
                    return t

                tqk = {0: QK(0), 1: QK(1)}
                tpv = None
                for g in range(NG):
                    sb = g % 2
                    pb = g % 3
                    tex = P.op("act", "activation", dict(out=pT[pb][:, 0:2 * bw], in_=Sb[sb][:, 0:2 * bw], func=AF.Exp),
                                 [tqk[g], pT_free[pb]])
                    S_free[sb] = tex
                    for j in range(2):
                        kc = 2 * g + j
                        P.op("pe", "matmul", dict(
                            out=Ob[ab][:, 0:bw], lhsT=VAh[:, kc, :], rhs=pT[pb][:, j * bw:(j + 1) * bw],
                            start=(g == 0 and j == 0), stop=(g == NG - 1 and j == 1)),
                            [tex, acc_free[ab]] if j == 0 else [])
                    tpair = P.op("dve", "tensor_tensor", dict(out=pr[g % 2][:, 0:bw], in0=pT[pb][:, 0:bw],
                                                              in1=pT[pb][:, bw:2 * bw], op=ALU.add),
                                  [tex, pr_free[g % 2]])
                    if g == 0:
                        tacc = P.op("dve", "tensor_copy", dict(out=acc[ab][:, 0:bw], in_=pr[g % 2][:, 0:bw]),
                                    [tpair, accs_free[ab]])
                    else:
                        tacc = P.op("dve", "tensor_tensor", dict(out=acc[ab][:, 0:bw], in0=acc[ab][:, 0:bw],
                                                                 in1=pr[g % 2][:, 0:bw], op=ALU.add), [tpair])
                    pr_free[g % 2] = tacc
                    tpv = P.last("pe")
                    pT_free[pb] = [tpv, tpair]
                    if g + 2 < NG:
                        tqk[g + 2] = QK(g + 2)
                thi = P.op("dve", "tensor_copy", dict(out=ahl[ab][:, 0, 0:bw], in_=acc[ab][:, 0:bw]), [tacc, ahl_free[ab]])
                tlo = P.op("dve", "tensor_tensor", dict(out=ahl[ab][:, 1, 0:bw], in0=acc[ab][:, 0:bw],
                                                        in1=ahl[ab][:, 0, 0:bw], op=ALU.subtract), [thi])
                accs_free[ab] = tlo
                P.op("pe", "matmul", dict(out=Ub[ab][:, 0:bw], lhsT=C.onesb, rhs=ahl[ab][:, 0, 0:bw],
                                          start=True, stop=False), [tlo])
                tpv = P.op("pe", "matmul", dict(out=Ub[ab][:, 0:bw], lhsT=C.onesb, rhs=ahl[ab][:, 1, 0:bw],
                                                start=False, stop=True), [])
                ahl_free[ab] = tpv
                t = P.op("dve", "reciprocal", dict(out=rs[ab][:, 0:bw], in_=Ub[ab][:, 0:bw]), [tpv, rs_free[ab]])
                t = P.op("dve", "tensor_tensor", dict(out=yf[ab][:, 0:bw], in0=Ob[ab][:, 0:bw], in1=rs[ab][:, 0:bw],
                                                      op=ALU.mult), [t])
                acc_free[ab] = t
                t = P.op("dve", "tensor_tensor", dict(
                    out=yb[ab][:, 0:bw], in0=yf[ab][:, 0:bw], in1=zblk[bb][:, hd, 0:bw], op=ALU.mult),
                    [t, tz, yb_free[ab]])
                rs_free[ab] = t
                td = P.dma("pool", ys[ab], YT_d[kvh * 4 + hd, :, c0:c0 + bw], yb[ab][:, 0:bw], [t])
                yb_free[ab] = td
                lastuse = [tpv, t]
            qblk_free[bb] = lastuse
            zblk_free[bb] = lastuse
        kv_free = lastuse
    p3a_done = P.full_barrier()
    A.release(m4)

    m5 = A.mark()
    qw = [A.alloc([8, 128], BF16) for _ in range(2)]
    zw = [A.alloc([8, 128], BF16) for _ in range(2)]
    pT5 = [A.alloc([5, 512], BF16) for _ in range(2)]
    su = [A.alloc([512], F32) for _ in range(2)]
    yf = [A.alloc([512], F32) for _ in range(2)]
    yb = [A.alloc([4, 128], BF16) for _ in range(2)]
    qs = [P.pslot("qb0"), P.pslot("qb1")]
    zs = [P.pslot("zb0"), P.pslot("zb1")]
    ys = [P.pslot("y0"), P.pslot("y1")]
    S5 = PS.f32(0, 5)
    Ob = PS.f32(5)
    Ub = PS.f32(6)
    S_free = None
    acc_free = None
    pT_free = [None, None]
    su_free = [None, None]
    yb_free = [None, None]
    qw_free = [None, None]
    it = 0
    for o in range(NQ):
        bb = o % 2
        so = slotof(o)
        tq = P.dma("sp", qs[bb], qw[bb], QT_d[8:16, :, so * 128:(so + 1) * 128].rearrange("h d t -> d h t"),
                   [qw_free[bb], p3a_done if o < 2 else None])
        tz = P.dma("sp", zs[bb], zw[bb], SZT_d[8:16, :, so * 128:(so + 1) * 128].rearrange("h d t -> d h t"),
                   [qw_free[bb], p3a_done if o < 2 else None])
        lastuse = None
        for kvh in range(2):
            ab = it % 2
            it += 1
            qrhs = qw[bb][:, kvh * 4:(kvh + 1) * 4, :]
            mm = None
            for j in range(5):
                if j < 3:
                    lhs = KBT[:, kvh, (o + j) * 128:(o + j + 1) * 128]
                else:
                    lhs = KBTc[:, kvh, (j - 3) * 128:(j - 2) * 128]
                mm = P.op("pe", "matmul", dict(
                    out=S5[:, j * 512:(j + 1) * 512], lhsT=lhs, rhs=qrhs, start=True, stop=True),
                    [S_free, tq] if j == 0 else [])
            p5 = pT5[ab]
            tex = P.op("act", "activation", dict(out=p5.rearrange("p a b -> p (a b)"), in_=S5,
                                                              func=AF.Exp), [mm, pT_free[ab]])
            S_free = tex
            mlo = masks[:, 2 if o <= 1 else 0, :].unsqueeze(1).broadcast_to([128, 4, 128])
            mhi = masks[:, 3 if o >= NQ - 2 else 1, :].unsqueeze(1).broadcast_to([128, 4, 128])
            v0 = p5[:, 0, :].rearrange("p (h t) -> p h t", t=128)
            v2 = p5[:, 2, :].rearrange("p (h t) -> p h t", t=128)
            tm = P.op("dve", "tensor_tensor", dict(out=v0, in0=v0, in1=mlo, op=ALU.mult), [tex])
            tm = P.op("dve", "tensor_tensor", dict(out=v2, in0=v2, in1=mhi, op=ALU.mult), [tm])
            tpv = None
            for j in range(5):
                if j < 3:
                    lhs = VB[:, o + j, kvh, :]
                else:
                    lhs = VBc[:, j - 3, kvh, :]
                P.op("pe", "matmul", dict(
                    out=Ob, lhsT=lhs, rhs=p5[:, j, :], start=(j == 0), stop=(j == 4)),
                    [tm, acc_free] if j == 0 else [])
            for j in range(5):
                tpv = P.op("pe", "matmul", dict(
                    out=Ub, lhsT=C.onesb, rhs=p5[:, j, :], start=(j == 0), stop=(j == 4)), [])
            pT_free[ab] = tpv
            es = esink[:, kvh * 4:(kvh + 1) * 4].unsqueeze(2).broadcast_to([128, 4, 128])
            t = P.op("dve", "tensor_tensor", dict(
                out=su[ab].rearrange("p (h t) -> p h t", t=128), in0=Ub.rearrange("p (h t) -> p h t", t=128),
                in1=es, op=ALU.add), [tpv, su_free[ab]])
            t = P.op("dve", "reciprocal", dict(out=su[ab], in_=su[ab]), [t])
            t = P.op("dve", "tensor_tensor", dict(out=yf[ab], in0=Ob, in1=su[ab], op=ALU.mult), [t])
            acc_free = t
            t = P.op("dve", "tensor_tensor", dict(
                out=yb[ab].rearrange("p h t -> p (h t)"), in0=yf[ab],
                in1=zw[bb][:, kvh * 4:(kvh + 1) * 4, :].rearrange("p h t -> p (h t)"), op=ALU.mult),
                [t, tz, yb_free[ab]])
            su_free[ab] = t
            td = P.dma("pool", ys[ab], YT_d[8 + kvh * 4:8 + (kvh + 1) * 4, :, so * 128:(so + 1) * 128]
                       .rearrange("h d t -> d h t"), yb[ab], [t])
            yb_free[ab] = td
            lastuse = [tpv, t]
        qw_free[bb] = lastuse
    p3b_done = P.full_barrier()
    A.release(m5)

    qs_of_slot = [q for s_ in range(NQ) for q in range(NQ) if slotof(q) == s_]
    out_proj_phase(P, A, PS, C, w_out, YT_d, G_d,
                   lambda s_: xo[(qs_of_slot[s_] + 1) * 128:(qs_of_slot[s_] + 2) * 128, :],
                   lambda s_: X1_d[qs_of_slot[s_] * 128:(qs_of_slot[s_] + 1) * 128, :], p3b_done, NQ)
    done = P.full_barrier()
    A.release(mall)
    return done


def gate_phase(P, A, PS, C, w_dram, col0, A_fm, Brep, XNT_d, OUT_d, MUL_d, deps, ntiles):
    m3 = A.mark()
    WZ = A.alloc([16, 2048], BF16)
    bZ = A.alloc([2048], F32)
    bz_fm = A.alloc([16], F32)
    twz = prep_weight(P, A, PS, C, w_dram, col0, 2048, A_fm, Brep, WZ, bZ, deps, psum_bank0=4)
    tbz = diag_extract(P, A, C, bZ, bz_fm, twz)
    blk = [A.alloc([16, 512], BF16) for _ in range(2)]
    bslots = [P.pslot("blk0"), P.pslot("blk1")]
    szb = [A.alloc([512], BF16) for _ in range(4)]
    sslots = [P.pslot("sz%d" % i) for i in range(4)]
    blk_free = [None, None]
    szb_free = [None] * 4
    pz = [PS.f32(0), PS.f32(1), PS.f32(2), PS.f32(3)]
    pz_free = [None] * 4
    if MUL_d is not None:
        mblk = [A.alloc([16, 512], BF16) for _ in range(2)]
        mslots = [P.pslot("mb0"), P.pslot("mb1")]
    k = 0
    for tb, (c0, bw) in enumerate(blocks_of(ntiles)):
        b = tb % 2
        tl = None
        for j in range(bw // 128):
            tl = P.dma("sp", bslots[b], blk[b][:, :, j * 128:(j + 1) * 128],
                       XNT_d[tb * 4 + j].rearrange("p (c t) -> p c t", t=128),
                       [blk_free[b], [twz, tbz] if tb < 2 else None])
        tmul = None
        if MUL_d is not None:
            tmul = P.dma("sp", mslots[b], mblk[b][:, :, 0:bw], MUL_d[:, :, c0:c0 + bw].rearrange("c d t -> d c t"),
                         [blk_free[b], [twz, tbz] if tb < 2 else None])
        lastmm = None
        lastrd = None
        for f in range(16):
            pb = k % 4
            mm = None
            for c in range(16):
                mm = P.op("pe", "matmul", dict(
                    out=pz[pb][:, 0:bw], lhsT=WZ[:, c, f * 128:(f + 1) * 128], rhs=blk[b][:, c, 0:bw],
                    start=(c == 0), stop=(c == 15)), [tl, pz_free[pb], twz] if c == 0 else [])
            ta = P.op("act", "activation", dict(
                out=szb[pb][:, 0:bw], in_=pz[pb][:, 0:bw], func=AF.Silu, bias=bz_fm[:, f:f + 1]),
                [mm, szb_free[pb], tbz])
            pz_free[pb] = ta
            if MUL_d is not None:
                ta = P.op("dve", "tensor_tensor", dict(out=szb[pb][:, 0:bw], in0=szb[pb][:, 0:bw],
                                                       in1=mblk[b][:, f, 0:bw], op=ALU.mult), [ta, tmul])
                lastrd = ta
            td = P.dma("pool", sslots[pb], OUT_d[f, :, c0:c0 + bw], szb[pb][:, 0:bw], [ta])
            szb_free[pb] = td
            lastmm = mm
            k += 1
        blk_free[b] = [lastmm, lastrd]
    done = P.full_barrier()
    A.release(m3)
    return done


def out_proj_phase(P, A, PS, C, w_out, YT_d, G_d, xsrc, xdst, deps, ntiles):
    m = A.mark()
    WO = A.alloc([16, 2048], BF16)
    G = A.alloc([2048], F32)
    gs = P.pslot("adaln")
    tg = P.dma("pool", gs, G, G_d, [deps])
    two = prep_weight(P, A, PS, C, w_out, 0, 2048, None, None, WO, None, deps)
    yblk = [A.alloc([16, 512], BF16) for _ in range(2)]
    xt = [A.alloc([2048], F32) for _ in range(2)]
    xo_ = [A.alloc([2048], F32) for _ in range(2)]
    tmpf = A.alloc([2048], F32)
    junk = A.alloc([2048], BF16)
    ss = A.alloc([8], F32)
    bsl = [P.pslot("blk0"), P.pslot("blk1")]
    xsl = [P.pslot("fe0"), P.pslot("fe1")]
    osl = [P.pslot("y0"), P.pslot("y1")]
    po = [PS.f32(0, 4), PS.f32(4, 4)]
    po_free = [None, None]
    yblk_free = [None, None]
    xt_free = [None, None]
    xo_free = [None, None]
    outs = []
    tl = None
    blks = blocks_of(ntiles)
    for o in range(ntiles):
        b = o % 2
        tb, j = divmod(o, 4)
        bb = tb % 2
        c0, bw = blks[tb]
        k0 = float(o)
        P.set_key(k0)
        if j == 0:
            tl = P.dma("sp", bsl[bb], yblk[bb][:, :, 0:bw], YT_d[:, :, c0:c0 + bw].rearrange("c d t -> d c t"),
                       [yblk_free[bb], deps])
        tx = P.dma("sp", xsl[b], xt[b], xsrc(o), [xt_free[b], deps])
        P.set_key(k0 + 1)
        mm = None
        for sl in range(4):
            for c in range(16):
                mm = P.op("pe", "matmul", dict(
                    out=po[b][:, sl * 512:(sl + 1) * 512], lhsT=yblk[bb][:, c, j * 128:(j + 1) * 128],
                    rhs=WO[:, c, sl * 512:(sl + 1) * 512], start=(c == 0), stop=(c == 15)),
                    [tl, two, po_free[b]] if (c == 0 and sl == 0) else [])
        if j == bw // 128 - 1:
            yblk_free[bb] = mm
        k = o % 8
        P.set_key(k0 + 2)
        tsq = P.op("act", "activation", dict(out=junk, in_=po[b], func=AF.Square,
                                                             accum_out=ss[:, k:k + 1]), [mm])
        P.set_key(k0 + 3)
        t = P.op("dve", "tensor_scalar", dict(out=ss[:, k:k + 1], in0=ss[:, k:k + 1], scalar1=1.0 / D,
                                                         scalar2=EPS, op0=ALU.mult, op1=ALU.add), [tsq])
        P.set_key(k0 + 4)
        t = P.op("act", "activation", dict(out=ss[:, k:k + 1], in_=ss[:, k:k + 1], func=AF.Sqrt), [t])
        P.set_key(k0 + 5)
        t = P.op("dve", "reciprocal", dict(out=ss[:, k:k + 1], in_=ss[:, k:k + 1]), [t])
        t = P.op("dve", "scalar_tensor_tensor", dict(
            out=tmpf, in0=po[b], scalar=ss[:, k:k + 1], in1=G, op0=ALU.mult, op1=ALU.mult), [t, tg])
        po_free[b] = t
        t = P.op("dve", "tensor_tensor", dict(out=xo_[b], in0=tmpf, in1=xt[b], op=ALU.add),
                   [t, tx, xo_free[b]])
        xt_free[b] = t
        td = P.dma("pool", osl[b], xdst(o), xo_[b], [t])
        xo_free[b] = td
        outs.append(td)
    A.release(m)
    return outs[-2:]


POOL_SIZES = (2, 4, 8, 16)


def pool_phase(P, A, PS, C, xe, w_in, pool_w, pool_scale, pm_d, ic_d, A_fm, Brep, XNT_d, MT_d, deps):
    m = A.mark()
    WU = A.alloc([16, 2048], BF16)
    bU = A.alloc([2048], F32)
    twu = prep_weight(P, A, PS, C, w_in, 0, 2048, A_fm, Brep, WU, bU, deps, psum_bank0=2)
    PW = A.alloc([4, 4, 512], BF16)
    pwst = A.alloc([4, 512], F32)
    psc = A.alloc([16], F32)
    PM = A.alloc([36, 128], BF16)
    IC = A.alloc([12, 128], F32)
    sl = P.pslot("adaln")
    tl = None
    for g in range(4):
        tl = P.dma("pool", sl, pwst, pool_w[g].rearrange("(ci p) d -> p ci d", p=128), [twu, tl])
        tl = P.op("dve", "tensor_copy", dict(out=PW[:, g, :, :], in_=pwst), [tl])
    t1 = P.dma("pool", sl, psc, pool_scale.rearrange("(c p) -> p c", p=128), [tl], allow_slow_non_contiguous=True)
    t2 = P.dma("pool", sl, PM, pm_d, [t1])
    t3 = P.dma("pool", sl, IC, ic_d.partition_broadcast(128), [t2])
    t4 = t3
    ready = [twu, t4]
    FE = FrontEnd(P, A, PS, C, pt_bank=0)
    pu = PS.f32(2, 4)
    pp = PS.f32(6).rearrange("p (a b) -> p a b", b=128)
    pmx = PS.f32(7).rearrange("p (a b) -> p a b", b=128)
    ub = [A.alloc([2048], BF16) for _ in range(4)]
    pl = [A.alloc([4, 128], BF16) for _ in range(2)]
    mt = [A.alloc([16, 128], BF16) for _ in range(2)]
    xslots = [P.pslot("xn0"), P.pslot("xn1")]
    mslots = [P.pslot("q0"), P.pslot("q1")]
    ub_free = [None] * 4
    ub_ready = [None] * 4
    pu_free = None
    pp_free = None
    pmx_free = None
    pl_free = [None, None]
    mt_free = [None, None]
    gi = 0
    for e_ in range(NTE):
        k0 = float(e_)
        xnT, tready, xt, tx, _ = FE.run(xe[e_ * 128:(e_ + 1) * 128, :], ready if e_ < 4 else None, k0)
        fb = FE.cur
        rd = []
        if 1 <= e_ <= NTO:
            txd = P.dma("pool", xslots[e_ % 2], XNT_d[e_ - 1].rearrange("p (c t) -> p c t", t=128), xnT, [tready])
            rd.append(txd)
        mm = None
        for s4 in range(4):
            for c in range(16):
                mm = P.op("pe", "matmul", dict(
                    out=pu[:, s4 * 512:(s4 + 1) * 512], lhsT=xnT[:, c, :], rhs=WU[:, c, s4 * 512:(s4 + 1) * 512],
                    start=(c == 0), stop=(c == 15)), [tready, pu_free] if (c == 0 and s4 == 0) else [])
        rd.append(mm)
        FE.readers(fb, rd)
        ui = e_ % 4
        P.set_key(k0 + 8)
        tev = P.op("dve", "tensor_tensor", dict(out=ub[ui], in0=pu, in1=bU, op=ALU.add), [mm, ub_free[ui]])
        pu_free = tev
        ub_ready[ui] = tev
        if e_ >= 2:
            o = e_ - 2
            kind = 0 if o == 0 else (2 if o == NTO - 1 else 1)
            mb = o % 2
            lastp = None
            for g in range(4):
                pb = gi % 2
                gi += 1
                P.set_key(k0 + 9 + g)
                for fc in range(4):
                    f = g * 4 + fc
                    for r in range(3):
                        lastp = P.op("pe", "matmul", dict(
                            out=pp[:, fc, :], lhsT=ub[(o + r) % 4][:, f * 128:(f + 1) * 128],
                            rhs=PM[:, (kind * 4 + g) * 3 + r, :], start=(r == 0), stop=(r == 2)),
                            [ub_ready[(o + r) % 4], pp_free] if fc == 0 else [])
                icb = IC[:, kind * 4 + g, :].unsqueeze(1).broadcast_to([128, 4, 128])
                P.bump()
                tpl = P.op("dve", "tensor_tensor", dict(out=pl[pb], in0=pp, in1=icb, op=ALU.mult),
                           [lastp, pl_free[pb]])
                P.bump()
                pp_free = tpl
                lm = None
                for fo in range(4):
                    for ci in range(4):
                        lm = P.op("pe", "matmul", dict(
                            out=pmx[:, fo, :], lhsT=PW[:, g, ci, fo * 128:(fo + 1) * 128], rhs=pl[pb][:, ci, :],
                            start=(ci == 0), stop=(ci == 3)), [tpl, pmx_free] if (fo == 0 and ci == 0) else [])
                pl_free[pb] = lm
                pscb = psc[:, g * 4:(g + 1) * 4].unsqueeze(2).broadcast_to([128, 4, 128])
                P.bump()
                tmx = P.op("dve", "tensor_tensor", dict(out=mt[mb][:, g * 4:(g + 1) * 4, :], in0=pmx, in1=pscb,
                                                        op=ALU.mult), [lm, mt_free[mb] if g == 0 else None])
                pmx_free = tmx
            ub_free[o % 4] = lastp
            tmd = P.dma("pool", mslots[mb], MT_d[:, :, o * 128:(o + 1) * 128].rearrange("c d t -> d c t"), mt[mb],
                        [tmx])
            mt_free[mb] = tmd
    done = P.full_barrier()
    A.release(m)
    return done


def emit_l1(nc, P, A, PS, C, din, dscr, X1_d, deps):
    cvec = din("cvec1", [1, D])
    mod_w = din("mod_w1", [D, 3 * D])
    mod_b = din("mod_b1", [3 * D])
    pre_g = din("pre_g1", [D])
    post_g = din("post_g1", [D])
    w_in = din("w_in1", [D, 4096])
    pool_w = din("pool_w", [4, 512, 512])
    pool_scale = din("pool_scale", [D])
    w_out = din("w_out1", [D, D])
    pm_d = din("pm", [128, 36, 128], BF16)
    ic_d = din("ic", [12 * 128])
    out = nc.dram_tensor("out", [OWN, D], F32, kind="ExternalOutput").ap()
    G_d = dscr("G1_d", [128, D], F32)
    XNT_d = dscr("XNT1_d", [NTO, 128, D])
    MT_d = dscr("MT_d", [16, 128, OWN])
    YT_d = dscr("YT1_d", [16, 128, OWN])
    fms, t_ad = adaln_vectors(P, A, PS, C, cvec, 1, mod_w, mod_b, pre_g, post_g, G_d, deps)
    (A_fm, B_fm), = fms
    Brep = A.alloc([16, 128], F32)
    rep16(P, C, B_fm, Brep, t_ad)
    p0_done = P.full_barrier()
    pa_done = pool_phase(P, A, PS, C, X1_d, w_in, pool_w, pool_scale, pm_d, ic_d, A_fm, Brep, XNT_d, MT_d, p0_done)
    pb_done = gate_phase(P, A, PS, C, w_in, 2048, A_fm, Brep, XNT_d, YT_d, MT_d, pa_done, NTO)
    out_proj_phase(P, A, PS, C, w_out, YT_d, G_d, lambda o: X1_d[(o + 1) * 128:(o + 2) * 128, :],
                   lambda o: out[o * 128:(o + 1) * 128, :], pb_done, NTO)
    return P.full_barrier()


def build_fused(debug=False):
    nc = bass.Bass("TRN2", target_bir_lowering=False)

    def din(name, shape, dt=F32):
        return nc.dram_tensor(name, list(shape), dt, kind="ExternalInput").ap()

    def dscr(name, shape, dt=BF16):
        return nc.dram_tensor(name, list(shape), dt, kind=("ExternalOutput" if debug else "Internal")).ap()

    P = Prog(nc)
    A = Arena(nc, 206 * 1024)
    PS = Psum(nc)
    C = Ctx()
    X1_d = dscr("X1_d", [NQ * 128, D], F32)
    ident_d = din("ident", [128, 128])
    C.cons = load_consts(P, A, C, ident_d)
    d0 = emit_l0(nc, P, A, PS, C, din, dscr, X1_d)
    emit_l1(nc, P, A, PS, C, din, dscr, X1_d, d0)
    P.build()
    return nc


def _rope_table(pos):
    pos = np.asarray(pos)
    row = (pos // GRID_W).astype(np.float32)
    col = (pos % GRID_W).astype(np.float32)
    inv = (np.float32(10000.0) ** (-np.arange(32, dtype=np.float32) / np.float32(32))).astype(np.float32)
    ang = np.concatenate([row[:, None] * inv, col[:, None] * inv], axis=-1).astype(np.float32)
    return np.concatenate([np.cos(ang), np.sin(ang)], axis=-1).astype(np.float32)


def _ext_rows(xb_, j, halo=128):
    out = np.zeros((OWN + 2 * halo,) + xb_.shape[1:], dtype=xb_.dtype)
    lo = j * OWN - halo
    hi = (j + 1) * OWN + halo
    slo, shi = max(lo, 0), min(hi, xb_.shape[0])
    out[slo - lo:shi - lo] = xb_[slo:shi]
    return out


def _masks(j):
    kl = np.arange(128)[:, None]
    ql = np.arange(128)[None, :]
    lo = (kl >= ql).astype(np.float32)
    hi = (kl <= ql).astype(np.float32)
    m = np.stack([lo, hi, lo * (1.0 if j > 0 else 0.0), hi * (1.0 if j < 3 else 0.0)], axis=1)
    return np.ascontiguousarray(m.astype(np.float32))


_NC_CACHE = {}


def _pool_tables(j):
    pm = np.zeros((128, 36, 128), np.float32)
    ic = np.zeros((12, 128), np.float32)
    bases = [j * OWN, 5 * 128, (j + 1) * OWN - 128]
    for kind, base in enumerate(bases):
        for g, w in enumerate(POOL_SIZES):
            half = w // 2
            t = base + np.arange(128)
            lo = np.clip(t - half, 0, SEQ)
            hi = np.clip(t + half, 0, SEQ)
            cnt = (hi - lo).astype(np.float32)
            ic[kind * 4 + g] = 1.0 / cnt
            for r in range(3):
                sidx = base + (r - 1) * 128 + np.arange(128)
                mtx = ((sidx[:, None] >= lo[None, :]) & (sidx[:, None] < hi[None, :])).astype(np.float32)
                mtx -= (sidx[:, None] == t[None, :]).astype(np.float32) * cnt[None, :]
                pm[:, (kind * 4 + g) * 3 + r, :] = mtx
    return pm, ic.reshape(-1)


def _f32(a):
    return np.ascontiguousarray(np.asarray(a, dtype=np.float32))


def run_fused(inp, debug=False):
    key = "fd" if debug else "f"
    if key not in _NC_CACHE:
        _NC_CACHE[key] = build_fused(debug=debug)
    nc = _NC_CACHE[key]
    x = _f32(inp["x"])
    ropeb = _rope_table(np.arange(SEQ))
    ident = np.eye(128, dtype=np.float32)
    shared = {
        "mod_w": _f32(inp["ev_mod_w"][0]), "mod_b": _f32(inp["ev_mod_b"][0]),
        "pre_g": _f32(inp["ev_pre_g"][0]), "post_g": _f32(inp["ev_post_g"][0]),
        "w_in": _f32(inp["ev_w_in"][0]), "q_norm": _f32(inp["ev_q_norm"][0]), "k_norm": _f32(inp["ev_k_norm"][0]),
        "sink": _f32(inp["ev_sink"][0]), "w_out": _f32(inp["ev_w_out"][0]),
        "mod_w1": _f32(inp["od_mod_w"][0]), "mod_b1": _f32(inp["od_mod_b"][0]),
        "pre_g1": _f32(inp["od_pre_g"][0]), "post_g1": _f32(inp["od_post_g"][0]),
        "w_in1": _f32(inp["od_w_in"][0]), "pool_w": _f32(inp["od_pool_w"][0]),
        "pool_scale": _f32(inp["od_pool_scale"][0]), "w_out1": _f32(inp["od_w_out"][0]),
        "ropeb": ropeb, "ident": ident,
    }
    c = _f32(inp["c"])
    c_ctx = _f32(inp["c_ctx"])
    ctx = _f32(inp["ctx"])
    in_maps = []
    for core in range(8):
        b, j = divmod(core, 4)
        pos = np.clip(np.arange(j * OWN - 256, (j + 1) * OWN + 256), 0, SEQ - 1)
        pm, ic = _pool_tables(j)
        m = dict(shared)
        m.update({
            "xb": x[b], "xo": _ext_rows(x[b], j, halo=256), "ctx": ctx[b],
            "cvec": np.ascontiguousarray(np.stack([c[b], c_ctx])), "cvec1": np.ascontiguousarray(c[b][None]),
            "ropeo": _rope_table(pos), "masks": _masks(j),
            "pm": pm.astype(ml_dtypes.bfloat16), "ic": ic,
        })
        in_maps.append(m)
    res = run_bass_kernel_spmd(nc, in_maps, core_ids=list(range(8)))
    out = np.empty_like(x)
    for core in range(8):
        b, j = divmod(core, 4)
        out[b, j * OWN:(j + 1) * OWN] = res.results[core]["out"]
    if debug:
        return out, res.results
    return out


def kernel(**inputs):
    return run_fused(inputs)
```

```python
import numpy as np
import ml_dtypes
import concourse.bass as bass
import concourse.mybir as mybir
from concourse.bass_utils import run_bass_kernel_spmd

F32 = mybir.dt.float32
BF16 = mybir.dt.bfloat16
AF = mybir.ActivationFunctionType
ALU = mybir.AluOpType
AX = mybir.AxisListType

D = 2048
SEQ = 16384
NBATCH = 2
CTX = 256
HD = 128
OWN = 4096
NTO = OWN // 128
NTE = NTO + 2
NTB = SEQ // 128
NKC = NTB + CTX // 128
NQ = NTO + 2
NKB = NQ + 2
QCOLS = NQ * 128


def slotof(q):
    return q - 1 if 1 <= q <= NTO else (NTO if q == 0 else NTO + 1)


def blocks_of(ntiles):
    out = []
    c = 0
    while c < ntiles * 128:
        w = min(512, ntiles * 128 - c)
        out.append((c, w))
        c += w
    return out
EPS = 1e-6
ATTN_SCALE = HD ** -0.5
GRID_W = 64
ENGS = ("pe", "act", "dve", "pool", "sp")


class Op:
    __slots__ = ("kind", "eng", "fn", "waits", "key", "seq", "slot", "sem", "val", "snap")

    def __init__(self, kind, eng, fn, waits, key, seq, slot=None):
        self.kind, self.eng, self.fn, self.waits, self.key, self.seq, self.slot = kind, eng, fn, waits, key, seq, slot
        self.sem = None
        self.val = None
        self.snap = None


class DmaSlot:
    def __init__(self, prog, name):
        self.sem = prog.nc.alloc_semaphore(name)
        self.count = 0
        self.eng = None


class Prog:
    def __init__(self, nc):
        self.nc = nc
        self.recs = []
        self.sems = {e: nc.alloc_semaphore("s_" + e) for e in ENGS}
        self.last_op = {e: None for e in ENGS}
        self.nslot = 0
        self.slots = []
        self.named = {}
        self.base = 0.0
        self.key = 0.0
        self.seq = 0

    def set_key(self, k):
        self.key = self.base + float(k)

    def bump(self, d=1.0):
        self.key += d

    def slot(self, name=None):
        self.nslot += 1
        sl = DmaSlot(self, name or ("dslot%d" % self.nslot))
        self.slots.append(sl)
        return sl

    def pslot(self, name):
        if name not in self.named:
            self.named[name] = self.slot(name)
        return self.named[name]

    def _new(self, kind, eng, fn, waits, slot=None):
        waits = _flat(waits)
        k = self.key
        for w in waits:
            if w.key > k:
                k = w.key
        self.key = k
        op = Op(kind, eng, fn, waits, k, self.seq, slot)
        self.seq += 1
        self.recs.append(op)
        return op

    def emit(self, eng, fn, waits=()):
        op = self._new("c", eng, fn, waits)
        self.last_op[eng] = op
        return op

    def op(self, eng, method, kwargs, waits=()):
        kw = dict(kwargs)
        return self.emit(eng, lambda e, m=method, k=kw: getattr(e, m)(**k), waits)

    def dma(self, eng, slot, out, in_, waits=(), **kw):
        assert slot.eng in (None, eng), "a DMA slot must be used from a single queue"
        slot.eng = eng
        return self._new("d", eng, lambda e, o=out, i=in_, k=kw: e.dma_start(out=o, in_=i, **k), waits, slot)

    def wait_only(self, eng, waits):
        return self._new("w", eng, None, waits)

    def last(self, eng):
        return self.last_op[eng]

    def full_barrier(self):
        self.key = self.base + 900000.0
        self._new("b", None, None, ())
        self.base += 1000000.0
        self.key = self.base
        return []

    def build(self):
        nc = self.nc
        order = sorted(self.recs, key=lambda r: (r.key, r.seq))
        cnt = {e: 0 for e in ENGS}
        for sl in self.slots:
            sl.count = 0
        per = {e: [] for e in ENGS}
        for r in order:
            if r.kind == "c":
                cnt[r.eng] += 1
                r.sem, r.val = self.sems[r.eng], cnt[r.eng]
                per[r.eng].append(r)
            elif r.kind == "d":
                r.slot.count += 16
                r.sem, r.val = r.slot.sem, r.slot.count
                per[r.eng].append(r)
            elif r.kind == "w":
                per[r.eng].append(r)
            else:
                r.snap = [(self.sems[e], cnt[e]) for e in ENGS if cnt[e]] + \
                         [(sl.sem, sl.count) for sl in self.slots if sl.count]
                for e in ENGS:
                    per[e].append(r)
        with nc.Block() as block:
            def make(engname):
                def body(e):
                    seen = {}

                    def wait(sem, val):
                        key = id(sem)
                        if seen.get(key, 0) >= val:
                            return
                        e.wait_ge(sem, val)
                        seen[key] = val
                    for r in per[engname]:
                        if r.kind == "b":
                            for sem, val in r.snap:
                                wait(sem, val)
                            continue
                        for w in r.waits:
                            wait(w.sem, w.val)
                        if r.fn is not None:
                            inst = r.fn(e)
                            inst.then_inc(r.sem, 1 if r.kind == "c" else 16)
                return body
            block.tensor(make("pe"))
            block.scalar(make("act"))
            block.vector(make("dve"))
            block.gpsimd(make("pool"))
            block.sync(make("sp"))


def _flat(waits):
    out = []
    for w in waits:
        if w is None:
            continue
        if isinstance(w, (list, tuple)):
            out.extend(_flat(w))
        else:
            out.append(w)
    return tuple(out)


class Arena:
    def __init__(self, nc, nbytes):
        self.t = nc.alloc_sbuf_tensor("arena", [128, nbytes // 4], F32).ap()
        self.nbytes = nbytes
        self.off = 0

    def alloc(self, free_shape, dtype):
        n = int(np.prod(free_shape))
        esz = 2 if dtype == BF16 else 4
        nb = (n * esz + 31) // 32 * 32
        assert self.off + nb <= self.nbytes, ("SBUF arena overflow", self.off, nb, self.nbytes)
        ap = self.t[:, self.off // 4:(self.off + nb) // 4]
        self.off += nb
        if dtype == BF16:
            ap = ap.bitcast(BF16)
        ap = ap[:, 0:n]
        if len(free_shape) == 2:
            ap = ap.rearrange("p (a b) -> p a b", b=free_shape[1])
        elif len(free_shape) == 3:
            ap = ap.rearrange("p (a b c) -> p a b c", b=free_shape[1], c=free_shape[2])
        return ap

    def mark(self):
        return self.off

    def release(self, m):
        self.off = m


class Psum:
    def __init__(self, nc):
        self.t = nc.alloc_psum_tensor("psum_all", [128, 8, 512], F32).ap()

    def f32(self, bank, nbanks=1):
        ap = self.t[:, bank:bank + nbanks, :]
        return ap.rearrange("p a b -> p (a b)")

    def bf16(self, bank, nbanks, inner):
        ap = self.t[:, bank:bank + nbanks, :].rearrange("p a b -> p (a b)").bitcast(BF16)
        return ap.rearrange("p (c t) -> p c t", t=inner)


class Ctx:
    pass


def load_consts(P, A, C, ident_d):
    C.slot_c = P.slot("c_const")
    C.identf = A.alloc([128], F32)
    C.identb = A.alloc([128], BF16)
    C.onesf = A.alloc([128], F32)
    C.onesb = A.alloc([128], BF16)
    t = P.dma("sp", C.slot_c, C.identf, ident_d)
    t1 = P.op("dve", "tensor_copy", dict(out=C.identb, in_=C.identf), [t])
    t2 = P.op("dve", "memset", dict(ap=C.onesf, constant=1.0))
    t3 = P.op("dve", "memset", dict(ap=C.onesb, constant=1.0))
    return [t1, t2, t3]


def modulation(P, A, PS, C, cvec_d, ncv, mod_w, mod_b, res, deps):
    m1 = A.mark()
    cfm = A.alloc([ncv, 16], F32)
    screp = A.alloc([ncv, 16, 128], F32)
    sl = P.pslot("mod_c")
    tl = None
    for v in range(ncv):
        tl = P.dma("sp", sl, cfm[:, v, :], cvec_d[v].rearrange("(c p) -> p c", p=128),
                   allow_slow_non_contiguous=True)
    ts = P.op("act", "activation", dict(out=cfm, in_=cfm, func=AF.Silu), [tl, deps])
    tr = None
    for v in range(ncv):
        for c in range(16):
            tr = P.op("dve", "tensor_scalar", dict(
                out=screp[:, v, c, :], in0=C.onesf, scalar1=cfm[:, v, c:c + 1], scalar2=None, op0=ALU.mult),
                [ts, deps])
    allg = sorted(set(g for (v, g) in res))
    wt = [A.alloc([2048], F32) for _ in range(3)]
    brow = A.alloc([2048], F32)
    wslots = [P.pslot("pw%d" % i) for i in range(3)]
    bslot = P.pslot("mod_b")
    wfree = [deps, deps, deps]
    k = 0
    ev_prev = None
    for g in allg:
        vs = [v for v in range(ncv) if (v, g) in res]
        tb = P.dma("pool", bslot, brow, mod_b[g * 2048:(g + 1) * 2048].partition_broadcast(128), [ev_prev])
        last_mm = None
        for c in range(16):
            s = k % 3
            tw = P.dma("sp", wslots[s], wt[s], mod_w[c * 128:(c + 1) * 128, g * 2048:(g + 1) * 2048], [wfree[s]])
            for vi, v in enumerate(vs):
                for nb in range(4):
                    last_mm = P.op("pe", "matmul", dict(
                        out=PS.f32(4 * vi + nb), lhsT=screp[:, v, c, :], rhs=wt[s][:, nb * 512:(nb + 1) * 512],
                        start=(c == 0), stop=(c == 15)), [tw, tr, ev_prev if c == 0 else None])
            wfree[s] = last_mm
            k += 1
        for vi, v in enumerate(vs):
            ev_prev = P.op("dve", "tensor_tensor", dict(
                out=res[(v, g)], in0=PS.f32(4 * vi, 4), in1=brow, op=ALU.add), [last_mm, tb])
    return ev_prev


def diag_extract(P, A, C, row, out_fm, deps):
    m = A.mark()
    tmp = A.alloc([16, 128], F32)
    t = None
    for j in range(16):
        t = P.op("dve", "tensor_tensor", dict(out=tmp[:, j, :], in0=row[:, j * 128:(j + 1) * 128],
                                                         in1=C.identf, op=ALU.mult), [deps])
    t = P.op("dve", "tensor_reduce", dict(out=out_fm, in_=tmp, axis=AX.X, op=ALU.add), [t])
    return t


def prep_weight(P, A, PS, C, w_dram, col0, ncols, a_fm, brep, wdst, bias_row, deps, psum_bank0=0, stage=None):
    m = A.mark()
    if stage is None:
        stage = [A.alloc([512], F32) for _ in range(3)]
    slots = [P.pslot("pw%d" % i) for i in range(3)]
    free = [deps, deps, deps]
    k = 0
    last = None
    nsl = ncols // 512
    assert nsl <= 4 or bias_row is None or True
    for s0 in range(0, nsl, 4):
        grp = list(range(s0, min(nsl, s0 + 4)))
        lastmm = {}
        for c in range(16):
            for sl in grp:
                b = k % 3
                tw = P.dma("sp", slots[b], stage[b],
                           w_dram[c * 128:(c + 1) * 128, col0 + sl * 512:col0 + (sl + 1) * 512], [free[b]])
                rd = []
                if a_fm is not None:
                    t1 = P.op("dve", "tensor_scalar", dict(
                        out=wdst[:, c, sl * 512:(sl + 1) * 512], in0=stage[b], scalar1=a_fm[:, c:c + 1], scalar2=None,
                        op0=ALU.mult), [tw, deps])
                else:
                    t1 = P.op("dve", "tensor_copy", dict(
                        out=wdst[:, c, sl * 512:(sl + 1) * 512], in_=stage[b]), [tw, deps])
                rd.append(t1)
                last = t1
                if bias_row is not None:
                    t2 = P.op("pe", "matmul", dict(
                        out=PS.f32(psum_bank0 + sl - s0), lhsT=brep[:, c, :], rhs=stage[b],
                        start=(c == 0), stop=(c == 15)), [tw, deps])
                    rd.append(t2)
                    lastmm[sl] = t2
                free[b] = rd
                k += 1
        if bias_row is not None:
            for sl in grp:
                last = P.op("act", "activation", dict(
                    out=bias_row[:, sl * 512:(sl + 1) * 512], in_=PS.f32(psum_bank0 + sl - s0), func=AF.Identity),
                    [lastmm[sl]])
            deps = [deps, last]
    return [last, t1]


class FrontEnd:
    def __init__(self, P, A, PS, C, pt_bank, nx=3):
        self.P, self.C, self.PS = P, C, PS
        self.nx = nx
        self.xt = [A.alloc([2048], F32) for _ in range(nx)]
        self.xn = [A.alloc([2048], BF16) for _ in range(2)]
        self.xnT = [A.alloc([16, 128], BF16) for _ in range(2)]
        self.junk = A.alloc([2048], BF16)
        self.ss = A.alloc([8], F32)
        self.rstd = A.alloc([8], F32)
        self.slots = [P.pslot("fe%d" % i) for i in range(nx)]
        self.pt = PS.bf16(pt_bank, 2, 128)
        self.xt_free = [[] for _ in range(nx)]
        self.xn_free = [None, None]
        self.xnT_free = [[], []]
        self.pt_free = None
        self.n = 0

    def run(self, src_ap, deps=None, k0=None):
        P, C = self.P, self.C
        i = self.n
        self.n += 1
        b = i % 2
        bx = i % self.nx
        k = i % 8
        if k0 is None:
            k0 = P.key - P.base
        xt, xn, xnT = self.xt[bx], self.xn[b], self.xnT[b]
        P.set_key(k0)
        tx = P.dma("sp" if bx % 2 == 0 else "act", self.slots[bx], xt, src_ap, [self.xt_free[bx], deps])
        P.set_key(k0 + 1)
        tsq = P.op("act", "activation", dict(out=self.junk, in_=xt, func=AF.Square,
                                                   accum_out=self.ss[:, k:k + 1]), [tx])
        P.set_key(k0 + 2)
        tms = P.op("dve", "tensor_scalar", dict(out=self.rstd[:, k:k + 1], in0=self.ss[:, k:k + 1],
                                                      scalar1=1.0 / D, scalar2=EPS, op0=ALU.mult, op1=ALU.add),
                     [tsq])
        P.set_key(k0 + 3)
        tsr = P.op("act", "activation", dict(out=self.rstd[:, k:k + 1], in_=self.rstd[:, k:k + 1],
                                                   func=AF.Sqrt), [tms])
        P.set_key(k0 + 4)
        trc = P.op("dve", "reciprocal", dict(out=self.rstd[:, k:k + 1], in_=self.rstd[:, k:k + 1]), [tsr])
        txn = P.op("dve", "tensor_scalar", dict(out=xn, in0=xt, scalar1=self.rstd[:, k:k + 1], scalar2=None,
                                                      op0=ALU.mult), [trc, self.xn_free[b]])
        P.set_key(k0 + 5)
        tt = None
        for c in range(16):
            tt = P.op("pe", "transpose", dict(out=self.pt[:, c, :], in_=xn[:, c * 128:(c + 1) * 128],
                                                         identity=C.identb),
                        [txn, self.pt_free] if c == 0 else [])
        self.xn_free[b] = tt
        P.set_key(k0 + 6)
        tcp = P.op("act", "activation", dict(out=xnT, in_=self.pt, func=AF.Copy), [tt, self.xnT_free[b]])
        self.pt_free = tcp
        self.xt_free[bx] = [tsq, txn]
        self.xnT_free[b] = []
        self.cur = b
        P.set_key(k0 + 7)
        return xnT, tcp, xt, tx, self.rstd[:, k:k + 1]

    def readers(self, b, toks, x_toks=()):
        self.xnT_free[b] = list(self.xnT_free[b]) + list(_flat(toks))


def head_post(P, A, src, nh, dst_bf, do_norm, gain_row, rope_cs, tmp, deps, scale=None):
    t = deps
    tmp["n"] = tmp.get("n", 0) + 1
    sq, st8 = tmp["sq"][tmp["n"] % len(tmp["sq"])], tmp["st8"][tmp["n"] % len(tmp["st8"])]
    if do_norm:
        t = P.op("dve", "tensor_tensor", dict(out=sq[:, 0:nh, :], in0=src, in1=src, op=ALU.mult), [t])
        t = P.op("dve", "tensor_reduce", dict(out=st8[:, 0:nh], in_=sq[:, 0:nh, :], axis=AX.X, op=ALU.add), [t])
        t = P.op("dve", "tensor_scalar", dict(out=st8[:, 0:nh], in0=st8[:, 0:nh], scalar1=1.0 / HD, scalar2=EPS,
                                                    op0=ALU.mult, op1=ALU.add), [t])
        P.bump()
        t = P.op("act", "activation", dict(out=st8[:, 0:nh], in_=st8[:, 0:nh], func=AF.Sqrt), [t])
        P.bump()
        t = P.op("dve", "reciprocal", dict(out=st8[:, 0:nh], in_=st8[:, 0:nh]), [t])
        t = P.op("dve", "tensor_tensor", dict(out=src, in0=src,
                                                    in1=st8[:, 0:nh].unsqueeze(2).broadcast_to([128, nh, 128]),
                                                    op=ALU.mult), [t])
        t = P.op("dve", "tensor_tensor", dict(out=src, in0=src,
                                                    in1=gain_row.unsqueeze(1).broadcast_to([128, nh, 128]),
                                                    op=ALU.mult), [t])
    elif scale is not None:
        t = P.op("dve", "tensor_scalar", dict(out=src, in0=src, scalar1=float(scale), scalar2=None,
                                                    op0=ALU.mult), [t])
    if rope_cs is not None:
        cosb = rope_cs[:, 0:64].unsqueeze(1).broadcast_to([128, nh, 64])
        sinb = rope_cs[:, 64:128].unsqueeze(1).broadcast_to([128, nh, 64])
        x1 = src[:, :, 0:64]
        x2 = src[:, :, 64:128]
        ta, tb_ = sq[:, 0:nh, 0:64], sq[:, 0:nh, 64:128]
        t = P.op("dve", "tensor_tensor", dict(out=ta, in0=x1, in1=cosb, op=ALU.mult), [t])
        t = P.op("dve", "tensor_tensor", dict(out=tb_, in0=x2, in1=sinb, op=ALU.mult), [t])
        t = P.op("dve", "tensor_tensor", dict(out=dst_bf[:, :, 0:64], in0=ta, in1=tb_, op=ALU.subtract), [t])
        t = P.op("dve", "tensor_tensor", dict(out=ta, in0=x2, in1=cosb, op=ALU.mult), [t])
        t = P.op("dve", "tensor_tensor", dict(out=tb_, in0=x1, in1=sinb, op=ALU.mult), [t])
        t = P.op("dve", "tensor_tensor", dict(out=dst_bf[:, :, 64:128], in0=ta, in1=tb_, op=ALU.add), [t])
    else:
        t = P.op("dve", "tensor_copy", dict(out=dst_bf, in_=src), [t])
    return t


def rep16(P, C, fm, rep, deps):
    t = None
    for c in range(16):
        t = P.op("dve", "tensor_scalar", dict(out=rep[:, c, :], in0=C.onesf, scalar1=fm[:, c:c + 1],
                                                         scalar2=None, op0=ALU.mult), [deps])
    return t


def adaln_vectors(P, A, PS, C, cvec_d, ncv, mod_w, mod_b, pre_g, post_g, G_d, deps):
    fms = [(A.alloc([16], F32), A.alloc([16], F32)) for _ in range(ncv)]
    m = A.mark()
    res = {}
    for v in range(ncv):
        res[(v, 0)] = A.alloc([2048], F32)
        res[(v, 1)] = A.alloc([2048], F32)
    res[(0, 2)] = A.alloc([2048], F32)
    prow = A.alloc([2048], F32)
    sl = P.pslot("adaln")
    tp = P.dma("pool", sl, prow, pre_g.partition_broadcast(128), [deps])
    tm = modulation(P, A, PS, C, cvec_d, ncv, mod_w, mod_b, res, deps)
    last = []
    for v in range(ncv):
        t = P.op("dve", "scalar_tensor_tensor", dict(out=res[(v, 1)], in0=res[(v, 1)], scalar=1.0, in1=prow,
                                                                op0=ALU.add, op1=ALU.mult), [tm, tp])
        t1 = diag_extract(P, A, C, res[(v, 1)], fms[v][0], [t])
        t2 = diag_extract(P, A, C, res[(v, 0)], fms[v][1], [tm])
        last += [t1, t2]
    tp2 = P.dma("pool", sl, prow, post_g.partition_broadcast(128), [last])
    tg = P.op("dve", "tensor_tensor", dict(out=res[(0, 2)], in0=res[(0, 2)], in1=prow, op=ALU.mult), [tm, tp2])
    tgd = P.dma("pool", sl, G_d, res[(0, 2)], [tg])
    last.append(tgd)
    A.release(m)
    return fms, last


def emit_l0(nc, P, A, PS, C, din, dscr, X1_d):
    xb = din("xb", [SEQ, D])
    xo = din("xo", [NKB * 128, D])
    ctx_d = din("ctx", [CTX, D])
    cvec = din("cvec", [2, D])
    mod_w = din("mod_w", [D, 3 * D])
    mod_b = din("mod_b", [3 * D])
    pre_g = din("pre_g", [D])
    post_g = din("post_g", [D])
    w_in = din("w_in", [D, 5120])
    q_norm = din("q_norm", [HD])
    k_norm = din("k_norm", [HD])
    sink = din("sink", [8])
    w_out = din("w_out", [D, D])
    ropeb = din("ropeb", [SEQ, 128])
    ropeo = din("ropeo", [NKB * 128, 128])
    masks_d = din("masks", [128, 4, 128])

    G_d = dscr("G_d", [128, D], F32)
    KAT_d = dscr("KAT_d", [2, 128, NKC * 128])
    VA_d = dscr("VA_d", [NKC * 128, 2, 128])
    QT_d = dscr("QT_d", [16, 128, QCOLS])
    XNT_d = dscr("XNT_d", [NQ, 128, D])
    SZT_d = dscr("SZT_d", [16, 128, QCOLS])
    YT_d = dscr("YT_d", [16, 128, QCOLS])
    mall = A.mark()

    cons = C.cons
    fms, t_ad = adaln_vectors(P, A, PS, C, cvec, 2, mod_w, mod_b, pre_g, post_g, G_d, cons)
    (A_fm, B_fm), (Ac_fm, Bc_fm) = fms
    Brep = A.alloc([16, 128], F32)
    t_brep = rep16(P, C, B_fm, Brep, t_ad)
    qg_row = A.alloc([128], F32)
    kg_row = A.alloc([128], F32)
    masks = A.alloc([4, 128], BF16)
    esink = A.alloc([8], F32)
    KBT = A.alloc([2, NKB * 128], BF16)
    VB = A.alloc([NKB, 2, 128], BF16)
    KBTc = A.alloc([2, CTX], BF16)
    VBc = A.alloc([2, 2, 128], BF16)
    sl0 = P.slot("p0misc")
    mk = A.mark()
    mstage = A.alloc([4, 128], F32)
    t1 = P.dma("pool", sl0, qg_row, q_norm.partition_broadcast(128), [t_ad, t_brep])
    t2 = P.dma("pool", sl0, kg_row, k_norm.partition_broadcast(128), [t_ad, t_brep])
    t3 = P.dma("pool", sl0, esink, sink.partition_broadcast(128), [t_ad, t_brep])
    t4 = P.dma("pool", sl0, mstage, masks_d, [t_ad, t_brep])
    tq = P.op("dve", "tensor_scalar", dict(out=qg_row, in0=qg_row, scalar1=float(ATTN_SCALE), scalar2=None,
                                                 op0=ALU.mult), [t1, t2, t3, t4])
    tmk = P.op("dve", "tensor_copy", dict(out=masks, in_=mstage), [tq])
    tes = P.op("act", "activation", dict(out=esink, in_=esink, func=AF.Exp), [t4])
    A.release(mk)
    p0_done = P.full_barrier()

    mkv = A.mark()
    Brepc = A.alloc([16, 128], F32)
    t_brc = rep16(P, C, Bc_fm, Brepc, p0_done)
    WB = A.alloc([16, 512], BF16)
    WA = A.alloc([16, 512], BF16)
    WC = A.alloc([16, 1024], BF16)
    bB = A.alloc([512], F32)
    bA = A.alloc([512], F32)
    bC = A.alloc([1024], F32)
    pstage = [A.alloc([512], F32) for _ in range(3)]
    tw1 = prep_weight(P, A, PS, C, w_in, 2560, 512, A_fm, Brep, WB, bB, [t_brep, t_brc], psum_bank0=4, stage=pstage)
    tw2 = prep_weight(P, A, PS, C, w_in, 2048, 512, A_fm, Brep, WA, bA, [tw1], psum_bank0=4, stage=pstage)
    tw3 = prep_weight(P, A, PS, C, w_in, 2048, 1024, Ac_fm, Brepc, WC, bC, [tw2], psum_bank0=4, stage=pstage)
    wready = [tw1, tw2, tw3]
    FE = FrontEnd(P, A, PS, C, pt_bank=0, nx=3)
    pkv = PS.f32(2, 2)
    ptk = PS.bf16(4, 1, 128)
    kvf = [A.alloc([1024], F32) for _ in range(2)]
    kbf = [A.alloc([4, 128], BF16) for _ in range(2)]
    ropes = [A.alloc([128], F32) for _ in range(4)]
    rslots = [P.pslot("rope%d" % i) for i in range(4)]
    tmp = {"sq": [A.alloc([4, 128], F32)], "st8": [A.alloc([8], F32) for _ in range(4)]}
    kst = [A.alloc([2, 512], BF16) for _ in range(2)]
    vst = [A.alloc([2, 128], BF16) for _ in range(2)]
    kslots = [P.slot(), P.slot()]
    vslots = [P.slot(), P.slot()]
    st = Ctx()
    st.kvf_free = [None, None]
    st.kbf_free = [None, None]
    st.rope_free = [None] * 4
    st.kst_free = [None, None]
    st.vst_free = [None, None]
    st.pkv_free = None
    st.ptk_free = None
    st.n = 0
    st.out_toks = []

    def kv_vcopy(mode, idx, b, kf, tk):
        if mode == "ext":
            tv = P.op("dve", "tensor_copy", dict(out=VB[:, idx, :, :],
                                                 in_=kf[:, 256:512].rearrange("p (h d) -> p h d", d=128)), [tk])
            st.kvf_free[b] = tv
            return [tv]
        tv = P.op("dve", "tensor_copy", dict(
            out=vst[b], in_=kf[:, 256:512].rearrange("p (h d) -> p h d", d=128)), [tk, st.vst_free[b]])
        st.kvf_free[b] = tv
        if mode == "ctx":
            tv2 = P.op("dve", "tensor_copy", dict(
                out=VBc[:, idx, :, :], in_=kf[:, 768:1024].rearrange("p (h d) -> p h d", d=128)), [tk])
            st.kvf_free[b] = tv2
            return [tv, tv2]
        return [tv]

    def kv_tile(src_ap, rope_ap, W, brow, ncols, mode, idx):
        i = st.n
        st.n += 1
        b = i % 2
        rb = i % 4
        k0 = float(i)
        P.set_key(k0 + 4)
        if rope_ap is not None:
            trope = P.dma("pool", rslots[rb], ropes[rb], rope_ap, [st.rope_free[rb], wready if i < 4 else None])
        else:
            trope = None
        xnT, tready, xt, tx, _ = FE.run(src_ap, wready if i < 4 else None, k0)
        fb = FE.cur
        nsl = ncols // 512
        mm = None
        for sl in range(nsl):
            for c in range(16):
                mm = P.op("pe", "matmul", dict(
                    out=pkv[:, sl * 512:(sl + 1) * 512], lhsT=xnT[:, c, :], rhs=W[:, c, sl * 512:(sl + 1) * 512],
                    start=(c == 0), stop=(c == 15)), [tready, st.pkv_free] if (c == 0 and sl == 0) else [])
        FE.readers(fb, [mm])
        kf = kvf[b]
        P.set_key(k0 + 8)
        tev = P.op("dve", "tensor_tensor", dict(out=kf[:, 0:ncols], in0=pkv[:, 0:ncols], in1=brow[:, 0:ncols],
                                                      op=ALU.add), [mm, st.kvf_free[b]])
        st.pkv_free = tev
        kb = kbf[b]
        toks = []
        if mode == "bat":
            ksrc = kf[:, 0:256].rearrange("p (h d) -> p h d", d=128)
            tk = head_post(P, A, ksrc, 2, kb[:, 0:2, :], True, kg_row, ropes[rb], tmp, [tev, trope, st.kbf_free[b]])
            st.rope_free[rb] = tk
            nk = 2
        elif mode == "ext":
            ksrc = kf[:, 0:256].rearrange("p (h d) -> p h d", d=128)
            tk = head_post(P, A, ksrc, 2, kb[:, 0:2, :], False, None, ropes[rb], tmp, [tev, trope, st.kbf_free[b]])
            st.rope_free[rb] = tk
            nk = 2
        else:
            ksrc = kf[:, 0:256].rearrange("p (h d) -> p h d", d=128)
            tk = head_post(P, A, ksrc, 2, kb[:, 0:2, :], True, kg_row, None, tmp, [tev, st.kbf_free[b]])
            ksrc2 = kf[:, 512:768].rearrange("p (h d) -> p h d", d=128)
            tk = head_post(P, A, ksrc2, 2, kb[:, 2:4, :], False, None, None, tmp, [tk])
            nk = 4
        P.set_key(k0 + 8)
        vtoks = kv_vcopy(mode, idx, b, kf, tk)
        P.set_key(k0 + 11)
        tt = None
        for h in range(nk):
            tt = P.op("pe", "transpose", dict(out=ptk[:, h, :], in_=kb[:, h, :], identity=C.identb),
                        [tk, st.ptk_free] if h == 0 else [])
        st.kbf_free[b] = tt
        P.set_key(k0 + 12)
        if mode == "ext":
            tc = P.op("act", "activation", dict(out=KBT[:, :, idx * 128:(idx + 1) * 128], in_=ptk[:, 0:2, :],
                                                      func=AF.Copy), [tt])
            st.ptk_free = tc
            toks += [tc]
        else:
            grp = 4 if mode == "bat" else 2
            g, r = divmod(idx, grp)
            sb = g % 2
            tc = P.op("act", "activation", dict(out=kst[sb][:, :, r * 128:(r + 1) * 128], in_=ptk[:, 0:2, :],
                                                      func=AF.Copy), [tt, st.kst_free[sb] if r == 0 else None])
            if mode == "ctx":
                tc2 = P.op("act", "activation", dict(out=KBTc[:, :, idx * 128:(idx + 1) * 128],
                                                           in_=ptk[:, 2:4, :], func=AF.Copy), [tt])
                tc = tc2
            st.ptk_free = tc
            base = (0 if mode == "bat" else SEQ)
            tok0 = base + idx * 128
            tvd = P.dma("pool", vslots[b], VA_d[tok0:tok0 + 128, :, :], vst[b], [vtoks])
            st.vst_free[b] = tvd
            toks.append(tvd)
            if r == grp - 1:
                c0 = base + g * grp * 128
                tkd = P.dma("pool", kslots[sb], KAT_d[:, :, c0:c0 + grp * 128].rearrange("h d t -> d h t"),
                            kst[sb][:, :, 0:grp * 128], [tc])
                st.kst_free[sb] = tkd
                toks.append(tkd)
        st.out_toks = [st.out_toks[-8:], toks]
        st.all_toks.extend(toks)

    st.all_toks = []
    for e_ in range(NKB):
        kv_tile(xo[e_ * 128:(e_ + 1) * 128, :], ropeo[e_ * 128:(e_ + 1) * 128, :], WB, bB, 512, "ext", e_)
    for t_ in range(CTX // 128):
        kv_tile(ctx_d[t_ * 128:(t_ + 1) * 128, :], None, WC, bC, 1024, "ctx", t_)
    for t_ in range(NTB):
        kv_tile(xb[t_ * 128:(t_ + 1) * 128, :], ropeb[t_ * 128:(t_ + 1) * 128, :], WA, bA, 512, "bat", t_)
    pkv_done = P.full_barrier()
    A.release(mkv)

    m2 = A.mark()
    WQ = A.alloc([16, 2048], BF16)
    bQ = A.alloc([2048], F32)
    FE = FrontEnd(P, A, PS, C, pt_bank=0, nx=3)
    twq = prep_weight(P, A, PS, C, w_in, 0, 2048, A_fm, Brep, WQ, bQ, pkv_done, psum_bank0=2,
                      stage=[FE.xt[i][:, 0:512] for i in range(3)])
    pq = [PS.f32(2, 2), PS.f32(4, 2)]
    ptq = PS.bf16(6, 1, 128)
    qf = [A.alloc([8, 128], F32) for _ in range(4)]
    qbf = [A.alloc([16, 128], BF16) for _ in range(2)]
    qT = [A.alloc([16, 128], BF16) for _ in range(2)]
    ropes = [A.alloc([128], F32) for _ in range(4)]
    rslots = [P.pslot("rope%d" % i) for i in range(4)]
    qslots = [P.pslot("q0"), P.pslot("q1")]
    xslots = [P.pslot("xn0"), P.pslot("xn1")]
    tmp = {"sq": [A.alloc([8, 128], F32)], "st8": [A.alloc([8], F32) for _ in range(4)]}
    pq_free = [None, None]
    qf_free = [None] * 4
    qbf_free = [None, None]
    qT_free = [None, None]
    rope_free = [None] * 4
    ptq_free = None
    for o in range(NQ):
        b = o % 2
        e_ = o + 1
        so = slotof(o)
        k0 = float(o)
        rb = o % 4
        P.set_key(k0 + 4)
        trope = P.dma("pool", rslots[rb], ropes[rb], ropeo[e_ * 128:(e_ + 1) * 128, :],
                      [rope_free[rb], twq if o < 4 else None])
        xnT, tready, xt, tx, _ = FE.run(xo[e_ * 128:(e_ + 1) * 128, :], twq if o < 4 else None, k0)
        fb = FE.cur
        txd = P.dma("pool", xslots[b], XNT_d[so].rearrange("p (c t) -> p c t", t=128), xnT, [tready])
        mms = []
        for half in range(2):
            mm = None
            for sl2 in range(2):
                sl = half * 2 + sl2
                for c in range(16):
                    mm = P.op("pe", "matmul", dict(
                        out=pq[half][:, sl2 * 512:(sl2 + 1) * 512], lhsT=xnT[:, c, :],
                        rhs=WQ[:, c, sl * 512:(sl + 1) * 512], start=(c == 0), stop=(c == 15)),
                        [tready, pq_free[half]] if (c == 0 and sl2 == 0) else [])
            mms.append(mm)
        FE.readers(fb, [mms[1], txd])
        tpost = []
        for half in range(2):
            qi = half * 2 + b
            q3 = qf[qi]
            P.set_key(k0 + 8)
            tev = P.op("dve", "tensor_tensor", dict(
                out=q3.rearrange("p h d -> p (h d)"), in0=pq[half], in1=bQ[:, half * 1024:(half + 1) * 1024],
                op=ALU.add), [mms[half], qf_free[qi]])
            pq_free[half] = tev
            if half == 0:
                tk = head_post(P, A, q3, 8, qbf[b][:, 0:8, :], True, qg_row, ropes[rb], tmp,
                               [tev, trope, qbf_free[b]])
            else:
                tk = head_post(P, A, q3, 8, qbf[b][:, 8:16, :], False, None, ropes[rb], tmp,
                               [tev, trope, qbf_free[b]], scale=ATTN_SCALE)
            qf_free[qi] = tk
            tpost.append(tk)
        rope_free[rb] = tpost
        tc = None
        for half in range(2):
            P.set_key(k0 + 11 + half)
            tt = None
            for h in range(8):
                tt = P.op("pe", "transpose", dict(
                    out=ptq[:, h, :], in_=qbf[b][:, half * 8 + h, :], identity=C.identb),
                    [tpost[half], ptq_free] if h == 0 else [])
            P.set_key(k0 + 12 + half)
            tc = P.op("act", "activation", dict(
                out=qT[b][:, half * 8:(half + 1) * 8, :], in_=ptq, func=AF.Copy),
                [tt, qT_free[b] if half == 0 else None])
            ptq_free = tc
        qbf_free[b] = tt
        tqd = P.dma("pool", qslots[b], QT_d[:, :, so * 128:(so + 1) * 128].rearrange("h d t -> d h t"), qT[b], [tc])
        qT_free[b] = tqd
    p2a_done = P.full_barrier()
    A.release(m2)

    p2b_done = gate_phase(P, A, PS, C, w_in, 3072, A_fm, Brep, XNT_d, SZT_d, None, p2a_done, NQ)

    m4 = A.mark()
    KATh = A.alloc([NKC * 128], BF16)
    VAh = A.alloc([NKC, 128], BF16)
    qblk = [A.alloc([4, 512], BF16) for _ in range(2)]
    zblk = [A.alloc([4, 512], BF16) for _ in range(2)]
    pT = [A.alloc([1024], BF16) for _ in range(3)]
    pr = [A.alloc([512], BF16) for _ in range(2)]
    acc = [A.alloc([512], F32) for _ in range(2)]
    accs_free = [None, None]
    ahl = [A.alloc([2, 512], BF16) for _ in range(2)]
    ahl_free = [None, None]
    pr_free = [None, None]
    rs = [A.alloc([512], F32) for _ in range(2)]
    yf = [A.alloc([512], F32) for _ in range(2)]
    yb = [A.alloc([512], BF16) for _ in range(2)]
    kvs = [P.pslot("kvh0"), P.pslot("kvh1")]
    qs = [P.pslot("qb0"), P.pslot("qb1")]
    zs = [P.pslot("zb0"), P.pslot("zb1")]
    ys = [P.pslot("y0"), P.pslot("y1")]
    Sb = [PS.f32(0, 2), PS.f32(2, 2)]
    Ob = [PS.f32(4), PS.f32(5)]
    Ub = [PS.f32(6), PS.f32(7)]
    S_free = [None, None]
    pT_free = [None, None, None]
    acc_free = [None, None]
    rs_free = [None, None]
    yb_free = [None, None]
    qblk_free = [None, None]
    zblk_free = [None, None]
    kv_free = p2b_done
    NG = NKC // 2
    it = 0
    blkn = 0
    for kvh in range(2):
        tk1 = P.dma("sp", kvs[0], KATh, KAT_d[kvh], [kv_free])
        tk2 = P.dma("sp", kvs[1], VAh, VA_d[:, kvh, :].rearrange("(t p) d -> p t d", p=128), [kv_free])
        kvready = [tk1, tk2]
        lastuse = None
        for (c0, bw) in blocks_of(NQ):
            bb = blkn % 2
            blkn += 1
            tq = P.dma("sp", qs[bb], qblk[bb][:, :, 0:bw], QT_d[kvh * 4:(kvh + 1) * 4, :, c0:c0 + bw]
                       .rearrange("h d t -> d h t"), [qblk_free[bb]])
            tz = P.dma("sp", zs[bb], zblk[bb][:, :, 0:bw], SZT_d[kvh * 4:(kvh + 1) * 4, :, c0:c0 + bw]
                       .rearrange("h d t -> d h t"), [zblk_free[bb]])
            for hd in range(4):
                ab = it % 2
                it += 1
                qrhs = qblk[bb][:, hd, 0:bw]

                def QK(g, ab=ab, qrhs=qrhs):
                    sb = g % 2
                    t = None
                    for j in range(2):
                        kc = 2 * g + j
                        t = P.op("pe", "matmul", dict(
                            out=Sb[sb][:, j * bw:(j + 1) * bw], lhsT=KATh[:, kc * 128:(kc + 1) * 128], rhs=qrhs,
                            start=True, stop=True), [S_free[sb], tq, kvready] if j == 0 else [])
                    return t

                tqk = {0: QK(0), 1: QK(1)}
                tpv = None
                for g in range(NG):
                    sb = g % 2
                    pb = g % 3
                    tex = P.op("act", "activation", dict(out=pT[pb][:, 0:2 * bw], in_=Sb[sb][:, 0:2 * bw], func=AF.Exp),
                                 [tqk[g], pT_free[pb]])
                    S_free[sb] = tex
                    for j in range(2):
                        kc = 2 * g + j
                        P.op("pe", "matmul", dict(
                            out=Ob[ab][:, 0:bw], lhsT=VAh[:, kc, :], rhs=pT[pb][:, j * bw:(j + 1) * bw],
                            start=(g == 0 and j == 0), stop=(g == NG - 1 and j == 1)),
                            [tex, acc_free[ab]] if j == 0 else [])
                    tpair = P.op("dve", "tensor_tensor", dict(out=pr[g % 2][:, 0:bw], in0=pT[pb][:, 0:bw],
                                                              in1=pT[pb][:, bw:2 * bw], op=ALU.add),
                                  [tex, pr_free[g % 2]])
                    if g == 0:
                        tacc = P.op("dve", "tensor_copy", dict(out=acc[ab][:, 0:bw], in_=pr[g % 2][:, 0:bw]),
                                    [tpair, accs_free[ab]])
                    else:
                        tacc = P.op("dve", "tensor_tensor", dict(out=acc[ab][:, 0:bw], in0=acc[ab][:, 0:bw],
                                                                 in1=pr[g % 2][:, 0:bw], op=ALU.add), [tpair])
                    pr_free[g % 2] = tacc
                    tpv = P.last("pe")
                    pT_free[pb] = [tpv, tpair]
                    if g + 2 < NG:
                        tqk[g + 2] = QK(g + 2)
                thi = P.op("dve", "tensor_copy", dict(out=ahl[ab][:, 0, 0:bw], in_=acc[ab][:, 0:bw]), [tacc, ahl_free[ab]])
                tlo = P.op("dve", "tensor_tensor", dict(out=ahl[ab][:, 1, 0:bw], in0=acc[ab][:, 0:bw],
                                                        in1=ahl[ab][:, 0, 0:bw], op=ALU.subtract), [thi])
                accs_free[ab] = tlo
                P.op("pe", "matmul", dict(out=Ub[ab][:, 0:bw], lhsT=C.onesb, rhs=ahl[ab][:, 0, 0:bw],
                                          start=True, stop=False), [tlo])
                tpv = P.op("pe", "matmul", dict(out=Ub[ab][:, 0:bw], lhsT=C.onesb, rhs=ahl[ab][:, 1, 0:bw],
                                                start=False, stop=True), [])
                ahl_free[ab] = tpv
                t = P.op("dve", "reciprocal", dict(out=rs[ab][:, 0:bw], in_=Ub[ab][:, 0:bw]), [tpv, rs_free[ab]])
                t = P.op("dve", "tensor_tensor", dict(out=yf[ab][:, 0:bw], in0=Ob[ab][:, 0:bw], in1=rs[ab][:, 0:bw],
                                                      op=ALU.mult), [t])
                acc_free[ab] = t
                t = P.op("dve", "tensor_tensor", dict(
                    out=yb[ab][:, 0:bw], in0=yf[ab][:, 0:bw], in1=zblk[bb][:, hd, 0:bw], op=ALU.mult),
                    [t, tz, yb_free[ab]])
                rs_free[ab] = t
                td = P.dma("pool", ys[ab], YT_d[kvh * 4 + hd, :, c0:c0 + bw], yb[ab][:, 0:bw], [t])
                yb_free[ab] = td
                lastuse = [tpv, t]
            qblk_free[bb] = lastuse
            zblk_free[bb] = lastuse
        kv_free = lastuse
    p3a_done = P.full_barrier()
    A.release(m4)

    m5 = A.mark()
    qw = [A.alloc([8, 128], BF16) for _ in range(2)]
    zw = [A.alloc([8, 128], BF16) for _ in range(2)]
    pT5 = [A.alloc([5, 512], BF16) for _ in range(2)]
    su = [A.alloc([512], F32) for _ in range(2)]
    yf = [A.alloc([512], F32) for _ in range(2)]
    yb = [A.alloc([4, 128], BF16) for _ in range(2)]
    qs = [P.pslot("qb0"), P.pslot("qb1")]
    zs = [P.pslot("zb0"), P.pslot("zb1")]
    ys = [P.pslot("y0"), P.pslot("y1")]
    S5 = PS.f32(0, 5)
    Ob = PS.f32(5)
    Ub = PS.f32(6)
    S_free = None
    acc_free = None
    pT_free = [None, None]
    su_free = [None, None]
    yb_free = [None, None]
    qw_free = [None, None]
    it = 0
    for o in range(NQ):
        bb = o % 2
        so = slotof(o)
        P.set_key(float(it))
        tq = P.dma("sp", qs[bb], qw[bb], QT_d[8:16, :, so * 128:(so + 1) * 128].rearrange("h d t -> d h t"),
                   [qw_free[bb], p3a_done if o < 2 else None])
        tz = P.dma("sp", zs[bb], zw[bb], SZT_d[8:16, :, so * 128:(so + 1) * 128].rearrange("h d t -> d h t"),
                   [qw_free[bb], p3a_done if o < 2 else None])
        lastuse = None
        for kvh in range(2):
            ab = it % 2
            kb0 = float(it)
            it += 1
            qrhs = qw[bb][:, kvh * 4:(kvh + 1) * 4, :]
            mm = None
            P.set_key(kb0 + 1)
            for j in range(5):
                if j < 3:
                    lhs = KBT[:, kvh, (o + j) * 128:(o + j + 1) * 128]
                else:
                    lhs = KBTc[:, kvh, (j - 3) * 128:(j - 2) * 128]
                mm = P.op("pe", "matmul", dict(
                    out=S5[:, j * 512:(j + 1) * 512], lhsT=lhs, rhs=qrhs, start=True, stop=True),
                    [S_free, tq] if j == 0 else [])
            p5 = pT5[ab]
            P.set_key(kb0 + 2)
            tex = P.op("act", "activation", dict(out=p5.rearrange("p a b -> p (a b)"), in_=S5,
                                                              func=AF.Exp), [mm, pT_free[ab]])
            S_free = tex
            mlo = masks[:, 2 if o <= 1 else 0, :].unsqueeze(1).broadcast_to([128, 4, 128])
            mhi = masks[:, 3 if o >= NQ - 2 else 1, :].unsqueeze(1).broadcast_to([128, 4, 128])
            v0 = p5[:, 0, :].rearrange("p (h t) -> p h t", t=128)
            v2 = p5[:, 2, :].rearrange("p (h t) -> p h t", t=128)
            P.set_key(kb0 + 3)
            tm = P.op("dve", "tensor_tensor", dict(out=v0, in0=v0, in1=mlo, op=ALU.mult), [tex])
            tm = P.op("dve", "tensor_tensor", dict(out=v2, in0=v2, in1=mhi, op=ALU.mult), [tm])
            tpv = None
            P.set_key(kb0 + 4)
            for j in range(5):
                if j < 3:
                    lhs = VB[:, o + j, kvh, :]
                else:
                    lhs = VBc[:, j - 3, kvh, :]
                P.op("pe", "matmul", dict(
                    out=Ob, lhsT=lhs, rhs=p5[:, j, :], start=(j == 0), stop=(j == 4)),
                    [tm, acc_free] if j == 0 else [])
            for j in range(5):
                tpv = P.op("pe", "matmul", dict(
                    out=Ub, lhsT=C.onesb, rhs=p5[:, j, :], start=(j == 0), stop=(j == 4)), [])
            pT_free[ab] = tpv
            es = esink[:, kvh * 4:(kvh + 1) * 4].unsqueeze(2).broadcast_to([128, 4, 128])
            P.set_key(kb0 + 5)
            t = P.op("dve", "tensor_tensor", dict(
                out=su[ab].rearrange("p (h t) -> p h t", t=128), in0=Ub.rearrange("p (h t) -> p h t", t=128),
                in1=es, op=ALU.add), [tpv, su_free[ab]])
            t = P.op("dve", "reciprocal", dict(out=su[ab], in_=su[ab]), [t])
            t = P.op("dve", "tensor_tensor", dict(out=yf[ab], in0=Ob, in1=su[ab], op=ALU.mult), [t])
            acc_free = t
            t = P.op("dve", "tensor_tensor", dict(
                out=yb[ab].rearrange("p h t -> p (h t)"), in0=yf[ab],
                in1=zw[bb][:, kvh * 4:(kvh + 1) * 4, :].rearrange("p h t -> p (h t)"), op=ALU.mult),
                [t, tz, yb_free[ab]])
            su_free[ab] = t
            td = P.dma("pool", ys[ab], YT_d[8 + kvh * 4:8 + (kvh + 1) * 4, :, so * 128:(so + 1) * 128]
                       .rearrange("h d t -> d h t"), yb[ab], [t])
            yb_free[ab] = td
            lastuse = [tpv, t]
        qw_free[bb] = lastuse
    p3b_done = P.full_barrier()
    A.release(m5)

    qs_of_slot = [q for s_ in range(NQ) for q in range(NQ) if slotof(q) == s_]
    out_proj_phase(P, A, PS, C, w_out, YT_d, G_d,
                   lambda s_: xo[(qs_of_slot[s_] + 1) * 128:(qs_of_slot[s_] + 2) * 128, :],
                   lambda s_: X1_d[qs_of_slot[s_] * 128:(qs_of_slot[s_] + 1) * 128, :], p3b_done, NQ)
    done = P.full_barrier()
    A.release(mall)
    return done


def gate_phase(P, A, PS, C, w_dram, col0, A_fm, Brep, XNT_d, OUT_d, MUL_d, deps, ntiles):
    m3 = A.mark()
    WZ = A.alloc([16, 2048], BF16)
    bZ = A.alloc([2048], F32)
    bz_fm = A.alloc([16], F32)
    twz = prep_weight(P, A, PS, C, w_dram, col0, 2048, A_fm, Brep, WZ, bZ, deps, psum_bank0=4)
    tbz = diag_extract(P, A, C, bZ, bz_fm, twz)
    blk = [A.alloc([16, 512], BF16) for _ in range(2)]
    bslots = [P.pslot("blk0"), P.pslot("blk1")]
    szb = [A.alloc([512], BF16) for _ in range(4)]
    sslots = [P.pslot("sz%d" % i) for i in range(4)]
    blk_free = [None, None]
    szb_free = [None] * 4
    pz = [PS.f32(0), PS.f32(1), PS.f32(2), PS.f32(3)]
    pz_free = [None] * 4
    if MUL_d is not None:
        mblk = [A.alloc([16, 512], BF16) for _ in range(2)]
        mslots = [P.pslot("mb0"), P.pslot("mb1")]
    k = 0
    for tb, (c0, bw) in enumerate(blocks_of(ntiles)):
        b = tb % 2
        tl = None
        for j in range(bw // 128):
            tl = P.dma("sp", bslots[b], blk[b][:, :, j * 128:(j + 1) * 128],
                       XNT_d[tb * 4 + j].rearrange("p (c t) -> p c t", t=128),
                       [blk_free[b], [twz, tbz] if tb < 2 else None])
        tmul = None
        if MUL_d is not None:
            tmul = P.dma("sp", mslots[b], mblk[b][:, :, 0:bw], MUL_d[:, :, c0:c0 + bw].rearrange("c d t -> d c t"),
                         [blk_free[b], [twz, tbz] if tb < 2 else None])
        lastmm = None
        lastrd = None
        for f in range(16):
            pb = k % 4
            mm = None
            for c in range(16):
                mm = P.op("pe", "matmul", dict(
                    out=pz[pb][:, 0:bw], lhsT=WZ[:, c, f * 128:(f + 1) * 128], rhs=blk[b][:, c, 0:bw],
                    start=(c == 0), stop=(c == 15)), [tl, pz_free[pb], twz] if c == 0 else [])
            ta = P.op("act", "activation", dict(
                out=szb[pb][:, 0:bw], in_=pz[pb][:, 0:bw], func=AF.Silu, bias=bz_fm[:, f:f + 1]),
                [mm, szb_free[pb], tbz])
            pz_free[pb] = ta
            if MUL_d is not None:
                ta = P.op("dve", "tensor_tensor", dict(out=szb[pb][:, 0:bw], in0=szb[pb][:, 0:bw],
                                                       in1=mblk[b][:, f, 0:bw], op=ALU.mult), [ta, tmul])
                lastrd = ta
            td = P.dma("pool", sslots[pb], OUT_d[f, :, c0:c0 + bw], szb[pb][:, 0:bw], [ta])
            szb_free[pb] = td
            lastmm = mm
            k += 1
        blk_free[b] = [lastmm, lastrd]
    done = P.full_barrier()
    A.release(m3)
    return done


def out_proj_phase(P, A, PS, C, w_out, YT_d, G_d, xsrc, xdst, deps, ntiles):
    m = A.mark()
    WO = A.alloc([16, 2048], BF16)
    G = A.alloc([2048], F32)
    gs = P.pslot("adaln")
    tg = P.dma("pool", gs, G, G_d, [deps])
    two = prep_weight(P, A, PS, C, w_out, 0, 2048, None, None, WO, None, deps)
    yblk = [A.alloc([16, 512], BF16) for _ in range(2)]
    xt = [A.alloc([2048], F32) for _ in range(2)]
    xo_ = [A.alloc([2048], F32) for _ in range(2)]
    tmpf = A.alloc([2048], F32)
    junk = A.alloc([2048], BF16)
    ss = A.alloc([8], F32)
    bsl = [P.pslot("blk0"), P.pslot("blk1")]
    xsl = [P.pslot("ox0"), P.pslot("ox1")]
    osl = [P.pslot("y0"), P.pslot("y1")]
    po = [PS.f32(0, 4), PS.f32(4, 4)]
    po_free = [None, None]
    yblk_free = [None, None]
    xt_free = [None, None]
    xo_free = [None, None]
    outs = []
    tl = None
    blks = blocks_of(ntiles)
    for o in range(ntiles):
        b = o % 2
        tb, j = divmod(o, 4)
        bb = tb % 2
        c0, bw = blks[tb]
        k0 = float(o)
        P.set_key(k0)
        if j == 0:
            tl = P.dma("sp", bsl[bb], yblk[bb][:, :, 0:bw], YT_d[:, :, c0:c0 + bw].rearrange("c d t -> d c t"),
                       [yblk_free[bb], deps])
        tx = P.dma("sp", xsl[b], xt[b], xsrc(o), [xt_free[b], deps])
        P.set_key(k0 + 1)
        mm = None
        for sl in range(4):
            for c in range(16):
                mm = P.op("pe", "matmul", dict(
                    out=po[b][:, sl * 512:(sl + 1) * 512], lhsT=yblk[bb][:, c, j * 128:(j + 1) * 128],
                    rhs=WO[:, c, sl * 512:(sl + 1) * 512], start=(c == 0), stop=(c == 15)),
                    [tl, two, po_free[b]] if (c == 0 and sl == 0) else [])
        if j == bw // 128 - 1:
            yblk_free[bb] = mm
        k = o % 8
        P.set_key(k0 + 2)
        tsq = P.op("act", "activation", dict(out=junk, in_=po[b], func=AF.Square,
                                                             accum_out=ss[:, k:k + 1]), [mm])
        P.set_key(k0 + 3)
        t = P.op("dve", "tensor_scalar", dict(out=ss[:, k:k + 1], in0=ss[:, k:k + 1], scalar1=1.0 / D,
                                                         scalar2=EPS, op0=ALU.mult, op1=ALU.add), [tsq])
        P.set_key(k0 + 4)
        t = P.op("act", "activation", dict(out=ss[:, k:k + 1], in_=ss[:, k:k + 1], func=AF.Sqrt), [t])
        P.set_key(k0 + 5)
        t = P.op("dve", "reciprocal", dict(out=ss[:, k:k + 1], in_=ss[:, k:k + 1]), [t])
        t = P.op("dve", "scalar_tensor_tensor", dict(
            out=tmpf, in0=po[b], scalar=ss[:, k:k + 1], in1=G, op0=ALU.mult, op1=ALU.mult), [t, tg])
        po_free[b] = t
        t = P.op("dve", "tensor_tensor", dict(out=xo_[b], in0=tmpf, in1=xt[b], op=ALU.add),
                   [t, tx, xo_free[b]])
        xt_free[b] = t
        td = P.dma("pool", osl[b], xdst(o), xo_[b], [t])
        xo_free[b] = td
        outs.append(td)
    A.release(m)
    return outs[-2:]


POOL_SIZES = (2, 4, 8, 16)


def pool_phase(P, A, PS, C, xe, w_in, pool_w, pool_scale, pm_d, ic_d, A_fm, Brep, XNT_d, MT_d, deps):
    m = A.mark()
    WU = A.alloc([16, 2048], BF16)
    bU = A.alloc([2048], F32)
    twu = prep_weight(P, A, PS, C, w_in, 0, 2048, A_fm, Brep, WU, bU, deps, psum_bank0=2)
    PW = A.alloc([4, 4, 512], BF16)
    pwst = A.alloc([4, 512], F32)
    psc = A.alloc([16], F32)
    PM = A.alloc([36, 128], BF16)
    IC = A.alloc([12, 128], F32)
    sl = P.pslot("adaln")
    tl = None
    for g in range(4):
        tl = P.dma("pool", sl, pwst, pool_w[g].rearrange("(ci p) d -> p ci d", p=128), [twu, tl])
        tl = P.op("dve", "tensor_copy", dict(out=PW[:, g, :, :], in_=pwst), [tl])
    t1 = P.dma("pool", sl, psc, pool_scale.rearrange("(c p) -> p c", p=128), [tl], allow_slow_non_contiguous=True)
    t2 = P.dma("pool", sl, PM, pm_d, [t1])
    t3 = P.dma("pool", sl, IC, ic_d.partition_broadcast(128), [t2])
    t4 = t3
    ready = [twu, t4]
    FE = FrontEnd(P, A, PS, C, pt_bank=0)
    pu = PS.f32(2, 4)
    pp = PS.f32(6).rearrange("p (a b) -> p a b", b=128)
    pmx = PS.f32(7).rearrange("p (a b) -> p a b", b=128)
    ub = [A.alloc([2048], BF16) for _ in range(4)]
    pl = [A.alloc([4, 128], BF16) for _ in range(2)]
    mt = [A.alloc([16, 128], BF16) for _ in range(2)]
    xslots = [P.pslot("xn0"), P.pslot("xn1")]
    mslots = [P.pslot("q0"), P.pslot("q1")]
    ub_free = [None] * 4
    ub_ready = [None] * 4
    pu_free = None
    pp_free = None
    pmx_free = None
    pl_free = [None, None]
    mt_free = [None, None]
    gi = 0
    for e_ in range(NTE):
        k0 = float(e_)
        xnT, tready, xt, tx, _ = FE.run(xe[e_ * 128:(e_ + 1) * 128, :], ready if e_ < 4 else None, k0)
        fb = FE.cur
        rd = []
        if 1 <= e_ <= NTO:
            txd = P.dma("pool", xslots[e_ % 2], XNT_d[e_ - 1].rearrange("p (c t) -> p c t", t=128), xnT, [tready])
            rd.append(txd)
        mm = None
        for s4 in range(4):
            for c in range(16):
                mm = P.op("pe", "matmul", dict(
                    out=pu[:, s4 * 512:(s4 + 1) * 512], lhsT=xnT[:, c, :], rhs=WU[:, c, s4 * 512:(s4 + 1) * 512],
                    start=(c == 0), stop=(c == 15)), [tready, pu_free] if (c == 0 and s4 == 0) else [])
        rd.append(mm)
        FE.readers(fb, rd)
        ui = e_ % 4
        P.set_key(k0 + 8)
        tev = P.op("dve", "tensor_tensor", dict(out=ub[ui], in0=pu, in1=bU, op=ALU.add), [mm, ub_free[ui]])
        pu_free = tev
        ub_ready[ui] = tev
        if e_ >= 2:
            o = e_ - 2
            kind = 0 if o == 0 else (2 if o == NTO - 1 else 1)
            mb = o % 2
            lastp = None
            for g in range(4):
                pb = gi % 2
                gi += 1
                P.set_key(k0 + 9 + g)
                for fc in range(4):
                    f = g * 4 + fc
                    for r in range(3):
                        lastp = P.op("pe", "matmul", dict(
                            out=pp[:, fc, :], lhsT=ub[(o + r) % 4][:, f * 128:(f + 1) * 128],
                            rhs=PM[:, (kind * 4 + g) * 3 + r, :], start=(r == 0), stop=(r == 2)),
                            [ub_ready[(o + r) % 4], pp_free] if fc == 0 else [])
                icb = IC[:, kind * 4 + g, :].unsqueeze(1).broadcast_to([128, 4, 128])
                P.bump()
                tpl = P.op("dve", "tensor_tensor", dict(out=pl[pb], in0=pp, in1=icb, op=ALU.mult),
                           [lastp, pl_free[pb]])
                P.bump()
                pp_free = tpl
                lm = None
                for fo in range(4):
                    for ci in range(4):
                        lm = P.op("pe", "matmul", dict(
                            out=pmx[:, fo, :], lhsT=PW[:, g, ci, fo * 128:(fo + 1) * 128], rhs=pl[pb][:, ci, :],
                            start=(ci == 0), stop=(ci == 3)), [tpl, pmx_free] if (fo == 0 and ci == 0) else [])
                pl_free[pb] = lm
                pscb = psc[:, g * 4:(g + 1) * 4].unsqueeze(2).broadcast_to([128, 4, 128])
                P.bump()
                tmx = P.op("dve", "tensor_tensor", dict(out=mt[mb][:, g * 4:(g + 1) * 4, :], in0=pmx, in1=pscb,
                                                        op=ALU.mult), [lm, mt_free[mb] if g == 0 else None])
                pmx_free = tmx
            ub_free[o % 4] = lastp
            tmd = P.dma("pool", mslots[mb], MT_d[:, :, o * 128:(o + 1) * 128].rearrange("c d t -> d c t"), mt[mb],
                        [tmx])
            mt_free[mb] = tmd
    done = P.full_barrier()
    A.release(m)
    return done


def emit_l1(nc, P, A, PS, C, din, dscr, X1_d, deps):
    cvec = din("cvec1", [1, D])
    mod_w = din("mod_w1", [D, 3 * D])
    mod_b = din("mod_b1", [3 * D])
    pre_g = din("pre_g1", [D])
    post_g = din("post_g1", [D])
    w_in = din("w_in1", [D, 4096])
    pool_w = din("pool_w", [4, 512, 512])
    pool_scale = din("pool_scale", [D])
    w_out = din("w_out1", [D, D])
    pm_d = din("pm", [128, 36, 128], BF16)
    ic_d = din("ic", [12 * 128])
    out = nc.dram_tensor("out", [OWN, D], F32, kind="ExternalOutput").ap()
    G_d = dscr("G1_d", [128, D], F32)
    XNT_d = dscr("XNT1_d", [NTO, 128, D])
    MT_d = dscr("MT_d", [16, 128, OWN])
    YT_d = dscr("YT1_d", [16, 128, OWN])
    fms, t_ad = adaln_vectors(P, A, PS, C, cvec, 1, mod_w, mod_b, pre_g, post_g, G_d, deps)
    (A_fm, B_fm), = fms
    Brep = A.alloc([16, 128], F32)
    rep16(P, C, B_fm, Brep, t_ad)
    p0_done = P.full_barrier()
    pa_done = pool_phase(P, A, PS, C, X1_d, w_in, pool_w, pool_scale, pm_d, ic_d, A_fm, Brep, XNT_d, MT_d, p0_done)
    pb_done = gate_phase(P, A, PS, C, w_in, 2048, A_fm, Brep, XNT_d, YT_d, MT_d, pa_done, NTO)
    out_proj_phase(P, A, PS, C, w_out, YT_d, G_d, lambda o: X1_d[(o + 1) * 128:(o + 2) * 128, :],
                   lambda o: out[o * 128:(o + 1) * 128, :], pb_done, NTO)
    return P.full_barrier()


def build_fused(debug=False):
    nc = bass.Bass("TRN2", target_bir_lowering=False)

    def din(name, shape, dt=F32):
        return nc.dram_tensor(name, list(shape), dt, kind="ExternalInput").ap()

    def dscr(name, shape, dt=BF16):
        return nc.dram_tensor(name, list(shape), dt, kind=("ExternalOutput" if debug else "Internal")).ap()

    P = Prog(nc)
    A = Arena(nc, 206 * 1024)
    PS = Psum(nc)
    C = Ctx()
    X1_d = dscr("X1_d", [NQ * 128, D], F32)
    ident_d = din("ident", [128, 128])
    C.cons = load_consts(P, A, C, ident_d)
    d0 = emit_l0(nc, P, A, PS, C, din, dscr, X1_d)
    emit_l1(nc, P, A, PS, C, din, dscr, X1_d, d0)
    P.build()
    return nc


def _rope_table(pos):
    pos = np.asarray(pos)
    row = (pos // GRID_W).astype(np.float32)
    col = (pos % GRID_W).astype(np.float32)
    inv = (np.float32(10000.0) ** (-np.arange(32, dtype=np.float32) / np.float32(32))).astype(np.float32)
    ang = np.concatenate([row[:, None] * inv, col[:, None] * inv], axis=-1).astype(np.float32)
    return np.concatenate([np.cos(ang), np.sin(ang)], axis=-1).astype(np.float32)


def _ext_rows(xb_, j, halo=128):
    out = np.zeros((OWN + 2 * halo,) + xb_.shape[1:], dtype=xb_.dtype)
    lo = j * OWN - halo
    hi = (j + 1) * OWN + halo
    slo, shi = max(lo, 0), min(hi, xb_.shape[0])
    out[slo - lo:shi - lo] = xb_[slo:shi]
    return out


def _masks(j):
    kl = np.arange(128)[:, None]
    ql = np.arange(128)[None, :]
    lo = (kl >= ql).astype(np.float32)
    hi = (kl <= ql).astype(np.float32)
    m = np.stack([lo, hi, lo * (1.0 if j > 0 else 0.0), hi * (1.0 if j < 3 else 0.0)], axis=1)
    return np.ascontiguousarray(m.astype(np.float32))


_NC_CACHE = {}


def _pool_tables(j):
    pm = np.zeros((128, 36, 128), np.float32)
    ic = np.zeros((12, 128), np.float32)
    bases = [j * OWN, 5 * 128, (j + 1) * OWN - 128]
    for kind, base in enumerate(bases):
        for g, w in enumerate(POOL_SIZES):
            half = w // 2
            t = base + np.arange(128)
            lo = np.clip(t - half, 0, SEQ)
            hi = np.clip(t + half, 0, SEQ)
            cnt = (hi - lo).astype(np.float32)
            ic[kind * 4 + g] = 1.0 / cnt
            for r in range(3):
                sidx = base + (r - 1) * 128 + np.arange(128)
                mtx = ((sidx[:, None] >= lo[None, :]) & (sidx[:, None] < hi[None, :])).astype(np.float32)
                mtx -= (sidx[:, None] == t[None, :]).astype(np.float32) * cnt[None, :]
                pm[:, (kind * 4 + g) * 3 + r, :] = mtx
    return pm, ic.reshape(-1)


def _f32(a):
    return np.ascontiguousarray(np.asarray(a, dtype=np.float32))


def run_fused(inp, debug=False):
    key = "fd" if debug else "f"
    if key not in _NC_CACHE:
        _NC_CACHE[key] = build_fused(debug=debug)
    nc = _NC_CACHE[key]
    x = _f32(inp["x"])
    ropeb = _rope_table(np.arange(SEQ))
    ident = np.eye(128, dtype=np.float32)
    shared = {
        "mod_w": _f32(inp["ev_mod_w"][0]), "mod_b": _f32(inp["ev_mod_b"][0]),
        "pre_g": _f32(inp["ev_pre_g"][0]), "post_g": _f32(inp["ev_post_g"][0]),
        "w_in": _f32(inp["ev_w_in"][0]), "q_norm": _f32(inp["ev_q_norm"][0]), "k_norm": _f32(inp["ev_k_norm"][0]),
        "sink": _f32(inp["ev_sink"][0]), "w_out": _f32(inp["ev_w_out"][0]),
        "mod_w1": _f32(inp["od_mod_w"][0]), "mod_b1": _f32(inp["od_mod_b"][0]),
        "pre_g1": _f32(inp["od_pre_g"][0]), "post_g1": _f32(inp["od_post_g"][0]),
        "w_in1": _f32(inp["od_w_in"][0]), "pool_w": _f32(inp["od_pool_w"][0]),
        "pool_scale": _f32(inp["od_pool_scale"][0]), "w_out1": _f32(inp["od_w_out"][0]),
        "ropeb": ropeb, "ident": ident,
    }
    c = _f32(inp["c"])
    c_ctx = _f32(inp["c_ctx"])
    ctx = _f32(inp["ctx"])
    in_maps = []
    for core in range(8):
        b, j = divmod(core, 4)
        pos = np.clip(np.arange(j * OWN - 256, (j + 1) * OWN + 256), 0, SEQ - 1)
        pm, ic = _pool_tables(j)
        m = dict(shared)
        m.update({
            "xb": x[b], "xo": _ext_rows(x[b], j, halo=256), "ctx": ctx[b],
            "cvec": np.ascontiguousarray(np.stack([c[b], c_ctx])), "cvec1": np.ascontiguousarray(c[b][None]),
            "ropeo": _rope_table(pos), "masks": _masks(j),
            "pm": pm.astype(ml_dtypes.bfloat16), "ic": ic,
        })
        in_maps.append(m)
    res = run_bass_kernel_spmd(nc, in_maps, core_ids=list(range(8)))
    out = np.empty_like(x)
    for core in range(8):
        b, j = divmod(core, 4)
        out[b, j * OWN:(j + 1) * OWN] = res.results[core]["out"]
    if debug:
        return out, res.results
    return out


def kernel(**inputs):
    return run_fused(inputs)
```

```python
import numpy as np
import ml_dtypes
import concourse.bass as bass
import concourse.mybir as mybir
from concourse.bass_utils import run_bass_kernel_spmd

F32 = mybir.dt.float32
BF16 = mybir.dt.bfloat16
AF = mybir.ActivationFunctionType
ALU = mybir.AluOpType
AX = mybir.AxisListType

D = 2048
SEQ = 16384
NBATCH = 2
CTX = 256
HD = 128
OWN = 4096
NTO = OWN // 128
NTE = NTO + 2
NTB = SEQ // 128
NKC = NTB + CTX // 128
NQ = NTO + 2
NKB = NQ + 2
QCOLS = NQ * 128


def slotof(q):
    return q - 1 if 1 <= q <= NTO else (NTO if q == 0 else NTO + 1)


def blocks_of(ntiles):
    out = []
    c = 0
    while c < ntiles * 128:
        w = min(512, ntiles * 128 - c)
        out.append((c, w))
        c += w
    return out
EPS = 1e-6
ATTN_SCALE = HD ** -0.5
GRID_W = 64
ENGS = ("pe", "act", "dve", "pool", "sp")


class Op:
    __slots__ = ("kind", "eng", "fn", "waits", "key", "seq", "slot", "sem", "val", "snap")

    def __init__(self, kind, eng, fn, waits, key, seq, slot=None):
        self.kind, self.eng, self.fn, self.waits, self.key, self.seq, self.slot = kind, eng, fn, waits, key, seq, slot
        self.sem = None
        self.val = None
        self.snap = None


class DmaSlot:
    def __init__(self, prog, name):
        self.sem = prog.nc.alloc_semaphore(name)
        self.count = 0
        self.eng = None


class Prog:
    def __init__(self, nc):
        self.nc = nc
        self.recs = []
        self.sems = {e: nc.alloc_semaphore("s_" + e) for e in ENGS}
        self.last_op = {e: None for e in ENGS}
        self.nslot = 0
        self.slots = []
        self.named = {}
        self.base = 0.0
        self.key = 0.0
        self.seq = 0

    def set_key(self, k):
        self.key = self.base + float(k)

    def bump(self, d=1.0):
        self.key += d

    def slot(self, name=None):
        self.nslot += 1
        sl = DmaSlot(self, name or ("dslot%d" % self.nslot))
        self.slots.append(sl)
        return sl

    def pslot(self, name):
        if name not in self.named:
            self.named[name] = self.slot(name)
        return self.named[name]

    def _new(self, kind, eng, fn, waits, slot=None):
        waits = _flat(waits)
        k = self.key
        for w in waits:
            if w.key > k:
                k = w.key
        self.key = k
        op = Op(kind, eng, fn, waits, k, self.seq, slot)
        self.seq += 1
        self.recs.append(op)
        return op

    def emit(self, eng, fn, waits=()):
        op = self._new("c", eng, fn, waits)
        self.last_op[eng] = op
        return op

    def op(self, eng, method, kwargs, waits=()):
        kw = dict(kwargs)
        return self.emit(eng, lambda e, m=method, k=kw: getattr(e, m)(**k), waits)

    def dma(self, eng, slot, out, in_, waits=(), **kw):
        assert slot.eng in (None, eng), "a DMA slot must be used from a single queue"
        slot.eng = eng
        return self._new("d", eng, lambda e, o=out, i=in_, k=kw: e.dma_start(out=o, in_=i, **k), waits, slot)

    def wait_only(self, eng, waits):
        return self._new("w", eng, None, waits)

    def last(self, eng):
        return self.last_op[eng]

    def full_barrier(self):
        self.key = self.base + 900000.0
        self._new("b", None, None, ())
        self.base += 1000000.0
        self.key = self.base
        return []

    def build(self):
        nc = self.nc
        order = sorted(self.recs, key=lambda r: (r.key, r.seq))
        cnt = {e: 0 for e in ENGS}
        for sl in self.slots:
            sl.count = 0
        per = {e: [] for e in ENGS}
        for r in order:
            if r.kind == "c":
                cnt[r.eng] += 1
                r.sem, r.val = self.sems[r.eng], cnt[r.eng]
                per[r.eng].append(r)
            elif r.kind == "d":
                r.slot.count += 16
                r.sem, r.val = r.slot.sem, r.slot.count
                per[r.eng].append(r)
            elif r.kind == "w":
                per[r.eng].append(r)
            else:
                r.snap = [(self.sems[e], cnt[e]) for e in ENGS if cnt[e]] + \
                         [(sl.sem, sl.count) for sl in self.slots if sl.count]
                for e in ENGS:
                    per[e].append(r)
        with nc.Block() as block:
            def make(engname):
                def body(e):
                    seen = {}

                    def wait(sem, val):
                        key = id(sem)
                        if seen.get(key, 0) >= val:
                            return
                        e.wait_ge(sem, val)
                        seen[key] = val
                    for r in per[engname]:
                        if r.kind == "b":
                            for sem, val in r.snap:
                                wait(sem, val)
                            continue
                        for w in r.waits:
                            wait(w.sem, w.val)
                        if r.fn is not None:
                            inst = r.fn(e)
                            inst.then_inc(r.sem, 1 if r.kind == "c" else 16)
                return body
            block.tensor(make("pe"))
            block.scalar(make("act"))
            block.vector(make("dve"))
            block.gpsimd(make("pool"))
            block.sync(make("sp"))


def _flat(waits):
    out = []
    for w in waits:
        if w is None:
            continue
        if isinstance(w, (list, tuple)):
            out.extend(_flat(w))
        else:
            out.append(w)
    return tuple(out)


class Arena:
    def __init__(self, nc, nbytes):
        self.t = nc.alloc_sbuf_tensor("arena", [128, nbytes // 4], F32).ap()
        self.nbytes = nbytes
        self.off = 0

    def alloc(self, free_shape, dtype):
        n = int(np.prod(free_shape))
        esz = 2 if dtype == BF16 else 4
        nb = (n * esz + 31) // 32 * 32
        assert self.off + nb <= self.nbytes, ("SBUF arena overflow", self.off, nb, self.nbytes)
        ap = self.t[:, self.off // 4:(self.off + nb) // 4]
        self.off += nb
        if dtype == BF16:
            ap = ap.bitcast(BF16)
        ap = ap[:, 0:n]
        if len(free_shape) == 2:
            ap = ap.rearrange("p (a b) -> p a b", b=free_shape[1])
        elif len(free_shape) == 3:
            ap = ap.rearrange("p (a b c) -> p a b c", b=free_shape[1], c=free_shape[2])
        return ap

    def mark(self):
        return self.off

    def release(self, m):
        self.off = m


class Psum:
    def __init__(self, nc):
        self.t = nc.alloc_psum_tensor("psum_all", [128, 8, 512], F32).ap()

    def f32(self, bank, nbanks=1):
        ap = self.t[:, bank:bank + nbanks, :]
        return ap.rearrange("p a b -> p (a b)")

    def bf16(self, bank, nbanks, inner):
        ap = self.t[:, bank:bank + nbanks, :].rearrange("p a b -> p (a b)").bitcast(BF16)
        return ap.rearrange("p (c t) -> p c t", t=inner)


class Ctx:
    pass


def load_consts(P, A, C, ident_d):
    C.slot_c = P.slot("c_const")
    C.identf = A.alloc([128], F32)
    C.identb = A.alloc([128], BF16)
    C.onesf = A.alloc([128], F32)
    C.onesb = A.alloc([128], BF16)
    t = P.dma("sp", C.slot_c, C.identf, ident_d)
    t1 = P.op("dve", "tensor_copy", dict(out=C.identb, in_=C.identf), [t])
    t2 = P.op("dve", "memset", dict(ap=C.onesf, constant=1.0))
    t3 = P.op("dve", "memset", dict(ap=C.onesb, constant=1.0))
    return [t1, t2, t3]


def modulation(P, A, PS, C, cvec_d, ncv, mod_w, mod_b, res, deps):
    m1 = A.mark()
    cfm = A.alloc([ncv, 16], F32)
    screp = A.alloc([ncv, 16, 128], F32)
    sl = P.pslot("mod_c")
    tl = None
    for v in range(ncv):
        tl = P.dma("sp", sl, cfm[:, v, :], cvec_d[v].rearrange("(c p) -> p c", p=128),
                   allow_slow_non_contiguous=True)
    ts = P.op("act", "activation", dict(out=cfm, in_=cfm, func=AF.Silu), [tl, deps])
    tr = None
    for v in range(ncv):
        for c in range(16):
            tr = P.op("dve", "tensor_scalar", dict(
                out=screp[:, v, c, :], in0=C.onesf, scalar1=cfm[:, v, c:c + 1], scalar2=None, op0=ALU.mult),
                [ts, deps])
    allg = sorted(set(g for (v, g) in res))
    wt = [A.alloc([2048], F32) for _ in range(3)]
    brow = A.alloc([2048], F32)
    wslots = [P.pslot("pw%d" % i) for i in range(3)]
    bslot = P.pslot("mod_b")
    wfree = [deps, deps, deps]
    k = 0
    ev_prev = None
    for g in allg:
        vs = [v for v in range(ncv) if (v, g) in res]
        tb = P.dma("pool", bslot, brow, mod_b[g * 2048:(g + 1) * 2048].partition_broadcast(128), [ev_prev])
        last_mm = None
        for c in range(16):
            s = k % 3
            tw = P.dma("sp", wslots[s], wt[s], mod_w[c * 128:(c + 1) * 128, g * 2048:(g + 1) * 2048], [wfree[s]])
            for vi, v in enumerate(vs):
                for nb in range(4):
                    last_mm = P.op("pe", "matmul", dict(
                        out=PS.f32(4 * vi + nb), lhsT=screp[:, v, c, :], rhs=wt[s][:, nb * 512:(nb + 1) * 512],
                        start=(c == 0), stop=(c == 15)), [tw, tr, ev_prev if c == 0 else None])
            wfree[s] = last_mm
            k += 1
        for vi, v in enumerate(vs):
            ev_prev = P.op("dve", "tensor_tensor", dict(
                out=res[(v, g)], in0=PS.f32(4 * vi, 4), in1=brow, op=ALU.add), [last_mm, tb])
    return ev_prev


def diag_extract(P, A, C, row, out_fm, deps):
    m = A.mark()
    tmp = A.alloc([16, 128], F32)
    t = None
    for j in range(16):
        t = P.op("dve", "tensor_tensor", dict(out=tmp[:, j, :], in0=row[:, j * 128:(j + 1) * 128],
                                                         in1=C.identf, op=ALU.mult), [deps])
    t = P.op("dve", "tensor_reduce", dict(out=out_fm, in_=tmp, axis=AX.X, op=ALU.add), [t])
    return t


def prep_weight(P, A, PS, C, w_dram, col0, ncols, a_fm, brep, wdst, bias_row, deps, psum_bank0=0, stage=None):
    m = A.mark()
    if stage is None:
        stage = [A.alloc([512], F32) for _ in range(3)]
    slots = [P.pslot("pw%d" % i) for i in range(3)]
    free = [deps, deps, deps]
    k = 0
    last = None
    nsl = ncols // 512
    assert nsl <= 4 or bias_row is None or True
    for s0 in range(0, nsl, 4):
        grp = list(range(s0, min(nsl, s0 + 4)))
        lastmm = {}
        for c in range(16):
            for sl in grp:
                b = k % 3
                tw = P.dma("sp", slots[b], stage[b],
                           w_dram[c * 128:(c + 1) * 128, col0 + sl * 512:col0 + (sl + 1) * 512], [free[b]])
                rd = []
                if a_fm is not None:
                    t1 = P.op("dve", "tensor_scalar", dict(
                        out=wdst[:, c, sl * 512:(sl + 1) * 512], in0=stage[b], scalar1=a_fm[:, c:c + 1], scalar2=None,
                        op0=ALU.mult), [tw, deps])
                else:
                    t1 = P.op("dve", "tensor_copy", dict(
                        out=wdst[:, c, sl * 512:(sl + 1) * 512], in_=stage[b]), [tw, deps])
                rd.append(t1)
                last = t1
                if bias_row is not None:
                    t2 = P.op("pe", "matmul", dict(
                        out=PS.f32(psum_bank0 + sl - s0), lhsT=brep[:, c, :], rhs=stage[b],
                        start=(c == 0), stop=(c == 15)), [tw, deps])
                    rd.append(t2)
                    lastmm[sl] = t2
                free[b] = rd
                k += 1
        if bias_row is not None:
            for sl in grp:
                last = P.op("act", "activation", dict(
                    out=bias_row[:, sl * 512:(sl + 1) * 512], in_=PS.f32(psum_bank0 + sl - s0), func=AF.Identity),
                    [lastmm[sl]])
            deps = [deps, last]
    return [last, t1]


class FrontEnd:
    def __init__(self, P, A, PS, C, pt_bank, nx=3):
        self.P, self.C, self.PS = P, C, PS
        self.nx = nx
        self.xt = [A.alloc([2048], F32) for _ in range(nx)]
        self.xn = [A.alloc([2048], BF16) for _ in range(2)]
        self.xnT = [A.alloc([16, 128], BF16) for _ in range(2)]
        self.junk = A.alloc([2048], BF16)
        self.ss = A.alloc([8], F32)
        self.rstd = A.alloc([8], F32)
        self.slots = [P.pslot("fe%d" % i) for i in range(nx)]
        self.pt = PS.bf16(pt_bank, 2, 128)
        self.xt_free = [[] for _ in range(nx)]
        self.xn_free = [None, None]
        self.xnT_free = [[], []]
        self.pt_free = None
        self.n = 0

    def run(self, src_ap, deps=None, k0=None):
        P, C = self.P, self.C
        i = self.n
        self.n += 1
        b = i % 2
        bx = i % self.nx
        k = i % 8
        if k0 is None:
            k0 = P.key - P.base
        xt, xn, xnT = self.xt[bx], self.xn[b], self.xnT[b]
        P.set_key(k0)
        tx = P.dma("sp" if bx % 2 == 0 else "act", self.slots[bx], xt, src_ap, [self.xt_free[bx], deps])
        P.set_key(k0 + 1)
        tsq = P.op("act", "activation", dict(out=self.junk, in_=xt, func=AF.Square,
                                                   accum_out=self.ss[:, k:k + 1]), [tx])
        P.set_key(k0 + 2)
        tms = P.op("dve", "tensor_scalar", dict(out=self.rstd[:, k:k + 1], in0=self.ss[:, k:k + 1],
                                                      scalar1=1.0 / D, scalar2=EPS, op0=ALU.mult, op1=ALU.add),
                     [tsq])
        P.set_key(k0 + 3)
        tsr = P.op("act", "activation", dict(out=self.rstd[:, k:k + 1], in_=self.rstd[:, k:k + 1],
                                                   func=AF.Sqrt), [tms])
        P.set_key(k0 + 4)
        trc = P.op("dve", "reciprocal", dict(out=self.rstd[:, k:k + 1], in_=self.rstd[:, k:k + 1]), [tsr])
        txn = P.op("dve", "tensor_scalar", dict(out=xn, in0=xt, scalar1=self.rstd[:, k:k + 1], scalar2=None,
                                                      op0=ALU.mult), [trc, self.xn_free[b]])
        P.set_key(k0 + 5)
        tt = None
        for c in range(16):
            tt = P.op("pe", "transpose", dict(out=self.pt[:, c, :], in_=xn[:, c * 128:(c + 1) * 128],
                                                         identity=C.identb),
                        [txn, self.pt_free] if c == 0 else [])
        self.xn_free[b] = tt
        P.set_key(k0 + 6)
        tcp = P.op("act", "activation", dict(out=xnT, in_=self.pt, func=AF.Copy), [tt, self.xnT_free[b]])
        self.pt_free = tcp
        self.xt_free[bx] = [tsq, txn]
        self.xnT_free[b] = []
        self.cur = b
        P.set_key(k0 + 7)
        return xnT, tcp, xt, tx, self.rstd[:, k:k + 1]

    def readers(self, b, toks, x_toks=()):
        self.xnT_free[b] = list(self.xnT_free[b]) + list(_flat(toks))


def head_post(P, A, src, nh, dst_bf, do_norm, gain_row, rope_cs, tmp, deps, scale=None):
    t = deps
    tmp["n"] = tmp.get("n", 0) + 1
    sq, st8 = tmp["sq"][tmp["n"] % len(tmp["sq"])], tmp["st8"][tmp["n"] % len(tmp["st8"])]
    if do_norm:
        t = P.op("dve", "tensor_tensor", dict(out=sq[:, 0:nh, :], in0=src, in1=src, op=ALU.mult), [t])
        t = P.op("dve", "tensor_reduce", dict(out=st8[:, 0:nh], in_=sq[:, 0:nh, :], axis=AX.X, op=ALU.add), [t])
        t = P.op("dve", "tensor_scalar", dict(out=st8[:, 0:nh], in0=st8[:, 0:nh], scalar1=1.0 / HD, scalar2=EPS,
                                                    op0=ALU.mult, op1=ALU.add), [t])
        P.bump()
        t = P.op("act", "activation", dict(out=st8[:, 0:nh], in_=st8[:, 0:nh], func=AF.Sqrt), [t])
        P.bump()
        t = P.op("dve", "reciprocal", dict(out=st8[:, 0:nh], in_=st8[:, 0:nh]), [t])
        t = P.op("dve", "tensor_tensor", dict(out=src, in0=src,
                                                    in1=st8[:, 0:nh].unsqueeze(2).broadcast_to([128, nh, 128]),
                                                    op=ALU.mult), [t])
        t = P.op("dve", "tensor_tensor", dict(out=src, in0=src,
                                                    in1=gain_row.unsqueeze(1).broadcast_to([128, nh, 128]),
                                                    op=ALU.mult), [t])
    elif scale is not None:
        t = P.op("dve", "tensor_scalar", dict(out=src, in0=src, scalar1=float(scale), scalar2=None,
                                                    op0=ALU.mult), [t])
    if rope_cs is not None:
        cosb = rope_cs[:, 0:64].unsqueeze(1).broadcast_to([128, nh, 64])
        sinb = rope_cs[:, 64:128].unsqueeze(1).broadcast_to([128, nh, 64])
        x1 = src[:, :, 0:64]
        x2 = src[:, :, 64:128]
        ta, tb_ = sq[:, 0:nh, 0:64], sq[:, 0:nh, 64:128]
        t = P.op("dve", "tensor_tensor", dict(out=ta, in0=x1, in1=cosb, op=ALU.mult), [t])
        t = P.op("dve", "tensor_tensor", dict(out=tb_, in0=x2, in1=sinb, op=ALU.mult), [t])
        t = P.op("dve", "tensor_tensor", dict(out=dst_bf[:, :, 0:64], in0=ta, in1=tb_, op=ALU.subtract), [t])
        t = P.op("dve", "tensor_tensor", dict(out=ta, in0=x2, in1=cosb, op=ALU.mult), [t])
        t = P.op("dve", "tensor_tensor", dict(out=tb_, in0=x1, in1=sinb, op=ALU.mult), [t])
        t = P.op("dve", "tensor_tensor", dict(out=dst_bf[:, :, 64:128], in0=ta, in1=tb_, op=ALU.add), [t])
    else:
        t = P.op("dve", "tensor_copy", dict(out=dst_bf, in_=src), [t])
    return t


def rep16(P, C, fm, rep, deps):
    t = None
    for c in range(16):
        t = P.op("dve", "tensor_scalar", dict(out=rep[:, c, :], in0=C.onesf, scalar1=fm[:, c:c + 1],
                                                         scalar2=None, op0=ALU.mult), [deps])
    return t


def adaln_vectors(P, A, PS, C, cvec_d, ncv, mod_w, mod_b, pre_g, post_g, G_d, deps):
    fms = [(A.alloc([16], F32), A.alloc([16], F32)) for _ in range(ncv)]
    m = A.mark()
    res = {}
    for v in range(ncv):
        res[(v, 0)] = A.alloc([2048], F32)
        res[(v, 1)] = A.alloc([2048], F32)
    res[(0, 2)] = A.alloc([2048], F32)
    prow = A.alloc([2048], F32)
    sl = P.pslot("adaln")
    tp = P.dma("pool", sl, prow, pre_g.partition_broadcast(128), [deps])
    tm = modulation(P, A, PS, C, cvec_d, ncv, mod_w, mod_b, res, deps)
    last = []
    for v in range(ncv):
        t = P.op("dve", "scalar_tensor_tensor", dict(out=res[(v, 1)], in0=res[(v, 1)], scalar=1.0, in1=prow,
                                                                op0=ALU.add, op1=ALU.mult), [tm, tp])
        t1 = diag_extract(P, A, C, res[(v, 1)], fms[v][0], [t])
        t2 = diag_extract(P, A, C, res[(v, 0)], fms[v][1], [tm])
        last += [t1, t2]
    tp2 = P.dma("pool", sl, prow, post_g.partition_broadcast(128), [last])
    tg = P.op("dve", "tensor_tensor", dict(out=res[(0, 2)], in0=res[(0, 2)], in1=prow, op=ALU.mult), [tm, tp2])
    tgd = P.dma("pool", sl, G_d, res[(0, 2)], [tg])
    last.append(tgd)
    A.release(m)
    return fms, last


def emit_l0(nc, P, A, PS, C, din, dscr, X1_d):
    xb = din("xb", [SEQ, D])
    xo = din("xo", [NKB * 128, D])
    ctx_d = din("ctx", [CTX, D])
    cvec = din("cvec", [2, D])
    mod_w = din("mod_w", [D, 3 * D])
    mod_b = din("mod_b", [3 * D])
    pre_g = din("pre_g", [D])
    post_g = din("post_g", [D])
    w_in = din("w_in", [D, 5120])
    q_norm = din("q_norm", [HD])
    k_norm = din("k_norm", [HD])
    sink = din("sink", [8])
    w_out = din("w_out", [D, D])
    ropeb = din("ropeb", [SEQ, 128])
    ropeo = din("ropeo", [NKB * 128, 128])
    masks_d = din("masks", [128, 4, 128])

    G_d = dscr("G_d", [128, D], F32)
    KAT_d = dscr("KAT_d", [2, 128, NKC * 128])
    VA_d = dscr("VA_d", [NKC * 128, 2, 128])
    QT_d = dscr("QT_d", [NQ, 128, 16 * 128])
    XNT_d = dscr("XNT_d", [NQ, 128, D])
    SZT_d = dscr("SZT_d", [16, 128, QCOLS])
    YT_d = dscr("YT_d", [16, 128, QCOLS])
    mall = A.mark()

    cons = C.cons
    fms, t_ad = adaln_vectors(P, A, PS, C, cvec, 2, mod_w, mod_b, pre_g, post_g, G_d, cons)
    (A_fm, B_fm), (Ac_fm, Bc_fm) = fms
    Brep = A.alloc([16, 128], F32)
    t_brep = rep16(P, C, B_fm, Brep, t_ad)
    qg_row = A.alloc([128], F32)
    kg_row = A.alloc([128], F32)
    masks = A.alloc([4, 128], BF16)
    esink = A.alloc([8], F32)
    KBT = A.alloc([2, NKB * 128], BF16)
    VB = A.alloc([NKB, 2, 128], BF16)
    KBTc = A.alloc([2, CTX], BF16)
    VBc = A.alloc([2, 2, 128], BF16)
    sl0 = P.slot("p0misc")
    mk = A.mark()
    mstage = A.alloc([4, 128], F32)
    t1 = P.dma("pool", sl0, qg_row, q_norm.partition_broadcast(128), [t_ad, t_brep])
    t2 = P.dma("pool", sl0, kg_row, k_norm.partition_broadcast(128), [t_ad, t_brep])
    t3 = P.dma("pool", sl0, esink, sink.partition_broadcast(128), [t_ad, t_brep])
    t4 = P.dma("pool", sl0, mstage, masks_d, [t_ad, t_brep])
    tq = P.op("dve", "tensor_scalar", dict(out=qg_row, in0=qg_row, scalar1=float(ATTN_SCALE), scalar2=None,
                                                 op0=ALU.mult), [t1, t2, t3, t4])
    tmk = P.op("dve", "tensor_copy", dict(out=masks, in_=mstage), [tq])
    tes = P.op("act", "activation", dict(out=esink, in_=esink, func=AF.Exp), [t4])
    A.release(mk)
    p0_done = P.full_barrier()

    mkv = A.mark()
    Brepc = A.alloc([16, 128], F32)
    t_brc = rep16(P, C, Bc_fm, Brepc, p0_done)
    WB = A.alloc([16, 512], BF16)
    WA = A.alloc([16, 512], BF16)
    WC = A.alloc([16, 1024], BF16)
    bB = A.alloc([512], F32)
    bA = A.alloc([512], F32)
    bC = A.alloc([1024], F32)
    pstage = [A.alloc([512], F32) for _ in range(3)]
    tw1 = prep_weight(P, A, PS, C, w_in, 2560, 512, A_fm, Brep, WB, bB, [t_brep, t_brc], psum_bank0=4, stage=pstage)
    tw2 = prep_weight(P, A, PS, C, w_in, 2048, 512, A_fm, Brep, WA, bA, [tw1], psum_bank0=4, stage=pstage)
    tw3 = prep_weight(P, A, PS, C, w_in, 2048, 1024, Ac_fm, Brepc, WC, bC, [tw2], psum_bank0=4, stage=pstage)
    wready = [tw1, tw2, tw3]
    FE = FrontEnd(P, A, PS, C, pt_bank=0, nx=3)
    pkv = PS.f32(2, 2)
    ptk = PS.bf16(4, 1, 128)
    kvf = [A.alloc([1024], F32) for _ in range(2)]
    kbf = [A.alloc([4, 128], BF16) for _ in range(2)]
    ropes = [A.alloc([128], F32) for _ in range(4)]
    rslots = [P.pslot("rope%d" % i) for i in range(4)]
    tmp = {"sq": [A.alloc([4, 128], F32)], "st8": [A.alloc([8], F32) for _ in range(4)]}
    kst = [A.alloc([2, 512], BF16) for _ in range(2)]
    vst = [A.alloc([2, 128], BF16) for _ in range(2)]
    kslots = [P.slot(), P.slot()]
    vslots = [P.slot(), P.slot()]
    st = Ctx()
    st.kvf_free = [None, None]
    st.kbf_free = [None, None]
    st.rope_free = [None] * 4
    st.kst_free = [None, None]
    st.vst_free = [None, None]
    st.pkv_free = None
    st.ptk_free = None
    st.n = 0
    st.out_toks = []

    def kv_vcopy(mode, idx, b, kf, tk):
        if mode == "ext":
            tv = P.op("dve", "tensor_copy", dict(out=VB[:, idx, :, :],
                                                 in_=kf[:, 256:512].rearrange("p (h d) -> p h d", d=128)), [tk])
            st.kvf_free[b] = tv
            return [tv]
        tv = P.op("dve", "tensor_copy", dict(
            out=vst[b], in_=kf[:, 256:512].rearrange("p (h d) -> p h d", d=128)), [tk, st.vst_free[b]])
        st.kvf_free[b] = tv
        if mode == "ctx":
            tv2 = P.op("dve", "tensor_copy", dict(
                out=VBc[:, idx, :, :], in_=kf[:, 768:1024].rearrange("p (h d) -> p h d", d=128)), [tk])
            st.kvf_free[b] = tv2
            return [tv, tv2]
        return [tv]

    def kv_tile(src_ap, rope_ap, W, brow, ncols, mode, idx):
        i = st.n
        st.n += 1
        b = i % 2
        rb = i % 4
        k0 = float(i)
        P.set_key(k0 + 4)
        if rope_ap is not None:
            trope = P.dma("pool", rslots[rb], ropes[rb], rope_ap, [st.rope_free[rb], wready if i < 4 else None])
        else:
            trope = None
        xnT, tready, xt, tx, _ = FE.run(src_ap, wready if i < 4 else None, k0)
        fb = FE.cur
        nsl = ncols // 512
        mm = None
        for sl in range(nsl):
            for c in range(16):
                mm = P.op("pe", "matmul", dict(
                    out=pkv[:, sl * 512:(sl + 1) * 512], lhsT=xnT[:, c, :], rhs=W[:, c, sl * 512:(sl + 1) * 512],
                    start=(c == 0), stop=(c == 15)), [tready, st.pkv_free] if (c == 0 and sl == 0) else [])
        FE.readers(fb, [mm])
        kf = kvf[b]
        P.set_key(k0 + 8)
        tev = P.op("dve", "tensor_tensor", dict(out=kf[:, 0:ncols], in0=pkv[:, 0:ncols], in1=brow[:, 0:ncols],
                                                      op=ALU.add), [mm, st.kvf_free[b]])
        st.pkv_free = tev
        kb = kbf[b]
        toks = []
        if mode == "bat":
            ksrc = kf[:, 0:256].rearrange("p (h d) -> p h d", d=128)
            tk = head_post(P, A, ksrc, 2, kb[:, 0:2, :], True, kg_row, ropes[rb], tmp, [tev, trope, st.kbf_free[b]])
            st.rope_free[rb] = tk
            nk = 2
        elif mode == "ext":
            ksrc = kf[:, 0:256].rearrange("p (h d) -> p h d", d=128)
            tk = head_post(P, A, ksrc, 2, kb[:, 0:2, :], False, None, ropes[rb], tmp, [tev, trope, st.kbf_free[b]])
            st.rope_free[rb] = tk
            nk = 2
        else:
            ksrc = kf[:, 0:256].rearrange("p (h d) -> p h d", d=128)
            tk = head_post(P, A, ksrc, 2, kb[:, 0:2, :], True, kg_row, None, tmp, [tev, st.kbf_free[b]])
            ksrc2 = kf[:, 512:768].rearrange("p (h d) -> p h d", d=128)
            tk = head_post(P, A, ksrc2, 2, kb[:, 2:4, :], False, None, None, tmp, [tk])
            nk = 4
        P.set_key(k0 + 8)
        vtoks = kv_vcopy(mode, idx, b, kf, tk)
        P.set_key(k0 + 11)
        tt = None
        for h in range(nk):
            tt = P.op("pe", "transpose", dict(out=ptk[:, h, :], in_=kb[:, h, :], identity=C.identb),
                        [tk, st.ptk_free] if h == 0 else [])
        st.kbf_free[b] = tt
        P.set_key(k0 + 12)
        if mode == "ext":
            tc = P.op("act", "activation", dict(out=KBT[:, :, idx * 128:(idx + 1) * 128], in_=ptk[:, 0:2, :],
                                                      func=AF.Copy), [tt])
            st.ptk_free = tc
            toks += [tc]
        else:
            grp = 4 if mode == "bat" else 2
            g, r = divmod(idx, grp)
            sb = g % 2
            tc = P.op("act", "activation", dict(out=kst[sb][:, :, r * 128:(r + 1) * 128], in_=ptk[:, 0:2, :],
                                                      func=AF.Copy), [tt, st.kst_free[sb] if r == 0 else None])
            if mode == "ctx":
                tc2 = P.op("act", "activation", dict(out=KBTc[:, :, idx * 128:(idx + 1) * 128],
                                                           in_=ptk[:, 2:4, :], func=AF.Copy), [tt])
                tc = tc2
            st.ptk_free = tc
            base = (0 if mode == "bat" else SEQ)
            tok0 = base + idx * 128
            tvd = P.dma("pool", vslots[b], VA_d[tok0:tok0 + 128, :, :], vst[b], [vtoks])
            st.vst_free[b] = tvd
            toks.append(tvd)
            if r == grp - 1:
                c0 = base + g * grp * 128
                tkd = P.dma("pool", kslots[sb], KAT_d[:, :, c0:c0 + grp * 128].rearrange("h d t -> d h t"),
                            kst[sb][:, :, 0:grp * 128], [tc])
                st.kst_free[sb] = tkd
                toks.append(tkd)
        st.out_toks = [st.out_toks[-8:], toks]
        st.all_toks.extend(toks)

    st.all_toks = []
    for e_ in range(NKB):
        kv_tile(xo[e_ * 128:(e_ + 1) * 128, :], ropeo[e_ * 128:(e_ + 1) * 128, :], WB, bB, 512, "ext", e_)
    for t_ in range(CTX // 128):
        kv_tile(ctx_d[t_ * 128:(t_ + 1) * 128, :], None, WC, bC, 1024, "ctx", t_)
    for t_ in range(NTB):
        kv_tile(xb[t_ * 128:(t_ + 1) * 128, :], ropeb[t_ * 128:(t_ + 1) * 128, :], WA, bA, 512, "bat", t_)
    pkv_done = P.full_barrier()
    A.release(mkv)

    m2 = A.mark()
    WQ = A.alloc([16, 2048], BF16)
    bQ = A.alloc([2048], F32)
    FE = FrontEnd(P, A, PS, C, pt_bank=0, nx=3)
    twq = prep_weight(P, A, PS, C, w_in, 0, 2048, A_fm, Brep, WQ, bQ, pkv_done, psum_bank0=2,
                      stage=[FE.xt[i][:, 0:512] for i in range(3)])
    pq = [PS.f32(2, 2), PS.f32(4, 2)]
    ptq = PS.bf16(6, 1, 128)
    qf = [A.alloc([8, 128], F32) for _ in range(4)]
    qbf = [A.alloc([16, 128], BF16) for _ in range(2)]
    qT = [A.alloc([16, 128], BF16) for _ in range(2)]
    ropes = [A.alloc([128], F32) for _ in range(4)]
    rslots = [P.pslot("rope%d" % i) for i in range(4)]
    qslots = [P.pslot("q0"), P.pslot("q1")]
    xslots = [P.pslot("xn0"), P.pslot("xn1")]
    tmp = {"sq": [A.alloc([8, 128], F32)], "st8": [A.alloc([8], F32) for _ in range(4)]}
    pq_free = [None, None]
    qf_free = [None] * 4
    qbf_free = [None, None]
    qT_free = [None, None]
    rope_free = [None] * 4
    ptq_free = None
    for o in range(NQ):
        b = o % 2
        e_ = o + 1
        so = slotof(o)
        k0 = float(o)
        rb = o % 4
        P.set_key(k0 + 4)
        trope = P.dma("pool", rslots[rb], ropes[rb], ropeo[e_ * 128:(e_ + 1) * 128, :],
                      [rope_free[rb], twq if o < 4 else None])
        xnT, tready, xt, tx, _ = FE.run(xo[e_ * 128:(e_ + 1) * 128, :], twq if o < 4 else None, k0)
        fb = FE.cur
        txd = P.dma("pool", xslots[b], XNT_d[so].rearrange("p (c t) -> p c t", t=128), xnT, [tready])
        mms = []
        for half in range(2):
            mm = None
            for sl2 in range(2):
                sl = half * 2 + sl2
                for c in range(16):
                    mm = P.op("pe", "matmul", dict(
                        out=pq[half][:, sl2 * 512:(sl2 + 1) * 512], lhsT=xnT[:, c, :],
                        rhs=WQ[:, c, sl * 512:(sl + 1) * 512], start=(c == 0), stop=(c == 15)),
                        [tready, pq_free[half]] if (c == 0 and sl2 == 0) else [])
            mms.append(mm)
        FE.readers(fb, [mms[1], txd])
        tpost = []
        for half in range(2):
            qi = half * 2 + b
            q3 = qf[qi]
            P.set_key(k0 + 8)
            tev = P.op("dve", "tensor_tensor", dict(
                out=q3.rearrange("p h d -> p (h d)"), in0=pq[half], in1=bQ[:, half * 1024:(half + 1) * 1024],
                op=ALU.add), [mms[half], qf_free[qi]])
            pq_free[half] = tev
            if half == 0:
                tk = head_post(P, A, q3, 8, qbf[b][:, 0:8, :], True, qg_row, ropes[rb], tmp,
                               [tev, trope, qbf_free[b]])
            else:
                tk = head_post(P, A, q3, 8, qbf[b][:, 8:16, :], False, None, ropes[rb], tmp,
                               [tev, trope, qbf_free[b]], scale=ATTN_SCALE)
            qf_free[qi] = tk
            tpost.append(tk)
        rope_free[rb] = tpost
        tc = None
        for half in range(2):
            P.set_key(k0 + 11 + half)
            tt = None
            for h in range(8):
                tt = P.op("pe", "transpose", dict(
                    out=ptq[:, h, :], in_=qbf[b][:, half * 8 + h, :], identity=C.identb),
                    [tpost[half], ptq_free] if h == 0 else [])
            P.set_key(k0 + 12 + half)
            tc = P.op("act", "activation", dict(
                out=qT[b][:, half * 8:(half + 1) * 8, :], in_=ptq, func=AF.Copy),
                [tt, qT_free[b] if half == 0 else None])
            ptq_free = tc
        qbf_free[b] = tt
        tqd = P.dma("pool", qslots[b], QT_d[so].rearrange("d (h t) -> d h t", t=128), qT[b], [tc])
        qT_free[b] = tqd
    p2a_done = P.full_barrier()
    A.release(m2)

    p2b_done = gate_phase(P, A, PS, C, w_in, 3072, A_fm, Brep, XNT_d, SZT_d, None, p2a_done, NQ)

    m4 = A.mark()
    KATh = A.alloc([NKC * 128], BF16)
    VAh = A.alloc([NKC, 128], BF16)
    qblk = [A.alloc([4, 4, 128], BF16) for _ in range(2)]
    zblk = [A.alloc([4, 512], BF16) for _ in range(2)]
    pT = [A.alloc([1024], BF16) for _ in range(3)]
    pr = [A.alloc([512], BF16) for _ in range(2)]
    acc = [A.alloc([512], F32) for _ in range(2)]
    accs_free = [None, None]
    ahl = [A.alloc([2, 512], BF16) for _ in range(2)]
    ahl_free = [None, None]
    pr_free = [None, None]
    rs = [A.alloc([512], F32) for _ in range(2)]
    yf = [A.alloc([512], F32) for _ in range(2)]
    yb = [A.alloc([512], BF16) for _ in range(2)]
    kvs = [P.pslot("kvh0"), P.pslot("kvh1")]
    qs = [P.pslot("qb0"), P.pslot("qb1")]
    zs = [P.pslot("zb0"), P.pslot("zb1")]
    ys = [P.pslot("y0"), P.pslot("y1")]
    Sb = [PS.f32(0, 2), PS.f32(2, 2)]
    Ob = [PS.f32(4), PS.f32(5)]
    Ub = [PS.f32(6), PS.f32(7)]
    S_free = [None, None]
    pT_free = [None, None, None]
    acc_free = [None, None]
    rs_free = [None, None]
    yb_free = [None, None]
    qblk_free = [None, None]
    zblk_free = [None, None]
    kv_free = p2b_done
    NG = NKC // 2
    it = 0
    blkn = 0
    for kvh in range(2):
        tk1 = P.dma("sp", kvs[0], KATh, KAT_d[kvh], [kv_free])
        tk2 = P.dma("sp", kvs[1], VAh, VA_d[:, kvh, :].rearrange("(t p) d -> p t d", p=128), [kv_free])
        kvready = [tk1, tk2]
        lastuse = None
        for (c0, bw) in blocks_of(NQ):
            bb = blkn % 2
            blkn += 1
            tq = None
            for jt in range(bw // 128):
                tq = P.dma("sp", qs[bb], qblk[bb][:, jt, :, :],
                           QT_d[c0 // 128 + jt][:, kvh * 512:(kvh + 1) * 512].rearrange("d (h t) -> d h t", t=128),
                           [qblk_free[bb]])
            tz = P.dma("sp", zs[bb], zblk[bb][:, :, 0:bw], SZT_d[kvh * 4:(kvh + 1) * 4, :, c0:c0 + bw]
                       .rearrange("h d t -> d h t"), [zblk_free[bb]])
            for hd in range(4):
                ab = it % 2
                it += 1
                qrhs = qblk[bb][:, 0:bw // 128, hd, :]

                def QK(g, ab=ab, qrhs=qrhs):
                    sb = g % 2
                    t = None
                    for j in range(2):
                        kc = 2 * g + j
                        t = P.op("pe", "matmul", dict(
                            out=Sb[sb][:, j * bw:(j + 1) * bw], lhsT=KATh[:, kc * 128:(kc + 1) * 128], rhs=qrhs,
                            start=True, stop=True), [S_free[sb], tq, kvready] if j == 0 else [])
                    return t

                tqk = {0: QK(0), 1: QK(1)}
                tpv = None
                for g in range(NG):
                    sb = g % 2
                    pb = g % 3
                    tex = P.op("act", "activation", dict(out=pT[pb][:, 0:2 * bw], in_=Sb[sb][:, 0:2 * bw], func=AF.Exp),
                                 [tqk[g], pT_free[pb]])
                    S_free[sb] = tex
                    for j in range(2):
                        kc = 2 * g + j
                        P.op("pe", "matmul", dict(
                            out=Ob[ab][:, 0:bw], lhsT=VAh[:, kc, :], rhs=pT[pb][:, j * bw:(j + 1) * bw],
                            start=(g == 0 and j == 0), stop=(g == NG - 1 and j == 1)),
                            [tex, acc_free[ab]] if j == 0 else [])
                    tpair = P.op("dve", "tensor_tensor", dict(out=pr[g % 2][:, 0:bw], in0=pT[pb][:, 0:bw],
                                                              in1=pT[pb][:, bw:2 * bw], op=ALU.add),
                                  [tex, pr_free[g % 2]])
                    if g == 0:
                        tacc = P.op("dve", "tensor_copy", dict(out=acc[ab][:, 0:bw], in_=pr[g % 2][:, 0:bw]),
                                    [tpair, accs_free[ab]])
                    else:
                        tacc = P.op("dve", "tensor_tensor", dict(out=acc[ab][:, 0:bw], in0=acc[ab][:, 0:bw],
                                                                 in1=pr[g % 2][:, 0:bw], op=ALU.add), [tpair])
                    pr_free[g % 2] = tacc
                    tpv = P.last("pe")
                    pT_free[pb] = [tpv, tpair]
                    if g + 2 < NG:
                        tqk[g + 2] = QK(g + 2)
                thi = P.op("dve", "tensor_copy", dict(out=ahl[ab][:, 0, 0:bw], in_=acc[ab][:, 0:bw]), [tacc, ahl_free[ab]])
                tlo = P.op("dve", "tensor_tensor", dict(out=ahl[ab][:, 1, 0:bw], in0=acc[ab][:, 0:bw],
                                                        in1=ahl[ab][:, 0, 0:bw], op=ALU.subtract), [thi])
                accs_free[ab] = tlo
                P.op("pe", "matmul", dict(out=Ub[ab][:, 0:bw], lhsT=C.onesb, rhs=ahl[ab][:, 0, 0:bw],
                                          start=True, stop=False), [tlo])
                tpv = P.op("pe", "matmul", dict(out=Ub[ab][:, 0:bw], lhsT=C.onesb, rhs=ahl[ab][:, 1, 0:bw],
                                                start=False, stop=True), [])
                ahl_free[ab] = tpv
                t = P.op("dve", "reciprocal", dict(out=rs[ab][:, 0:bw], in_=Ub[ab][:, 0:bw]), [tpv, rs_free[ab]])
                t = P.op("dve", "tensor_tensor", dict(out=yf[ab][:, 0:bw], in0=Ob[ab][:, 0:bw], in1=rs[ab][:, 0:bw],
                                                      op=ALU.mult), [t])
                acc_free[ab] = t
                t = P.op("dve", "tensor_tensor", dict(
                    out=yb[ab][:, 0:bw], in0=yf[ab][:, 0:bw], in1=zblk[bb][:, hd, 0:bw], op=ALU.mult),
                    [t, tz, yb_free[ab]])
                rs_free[ab] = t
                td = P.dma("pool", ys[ab], YT_d[kvh * 4 + hd, :, c0:c0 + bw], yb[ab][:, 0:bw], [t])
                yb_free[ab] = td
                lastuse = [tpv, t]
            qblk_free[bb] = lastuse
            zblk_free[bb] = lastuse
        kv_free = lastuse
    p3a_done = P.full_barrier()
    A.release(m4)

    m5 = A.mark()
    qw = [A.alloc([8, 128], BF16) for _ in range(2)]
    zw = [A.alloc([8, 128], BF16) for _ in range(2)]
    pT5 = [A.alloc([5, 512], BF16) for _ in range(2)]
    su = [A.alloc([512], F32) for _ in range(2)]
    yf = [A.alloc([512], F32) for _ in range(2)]
    yb = [A.alloc([4, 128], BF16) for _ in range(2)]
    qs = [P.pslot("qb0"), P.pslot("qb1")]
    zs = [P.pslot("zb0"), P.pslot("zb1")]
    ys = [P.pslot("y0"), P.pslot("y1")]
    S5 = PS.f32(0, 5)
    Ob = PS.f32(5)
    Ub = PS.f32(6)
    S_free = None
    acc_free = None
    pT_free = [None, None]
    su_free = [None, None]
    yb_free = [None, None]
    qw_free = [None, None]
    it = 0
    for o in range(NQ):
        bb = o % 2
        so = slotof(o)
        P.set_key(float(it))
        tq = P.dma("sp", qs[bb], qw[bb], QT_d[so][:, 8 * 128:16 * 128].rearrange("d (h t) -> d h t", t=128),
                   [qw_free[bb], p3a_done if o < 2 else None])
        tz = P.dma("sp", zs[bb], zw[bb], SZT_d[8:16, :, so * 128:(so + 1) * 128].rearrange("h d t -> d h t"),
                   [qw_free[bb], p3a_done if o < 2 else None])
        lastuse = None
        for kvh in range(2):
            ab = it % 2
            kb0 = float(it)
            it += 1
            qrhs = qw[bb][:, kvh * 4:(kvh + 1) * 4, :]
            mm = None
            P.set_key(kb0 + 1)
            for j in range(5):
                if j < 3:
                    lhs = KBT[:, kvh, (o + j) * 128:(o + j + 1) * 128]
                else:
                    lhs = KBTc[:, kvh, (j - 3) * 128:(j - 2) * 128]
                mm = P.op("pe", "matmul", dict(
                    out=S5[:, j * 512:(j + 1) * 512], lhsT=lhs, rhs=qrhs, start=True, stop=True),
                    [S_free, tq] if j == 0 else [])
            p5 = pT5[ab]
            P.set_key(kb0 + 2)
            tex = P.op("act", "activation", dict(out=p5.rearrange("p a b -> p (a b)"), in_=S5,
                                                              func=AF.Exp), [mm, pT_free[ab]])
            S_free = tex
            mlo = masks[:, 2 if o <= 1 else 0, :].unsqueeze(1).broadcast_to([128, 4, 128])
            mhi = masks[:, 3 if o >= NQ - 2 else 1, :].unsqueeze(1).broadcast_to([128, 4, 128])
            v0 = p5[:, 0, :].rearrange("p (h t) -> p h t", t=128)
            v2 = p5[:, 2, :].rearrange("p (h t) -> p h t", t=128)
            P.set_key(kb0 + 3)
            tm = P.op("dve", "tensor_tensor", dict(out=v0, in0=v0, in1=mlo, op=ALU.mult), [tex])
            tm = P.op("dve", "tensor_tensor", dict(out=v2, in0=v2, in1=mhi, op=ALU.mult), [tm])
            tpv = None
            P.set_key(kb0 + 4)
            for j in range(5):
                if j < 3:
                    lhs = VB[:, o + j, kvh, :]
                else:
                    lhs = VBc[:, j - 3, kvh, :]
                P.op("pe", "matmul", dict(
                    out=Ob, lhsT=lhs, rhs=p5[:, j, :], start=(j == 0), stop=(j == 4)),
                    [tm, acc_free] if j == 0 else [])
            for j in range(5):
                tpv = P.op("pe", "matmul", dict(
                    out=Ub, lhsT=C.onesb, rhs=p5[:, j, :], start=(j == 0), stop=(j == 4)), [])
            pT_free[ab] = tpv
            es = esink[:, kvh * 4:(kvh + 1) * 4].unsqueeze(2).broadcast_to([128, 4, 128])
            P.set_key(kb0 + 5)
            t = P.op("dve", "tensor_tensor", dict(
                out=su[ab].rearrange("p (h t) -> p h t", t=128), in0=Ub.rearrange("p (h t) -> p h t", t=128),
                in1=es, op=ALU.add), [tpv, su_free[ab]])
            t = P.op("dve", "reciprocal", dict(out=su[ab], in_=su[ab]), [t])
            t = P.op("dve", "tensor_tensor", dict(out=yf[ab], in0=Ob, in1=su[ab], op=ALU.mult), [t])
            acc_free = t
            t = P.op("dve", "tensor_tensor", dict(
                out=yb[ab].rearrange("p h t -> p (h t)"), in0=yf[ab],
                in1=zw[bb][:, kvh * 4:(kvh + 1) * 4, :].rearrange("p h t -> p (h t)"), op=ALU.mult),
                [t, tz, yb_free[ab]])
            su_free[ab] = t
            td = P.dma("pool", ys[ab], YT_d[8 + kvh * 4:8 + (kvh + 1) * 4, :, so * 128:(so + 1) * 128]
                       .rearrange("h d t -> d h t"), yb[ab], [t])
            yb_free[ab] = td
            lastuse = [tpv, t]
        qw_free[bb] = lastuse
    p3b_done = P.full_barrier()
    A.release(m5)

    qs_of_slot = [q for s_ in range(NQ) for q in range(NQ) if slotof(q) == s_]
    out_proj_phase(P, A, PS, C, w_out, YT_d, G_d,
                   lambda s_: xo[(qs_of_slot[s_] + 1) * 128:(qs_of_slot[s_] + 2) * 128, :],
                   lambda s_: X1_d[qs_of_slot[s_] * 128:(qs_of_slot[s_] + 1) * 128, :], p3b_done, NQ)
    done = P.full_barrier()
    A.release(mall)
    return done


def gate_phase(P, A, PS, C, w_dram, col0, A_fm, Brep, XNT_d, OUT_d, MUL_d, deps, ntiles):
    m3 = A.mark()
    WZ = A.alloc([16, 2048], BF16)
    bZ = A.alloc([2048], F32)
    bz_fm = A.alloc([16], F32)
    twz = prep_weight(P, A, PS, C, w_dram, col0, 2048, A_fm, Brep, WZ, bZ, deps, psum_bank0=4)
    tbz = diag_extract(P, A, C, bZ, bz_fm, twz)
    blk = [A.alloc([4, 16, 128], BF16) for _ in range(2)]
    bslots = [P.pslot("blk0"), P.pslot("blk1")]
    szb = [A.alloc([512], BF16) for _ in range(4)]
    sslots = [P.pslot("sz%d" % i) for i in range(4)]
    blk_free = [None, None]
    szb_free = [None] * 4
    pz = [PS.f32(0), PS.f32(1), PS.f32(2), PS.f32(3)]
    pz_free = [None] * 4
    if MUL_d is not None:
        mblk = [A.alloc([4, 16, 128], BF16) for _ in range(2)]
        mslots = [P.pslot("mb0"), P.pslot("mb1")]
    k = 0
    for tb, (c0, bw) in enumerate(blocks_of(ntiles)):
        b = tb % 2
        tl = None
        nt = bw // 128
        for j in range(nt):
            tl = P.dma("sp", bslots[b], blk[b][:, j, :, :],
                       XNT_d[tb * 4 + j].rearrange("p (c t) -> p c t", t=128),
                       [blk_free[b], [twz, tbz] if tb < 2 else None])
        tmul = None
        if MUL_d is not None:
            for j in range(nt):
                tmul = P.dma("sp", mslots[b], mblk[b][:, j, :, :],
                             MUL_d[tb * 4 + j].rearrange("d (c t) -> d c t", t=128),
                             [blk_free[b], [twz, tbz] if tb < 2 else None])
        lastmm = None
        lastrd = None
        for f in range(16):
            pb = k % 4
            mm = None
            for c in range(16):
                mm = P.op("pe", "matmul", dict(
                    out=pz[pb][:, 0:bw], lhsT=WZ[:, c, f * 128:(f + 1) * 128], rhs=blk[b][:, 0:nt, c, :],
                    start=(c == 0), stop=(c == 15)), [tl, pz_free[pb], twz] if c == 0 else [])
            ta = P.op("act", "activation", dict(
                out=szb[pb][:, 0:bw], in_=pz[pb][:, 0:bw], func=AF.Silu, bias=bz_fm[:, f:f + 1]),
                [mm, szb_free[pb], tbz])
            pz_free[pb] = ta
            if MUL_d is not None:
                ta = P.op("dve", "tensor_tensor", dict(
                    out=szb[pb][:, 0:bw].rearrange("p (j t) -> p j t", t=128),
                    in0=szb[pb][:, 0:bw].rearrange("p (j t) -> p j t", t=128),
                    in1=mblk[b][:, 0:nt, f, :], op=ALU.mult), [ta, tmul])
                lastrd = ta
            td = P.dma("pool", sslots[pb], OUT_d[f, :, c0:c0 + bw], szb[pb][:, 0:bw], [ta])
            szb_free[pb] = td
            lastmm = mm
            k += 1
        blk_free[b] = [lastmm, lastrd]
    done = P.full_barrier()
    A.release(m3)
    return done


def out_proj_phase(P, A, PS, C, w_out, YT_d, G_d, xsrc, xdst, deps, ntiles):
    m = A.mark()
    WO = A.alloc([16, 2048], BF16)
    G = A.alloc([2048], F32)
    gs = P.pslot("adaln")
    tg = P.dma("pool", gs, G, G_d, [deps])
    two = prep_weight(P, A, PS, C, w_out, 0, 2048, None, None, WO, None, deps)
    yblk = [A.alloc([16, 512], BF16) for _ in range(2)]
    xt = [A.alloc([2048], F32) for _ in range(2)]
    xo_ = [A.alloc([2048], F32) for _ in range(2)]
    tmpf = A.alloc([2048], F32)
    junk = A.alloc([2048], BF16)
    ss = A.alloc([8], F32)
    bsl = [P.pslot("blk0"), P.pslot("blk1")]
    xsl = [P.pslot("ox0"), P.pslot("ox1")]
    osl = [P.pslot("y0"), P.pslot("y1")]
    po = [PS.f32(0, 4), PS.f32(4, 4)]
    po_free = [None, None]
    yblk_free = [None, None]
    xt_free = [None, None]
    xo_free = [None, None]
    outs = []
    tl = None
    blks = blocks_of(ntiles)
    for o in range(ntiles):
        b = o % 2
        tb, j = divmod(o, 4)
        bb = tb % 2
        c0, bw = blks[tb]
        k0 = float(o)
        P.set_key(k0)
        if j == 0:
            tl = P.dma("sp", bsl[bb], yblk[bb][:, :, 0:bw], YT_d[:, :, c0:c0 + bw].rearrange("c d t -> d c t"),
                       [yblk_free[bb], deps])
        tx = P.dma("sp", xsl[b], xt[b], xsrc(o), [xt_free[b], deps])
        P.set_key(k0 + 1)
        mm = None
        for sl in range(4):
            for c in range(16):
                mm = P.op("pe", "matmul", dict(
                    out=po[b][:, sl * 512:(sl + 1) * 512], lhsT=yblk[bb][:, c, j * 128:(j + 1) * 128],
                    rhs=WO[:, c, sl * 512:(sl + 1) * 512], start=(c == 0), stop=(c == 15)),
                    [tl, two, po_free[b]] if (c == 0 and sl == 0) else [])
        if j == bw // 128 - 1:
            yblk_free[bb] = mm
        k = o % 8
        P.set_key(k0 + 2)
        tsq = P.op("act", "activation", dict(out=junk, in_=po[b], func=AF.Square,
                                                             accum_out=ss[:, k:k + 1]), [mm])
        P.set_key(k0 + 3)
        t = P.op("dve", "tensor_scalar", dict(out=ss[:, k:k + 1], in0=ss[:, k:k + 1], scalar1=1.0 / D,
                                                         scalar2=EPS, op0=ALU.mult, op1=ALU.add), [tsq])
        P.set_key(k0 + 4)
        t = P.op("act", "activation", dict(out=ss[:, k:k + 1], in_=ss[:, k:k + 1], func=AF.Sqrt), [t])
        P.set_key(k0 + 5)
        t = P.op("dve", "reciprocal", dict(out=ss[:, k:k + 1], in_=ss[:, k:k + 1]), [t])
        t = P.op("dve", "scalar_tensor_tensor", dict(
            out=tmpf, in0=po[b], scalar=ss[:, k:k + 1], in1=G, op0=ALU.mult, op1=ALU.mult), [t, tg])
        po_free[b] = t
        t = P.op("dve", "tensor_tensor", dict(out=xo_[b], in0=tmpf, in1=xt[b], op=ALU.add),
                   [t, tx, xo_free[b]])
        xt_free[b] = t
        td = P.dma("pool", osl[b], xdst(o), xo_[b], [t])
        xo_free[b] = td
        outs.append(td)
    A.release(m)
    return outs[-2:]


POOL_SIZES = (2, 4, 8, 16)


def pool_phase(P, A, PS, C, xe, w_in, pool_w, pool_scale, pm_d, ic_d, A_fm, Brep, XNT_d, MT_d, deps):
    m = A.mark()
    WU = A.alloc([16, 2048], BF16)
    bU = A.alloc([2048], F32)
    twu = prep_weight(P, A, PS, C, w_in, 0, 2048, A_fm, Brep, WU, bU, deps, psum_bank0=2)
    PW = A.alloc([4, 4, 512], BF16)
    pwst = A.alloc([4, 512], F32)
    psc = A.alloc([16], F32)
    PM = A.alloc([36, 128], BF16)
    IC = A.alloc([12, 128], F32)
    sl = P.pslot("adaln")
    tl = None
    for g in range(4):
        tl = P.dma("pool", sl, pwst, pool_w[g].rearrange("(ci p) d -> p ci d", p=128), [twu, tl])
        tl = P.op("dve", "tensor_copy", dict(out=PW[:, g, :, :], in_=pwst), [tl])
    t1 = P.dma("pool", sl, psc, pool_scale.rearrange("(c p) -> p c", p=128), [tl], allow_slow_non_contiguous=True)
    t2 = P.dma("pool", sl, PM, pm_d, [t1])
    t3 = P.dma("pool", sl, IC, ic_d.partition_broadcast(128), [t2])
    t4 = t3
    ready = [twu, t4]
    FE = FrontEnd(P, A, PS, C, pt_bank=0)
    pu = PS.f32(2, 4)
    pp = PS.f32(6).rearrange("p (a b) -> p a b", b=128)
    pmx = PS.f32(7).rearrange("p (a b) -> p a b", b=128)
    ub = [A.alloc([2048], BF16) for _ in range(4)]
    pl = [A.alloc([4, 128], BF16) for _ in range(2)]
    mt = [A.alloc([16, 128], BF16) for _ in range(2)]
    xslots = [P.pslot("xn0"), P.pslot("xn1")]
    mslots = [P.pslot("q0"), P.pslot("q1")]
    ub_free = [None] * 4
    ub_ready = [None] * 4
    pu_free = None
    pp_free = None
    pmx_free = None
    pl_free = [None, None]
    mt_free = [None, None]
    gi = 0
    for e_ in range(NTE):
        k0 = float(e_)
        xnT, tready, xt, tx, _ = FE.run(xe[e_ * 128:(e_ + 1) * 128, :], ready if e_ < 4 else None, k0)
        fb = FE.cur
        rd = []
        if 1 <= e_ <= NTO:
            txd = P.dma("pool", xslots[e_ % 2], XNT_d[e_ - 1].rearrange("p (c t) -> p c t", t=128), xnT, [tready])
            rd.append(txd)
        mm = None
        for s4 in range(4):
            for c in range(16):
                mm = P.op("pe", "matmul", dict(
                    out=pu[:, s4 * 512:(s4 + 1) * 512], lhsT=xnT[:, c, :], rhs=WU[:, c, s4 * 512:(s4 + 1) * 512],
                    start=(c == 0), stop=(c == 15)), [tready, pu_free] if (c == 0 and s4 == 0) else [])
        rd.append(mm)
        FE.readers(fb, rd)
        ui = e_ % 4
        P.set_key(k0 + 8)
        tev = P.op("dve", "tensor_tensor", dict(out=ub[ui], in0=pu, in1=bU, op=ALU.add), [mm, ub_free[ui]])
        pu_free = tev
        ub_ready[ui] = tev
        if e_ >= 2:
            o = e_ - 2
            kind = 0 if o == 0 else (2 if o == NTO - 1 else 1)
            mb = o % 2
            lastp = None
            for g in range(4):
                pb = gi % 2
                gi += 1
                P.set_key(k0 + 9 + g)
                for fc in range(4):
                    f = g * 4 + fc
                    for r in range(3):
                        lastp = P.op("pe", "matmul", dict(
                            out=pp[:, fc, :], lhsT=ub[(o + r) % 4][:, f * 128:(f + 1) * 128],
                            rhs=PM[:, (kind * 4 + g) * 3 + r, :], start=(r == 0), stop=(r == 2)),
                            [ub_ready[(o + r) % 4], pp_free] if fc == 0 else [])
                icb = IC[:, kind * 4 + g, :].unsqueeze(1).broadcast_to([128, 4, 128])
                P.bump()
                tpl = P.op("dve", "tensor_tensor", dict(out=pl[pb], in0=pp, in1=icb, op=ALU.mult),
                           [lastp, pl_free[pb]])
                P.bump()
                pp_free = tpl
                lm = None
                for fo in range(4):
                    for ci in range(4):
                        lm = P.op("pe", "matmul", dict(
                            out=pmx[:, fo, :], lhsT=PW[:, g, ci, fo * 128:(fo + 1) * 128], rhs=pl[pb][:, ci, :],
                            start=(ci == 0), stop=(ci == 3)), [tpl, pmx_free] if (fo == 0 and ci == 0) else [])
                pl_free[pb] = lm
                pscb = psc[:, g * 4:(g + 1) * 4].unsqueeze(2).broadcast_to([128, 4, 128])
                P.bump()
                tmx = P.op("dve", "tensor_tensor", dict(out=mt[mb][:, g * 4:(g + 1) * 4, :], in0=pmx, in1=pscb,
                                                        op=ALU.mult), [lm, mt_free[mb] if g == 0 else None])
                pmx_free = tmx
            ub_free[o % 4] = lastp
            tmd = P.dma("pool", mslots[mb], MT_d[o].rearrange("d (c t) -> d c t", t=128), mt[mb], [tmx])
            mt_free[mb] = tmd
    done = P.full_barrier()
    A.release(m)
    return done


def emit_l1(nc, P, A, PS, C, din, dscr, X1_d, deps):
    cvec = din("cvec1", [1, D])
    mod_w = din("mod_w1", [D, 3 * D])
    mod_b = din("mod_b1", [3 * D])
    pre_g = din("pre_g1", [D])
    post_g = din("post_g1", [D])
    w_in = din("w_in1", [D, 4096])
    pool_w = din("pool_w", [4, 512, 512])
    pool_scale = din("pool_scale", [D])
    w_out = din("w_out1", [D, D])
    pm_d = din("pm", [128, 36, 128], BF16)
    ic_d = din("ic", [12 * 128])
    out = nc.dram_tensor("out", [OWN, D], F32, kind="ExternalOutput").ap()
    G_d = dscr("G1_d", [128, D], F32)
    XNT_d = dscr("XNT1_d", [NTO, 128, D])
    MT_d = dscr("MT_d", [NTO, 128, 16 * 128])
    YT_d = dscr("YT1_d", [16, 128, OWN])
    fms, t_ad = adaln_vectors(P, A, PS, C, cvec, 1, mod_w, mod_b, pre_g, post_g, G_d, deps)
    (A_fm, B_fm), = fms
    Brep = A.alloc([16, 128], F32)
    rep16(P, C, B_fm, Brep, t_ad)
    p0_done = P.full_barrier()
    pa_done = pool_phase(P, A, PS, C, X1_d, w_in, pool_w, pool_scale, pm_d, ic_d, A_fm, Brep, XNT_d, MT_d, p0_done)
    pb_done = gate_phase(P, A, PS, C, w_in, 2048, A_fm, Brep, XNT_d, YT_d, MT_d, pa_done, NTO)
    out_proj_phase(P, A, PS, C, w_out, YT_d, G_d, lambda o: X1_d[(o + 1) * 128:(o + 2) * 128, :],
                   lambda o: out[o * 128:(o + 1) * 128, :], pb_done, NTO)
    return P.full_barrier()


def build_fused(debug=False):
    nc = bass.Bass("TRN2", target_bir_lowering=False)

    def din(name, shape, dt=F32):
        return nc.dram_tensor(name, list(shape), dt, kind="ExternalInput").ap()

    def dscr(name, shape, dt=BF16):
        return nc.dram_tensor(name, list(shape), dt, kind=("ExternalOutput" if debug else "Internal")).ap()

    P = Prog(nc)
    A = Arena(nc, 206 * 1024)
    PS = Psum(nc)
    C = Ctx()
    X1_d = dscr("X1_d", [NQ * 128, D], F32)
    ident_d = din("ident", [128, 128])
    C.cons = load_consts(P, A, C, ident_d)
    d0 = emit_l0(nc, P, A, PS, C, din, dscr, X1_d)
    emit_l1(nc, P, A, PS, C, din, dscr, X1_d, d0)
    P.build()
    return nc


def _rope_table(pos):
    pos = np.asarray(pos)
    row = (pos // GRID_W).astype(np.float32)
    col = (pos % GRID_W).astype(np.float32)
    inv = (np.float32(10000.0) ** (-np.arange(32, dtype=np.float32) / np.float32(32))).astype(np.float32)
    ang = np.concatenate([row[:, None] * inv, col[:, None] * inv], axis=-1).astype(np.float32)
    return np.concatenate([np.cos(ang), np.sin(ang)], axis=-1).astype(np.float32)


def _ext_rows(xb_, j, halo=128):
    out = np.zeros((OWN + 2 * halo,) + xb_.shape[1:], dtype=xb_.dtype)
    lo = j * OWN - halo
    hi = (j + 1) * OWN + halo
    slo, shi = max(lo, 0), min(hi, xb_.shape[0])
    out[slo - lo:shi - lo] = xb_[slo:shi]
    return out


def _masks(j):
    kl = np.arange(128)[:, None]
    ql = np.arange(128)[None, :]
    lo = (kl >= ql).astype(np.float32)
    hi = (kl <= ql).astype(np.float32)
    m = np.stack([lo, hi, lo * (1.0 if j > 0 else 0.0), hi * (1.0 if j < 3 else 0.0)], axis=1)
    return np.ascontiguousarray(m.astype(np.float32))


_NC_CACHE = {}


def _pool_tables(j):
    pm = np.zeros((128, 36, 128), np.float32)
    ic = np.zeros((12, 128), np.float32)
    bases = [j * OWN, 5 * 128, (j + 1) * OWN - 128]
    for kind, base in enumerate(bases):
        for g, w in enumerate(POOL_SIZES):
            half = w // 2
            t = base + np.arange(128)
            lo = np.clip(t - half, 0, SEQ)
            hi = np.clip(t + half, 0, SEQ)
            cnt = (hi - lo).astype(np.float32)
            ic[kind * 4 + g] = 1.0 / cnt
            for r in range(3):
                sidx = base + (r - 1) * 128 + np.arange(128)
                mtx = ((sidx[:, None] >= lo[None, :]) & (sidx[:, None] < hi[None, :])).astype(np.float32)
                mtx -= (sidx[:, None] == t[None, :]).astype(np.float32) * cnt[None, :]
                pm[:, (kind * 4 + g) * 3 + r, :] = mtx
    return pm, ic.reshape(-1)


def _f32(a):
    return np.ascontiguousarray(np.asarray(a, dtype=np.float32))


def run_fused(inp, debug=False):
    key = "fd" if debug else "f"
    if key not in _NC_CACHE:
        _NC_CACHE[key] = build_fused(debug=debug)
    nc = _NC_CACHE[key]
    x = _f32(inp["x"])
    ropeb = _rope_table(np.arange(SEQ))
    ident = np.eye(128, dtype=np.float32)
    shared = {
        "mod_w": _f32(inp["ev_mod_w"][0]), "mod_b": _f32(inp["ev_mod_b"][0]),
        "pre_g": _f32(inp["ev_pre_g"][0]), "post_g": _f32(inp["ev_post_g"][0]),
        "w_in": _f32(inp["ev_w_in"][0]), "q_norm": _f32(inp["ev_q_norm"][0]), "k_norm": _f32(inp["ev_k_norm"][0]),
        "sink": _f32(inp["ev_sink"][0]), "w_out": _f32(inp["ev_w_out"][0]),
        "mod_w1": _f32(inp["od_mod_w"][0]), "mod_b1": _f32(inp["od_mod_b"][0]),
        "pre_g1": _f32(inp["od_pre_g"][0]), "post_g1": _f32(inp["od_post_g"][0]),
        "w_in1": _f32(inp["od_w_in"][0]), "pool_w": _f32(inp["od_pool_w"][0]),
        "pool_scale": _f32(inp["od_pool_scale"][0]), "w_out1": _f32(inp["od_w_out"][0]),
        "ropeb": ropeb, "ident": ident,
    }
    c = _f32(inp["c"])
    c_ctx = _f32(inp["c_ctx"])
    ctx = _f32(inp["ctx"])
    in_maps = []
    for core in range(8):
        b, j = divmod(core, 4)
        pos = np.clip(np.arange(j * OWN - 256, (j + 1) * OWN + 256), 0, SEQ - 1)
        pm, ic = _pool_tables(j)
        m = dict(shared)
        m.update({
            "xb": x[b], "xo": _ext_rows(x[b], j, halo=256), "ctx": ctx[b],
            "cvec": np.ascontiguousarray(np.stack([c[b], c_ctx])), "cvec1": np.ascontiguousarray(c[b][None]),
            "ropeo": _rope_table(pos), "masks": _masks(j),
            "pm": pm.astype(ml_dtypes.bfloat16), "ic": ic,
        })
        in_maps.append(m)
    res = run_bass_kernel_spmd(nc, in_maps, core_ids=list(range(8)))
    out = np.empty_like(x)
    for core in range(8):
        b, j = divmod(core, 4)
        out[b, j * OWN:(j + 1) * OWN] = res.results[core]["out"]
    if debug:
        return out, res.results
    return out


def kernel(**inputs):
    return run_fused(inputs)
```
